# Optimizing a Trainium2 kernel written in Bass

```python
import jax, jax.numpy as jnp
from jax import lax
import numpy as np

D_MODEL = 2048
BATCH = 2
SEQ = 4096
DEPTH = 1

RWKV_HEAD_DIM = 64
RWKV_HEADS = (D_MODEL // 2) // RWKV_HEAD_DIM
RWKV_DIM = RWKV_HEADS * RWKV_HEAD_DIM
DECAY_LORA = 64
ICLR_LORA = 64
GATE_LORA = 160
ATTN_HEAD_DIM = 64
ATTN_Q_HEADS = (D_MODEL // 2) // ATTN_HEAD_DIM
ATTN_KV_HEADS = max(1, ATTN_Q_HEADS // 8)
ATTN_GROUP = ATTN_Q_HEADS // ATTN_KV_HEADS
WINDOW = 128
FFN_DIM = 4 * D_MODEL
NORM_EPS = 1e-6
RWKV_GN_EPS = 64e-5
N_BRANCHES = 2

RWKV_COLS = 3 * RWKV_DIM + DECAY_LORA + ICLR_LORA + GATE_LORA
ATTN_Q_COLS = ATTN_Q_HEADS * ATTN_HEAD_DIM
ATTN_KV_COLS = ATTN_KV_HEADS * ATTN_HEAD_DIM
ATTN_COLS = ATTN_Q_COLS + 2 * ATTN_KV_COLS
GATE_COLS = N_BRANCHES * D_MODEL
IN_COLS = RWKV_COLS + ATTN_COLS + GATE_COLS

kernel_name = "hybrid_rwkv7_swa_sink_alibi_block"


def rms_norm(x, gain):
    x32 = x.astype(jnp.float32)
    y = x32 * lax.rsqrt(jnp.mean(x32 * x32, axis=-1, keepdims=True) + NORM_EPS)
    return (y * gain.astype(jnp.float32)).astype(x.dtype)


def token_shift(p):
    return jnp.pad(p, ((0, 0), (1, 0), (0, 0)))[:, :-1]


def rwkv7_scan(r, decay, k, v, a, b):
    B, T, H, N = r.shape

    def step(S, inp):
        r_t, w_t, k_t, v_t, a_t, b_t = inp
        sa = jnp.einsum('bhij,bhj->bhi', S, a_t)
        S = S * w_t[:, :, None, :] + sa[..., None] * b_t[:, :, None, :] + v_t[..., None] * k_t[:, :, None, :]
        y = jnp.einsum('bhij,bhj->bhi', S, r_t)
        return S, y

    xs = tuple(jnp.moveaxis(t, 1, 0) for t in (r, decay, k, v, a, b))
    S0 = jnp.zeros((B, H, N, N), jnp.float32)
    _, y = lax.scan(step, S0, xs)
    return jnp.moveaxis(y, 0, 1)


def rwkv7_branch(p, mix, w0, w2, a0, a2, g2, k_k, k_a, r_k, ln_w, ln_b):
    B, T, _ = p.shape
    H, N, C = RWKV_HEADS, RWKV_HEAD_DIM, RWKV_DIM
    p = p + (token_shift(p) - p) * mix
    r = p[..., :C]
    k = p[..., C:2 * C]
    v = p[..., 2 * C:3 * C]
    o = 3 * C
    wd = p[..., o:o + DECAY_LORA]
    o += DECAY_LORA
    ad = p[..., o:o + ICLR_LORA]
    o += ICLR_LORA
    gd = p[..., o:o + GATE_LORA]

    w = -jax.nn.softplus(-(w0 + jnp.tanh(wd) @ w2)) - 0.5
    decay = jnp.exp(-jnp.exp(w.astype(jnp.float32)))
    a = jax.nn.sigmoid(a0 + ad @ a2)
    g = jax.nn.sigmoid(gd) @ g2

    hs = lambda t: t.reshape(B, T, H, N).astype(jnp.float32)
    kk = hs(k * k_k)
    kk = kk / jnp.maximum(jnp.sqrt(jnp.sum(kk * kk, axis=-1, keepdims=True)), 1e-12)
    k_mod = k * (1.0 + (a - 1.0) * k_a)
    r_h, k_h, v_h, a_h = hs(r), hs(k_mod), hs(v), hs(a)

    y = rwkv7_scan(r_h, hs(decay), k_h, v_h, -kk, kk * a_h)

    mu = jnp.mean(y, axis=-1, keepdims=True)
    var = jnp.mean(jnp.square(y - mu), axis=-1, keepdims=True)
    yn = ((y - mu) * lax.rsqrt(var + RWKV_GN_EPS)).reshape(B, T, C)
    yn = yn * ln_w.astype(jnp.float32) + ln_b.astype(jnp.float32)
    bonus = jnp.sum(r_h * k_h * r_k.astype(jnp.float32), axis=-1, keepdims=True) * v_h
    out = (yn + bonus.reshape(B, T, C)) * g.astype(jnp.float32)
    return out.astype(p.dtype)


def sliding_window_sink_attention(q, k, v, sinks):
    B, T = q.shape[:2]
    W, HKV, G, hd = WINDOW, ATTN_KV_HEADS, ATTN_GROUP, ATTN_HEAD_DIM
    nb = T // W
    qb = q.reshape(B, nb, W, HKV, G, hd)
    pad = ((0, 0), (W, 0), (0, 0), (0, 0))
    kp = jnp.pad(k, pad).reshape(B, nb + 1, W, HKV, hd)
    vp = jnp.pad(v, pad).reshape(B, nb + 1, W, HKV, hd)
    kw = jnp.concatenate([kp[:, :-1], kp[:, 1:]], axis=2)
    vw = jnp.concatenate([vp[:, :-1], vp[:, 1:]], axis=2)

    s = jnp.einsum('bnqhgd,bnkhd->bnhgqk', qb, kw).astype(jnp.float32) * (hd ** -0.5)
    qi = jnp.arange(W)[:, None]
    kj = jnp.arange(2 * W)[None, :]
    dist = qi + W - kj
    kpos = jnp.arange(nb)[:, None] * W - W + jnp.arange(2 * W)[None, :]
    valid = ((dist >= 0) & (dist < W))[None] & (kpos >= 0)[:, None, :]

    slopes = 2.0 ** (-8.0 * jnp.arange(1, ATTN_Q_HEADS + 1, dtype=jnp.float32) / ATTN_Q_HEADS)
    alibi = -slopes.reshape(HKV, G)[:, :, None, None] * dist.astype(jnp.float32)
    s = jnp.where(valid[None, :, None, None], s + alibi, -jnp.inf)
    sink = jnp.broadcast_to(sinks.astype(jnp.float32).reshape(HKV, G)[None, None, :, :, None, None],
                            s.shape[:-1] + (1,))
    prob = jax.nn.softmax(jnp.concatenate([s, sink], axis=-1), axis=-1)[..., :2 * W]
    o = jnp.einsum('bnhgqk,bnkhd->bnqhgd', prob.astype(v.dtype), vw)
    return o.reshape(B, T, ATTN_Q_HEADS * hd)


def setup_inputs(seed: int = 0) -> dict:
    key = jax.random.key(seed)
    ks = jax.random.split(key, 32)
    L, D = DEPTH, D_MODEL
    nrm = lambda k, shape, s: jax.random.normal(k, shape, jnp.float32) * s
    return {
        "x": nrm(ks[0], (BATCH, SEQ, D), 1.0),
        "c": nrm(ks[1], (BATCH, D), 1.0),
        "w_ada": nrm(ks[2], (L, D, 6 * D), 0.5 * D ** -0.5),
        "b_ada": nrm(ks[3], (L, 6 * D), 0.01),
        "norm1_gain": 1.0 + nrm(ks[4], (L, D), 0.02),
        "w_in": nrm(ks[5], (L, D, IN_COLS), D ** -0.5),
        "b_in": nrm(ks[6], (L, IN_COLS), 0.01),
        "rwkv_mix": jax.random.uniform(ks[7], (L, RWKV_COLS), jnp.float32, 0.0, 1.0),
        "rwkv_w0": jax.random.uniform(ks[8], (L, RWKV_DIM), jnp.float32, -6.0, -1.0),
        "rwkv_w2": nrm(ks[9], (L, DECAY_LORA, RWKV_DIM), 0.5 * DECAY_LORA ** -0.5),
        "rwkv_a0": nrm(ks[10], (L, RWKV_DIM), 0.1),
        "rwkv_a2": nrm(ks[11], (L, ICLR_LORA, RWKV_DIM), 0.5 * ICLR_LORA ** -0.5),
        "rwkv_g2": nrm(ks[12], (L, GATE_LORA, RWKV_DIM), GATE_LORA ** -0.5),
        "rwkv_k_k": 0.85 + nrm(ks[13], (L, RWKV_DIM), 0.05),
        "rwkv_k_a": 1.0 + nrm(ks[14], (L, RWKV_DIM), 0.05),
        "rwkv_r_k": -0.04 + nrm(ks[15], (L, RWKV_HEADS, RWKV_HEAD_DIM), 0.02),
        "rwkv_ln_w": 1.0 + nrm(ks[16], (L, RWKV_DIM), 0.02),
        "rwkv_ln_b": nrm(ks[17], (L, RWKV_DIM), 0.01),
        "attn_sinks": nrm(ks[18], (L, ATTN_Q_HEADS), 1.0),
        "w_branch_rwkv": nrm(ks[19], (L, RWKV_DIM, D), RWKV_DIM ** -0.5),
        "w_branch_attn": nrm(ks[20], (L, ATTN_Q_COLS, D), ATTN_Q_COLS ** -0.5),
        "w_out": nrm(ks[21], (L, D, D), D ** -0.5),
        "norm2_gain": 1.0 + nrm(ks[22], (L, D), 0.02),
        "w_up": nrm(ks[23], (L, D, FFN_DIM), D ** -0.5),
        "w_down": nrm(ks[24], (L, FFN_DIM, D), FFN_DIM ** -0.5),
        "final_gain": 1.0 + nrm(ks[25], (D,), 0.02),
    }


def reference(x, c, w_ada, b_ada, norm1_gain, w_in, b_in, rwkv_mix, rwkv_w0, rwkv_w2,
              rwkv_a0, rwkv_a2, rwkv_g2, rwkv_k_k, rwkv_k_a, rwkv_r_k, rwkv_ln_w, rwkv_ln_b,
              attn_sinks, w_branch_rwkv, w_branch_attn, w_out, norm2_gain, w_up, w_down,
              final_gain):
    B, T, D = x.shape
    c_act = jax.nn.silu(c)
    for l in range(DEPTH):
        mod = c_act @ w_ada[l] + b_ada[l]
        sh1, sc1, gt1, sh2, sc2, gt2 = [m[:, None, :] for m in jnp.split(mod, 6, axis=-1)]

        h = rms_norm(x, norm1_gain[l]) * (1.0 + sc1) + sh1
        p = h @ w_in[l] + b_in[l]
        p_rwkv = p[..., :RWKV_COLS]
        p_attn = p[..., RWKV_COLS:RWKV_COLS + ATTN_COLS]
        p_gate = p[..., RWKV_COLS + ATTN_COLS:]

        y_rwkv = rwkv7_branch(p_rwkv, rwkv_mix[l], rwkv_w0[l], rwkv_w2[l], rwkv_a0[l],
                              rwkv_a2[l], rwkv_g2[l], rwkv_k_k[l], rwkv_k_a[l], rwkv_r_k[l],
                              rwkv_ln_w[l], rwkv_ln_b[l])
        q = p_attn[..., :ATTN_Q_COLS].reshape(B, T, ATTN_Q_HEADS, ATTN_HEAD_DIM)
        k = p_attn[..., ATTN_Q_COLS:ATTN_Q_COLS + ATTN_KV_COLS].reshape(B, T, ATTN_KV_HEADS, ATTN_HEAD_DIM)
        v = p_attn[..., ATTN_Q_COLS + ATTN_KV_COLS:].reshape(B, T, ATTN_KV_HEADS, ATTN_HEAD_DIM)
        y_attn = sliding_window_sink_attention(q, k, v, attn_sinks[l])

        gates = jax.nn.sigmoid(p_gate)
        g_rwkv, g_attn = gates[..., :D], gates[..., D:]
        merged = g_rwkv * (y_rwkv @ w_branch_rwkv[l]) + g_attn * (y_attn @ w_branch_attn[l])
        x = x + gt1 * (merged @ w_out[l])

        h = rms_norm(x, norm2_gain[l]) * (1.0 + sc2) + sh2
        x = x + gt2 * (jnp.square(jax.nn.relu(h @ w_up[l])) @ w_down[l])
    return rms_norm(x, final_gain)
```

```python
import math
from contextlib import ExitStack

import numpy as np
import concourse.bass as bass
import concourse.mybir as mybir
from concourse.bass_utils import run_bass_kernel_spmd

F32 = mybir.dt.float32
BF16 = mybir.dt.bfloat16
AF = mybir.ActivationFunctionType
ALU = mybir.AluOpType
AX = mybir.AxisListType

D = 2048
HD = 64
NBLK_MIX = 8
CDEC = math.exp(-0.5)
NEG = -1.0e6


class Buf:
    __slots__ = ("w", "r")

    def __init__(self):
        self.w = None
        self.r = []


class V:
    __slots__ = ("ap", "bufs")

    def __init__(self, ap, bufs):
        self.ap, self.bufs = ap, bufs

    def __getitem__(self, idx):
        return V(self.ap[idx], self.bufs)

    def re(self, pat, **kw):
        return V(self.ap.rearrange(pat, **kw), self.bufs)


class TT:
    def __init__(self, t, slots=0):
        self.t = t
        self.ap = t[:] if not isinstance(t, bass.AP) else t
        self.slots = slots
        self.bufs = [Buf() for _ in range(max(1, slots))]

    @property
    def v(self):
        return V(self.ap, self.bufs)

    def __getitem__(self, idx):
        return V(self.ap[idx], self.bufs)

    def s(self, i, n=1):
        if n == 1:
            return V(self.ap[:, i], self.bufs[i:i + 1])
        return V(self.ap[:, i:i + n], self.bufs[i:i + n])


class Eng:
    def __init__(self, name, sem):
        self.name, self.sem = name, sem
        self.n = 0
        self.seen = {}
        self.prog = []


class Sync:
    ENG = ("pe", "act", "dve", "pool", "sp")

    def __init__(self, nc, sems):
        self.nc = nc
        self.E = {n: Eng(n, s) for n, s in zip(self.ENG, sems[:5])}
        self.free_sems = list(sems[5:])
        self.dmasem = {}
        self.block = None
        self.ninst = 0
        self.limit = None

    def _wait(self, e, ev):
        if ev is None:
            return
        key, cnt = ev
        if key == e.name and (key == "pe" or cnt > e.n):
            return
        if e.seen.get(key, 0) >= cnt:
            return
        e.seen[key] = cnt
        sem = self.E[key].sem if key in self.E else self.dmasem[key][0]
        e.prog.append(("wait_ge", (sem, cnt), {}, None))

    def _deps(self, e, reads, writes):
        for b in reads:
            self._wait(e, b.w)
        for b in writes:
            self._wait(e, b.w)
            for ev in b.r:
                self._wait(e, ev)

    def _record(self, ev, reads, writes):
        for b in reads:
            b.r.append(ev)
            if len(b.r) > 16:
                m = {}
                for k, c in b.r:
                    m[k] = max(m.get(k, 0), c)
                b.r = list(m.items())
        for b in writes:
            b.w = ev
            b.r = []

    def op(self, ename, method, outs, ins, args=(), kwargs=None, signal=True):
        if self.limit is not None and self.ninst >= self.limit:
            return
        e = self.E[ename]
        reads = [b for v in ins if isinstance(v, V) for b in v.bufs]
        writes = [b for v in outs for b in v.bufs]
        self._deps(e, reads, writes)
        if signal or True:
            e.n += 1
            ev = (ename, e.n)
            e.prog.append((method, args, kwargs or {}, (e.sem, 1)))
        else:
            ev = (ename, e.n + 1)
            e.prog.append((method, args, kwargs or {}, None))
        self._record(ev, reads, writes)
        self.ninst += 1

    def dma(self, qname, key, out, in_, outs=(), ins=()):
        e = self.E[qname]
        reads = [b for v in ins for b in v.bufs]
        writes = [b for v in outs for b in v.bufs]
        self._deps(e, reads, writes)
        if key not in self.dmasem:
            self.dmasem[key] = [self.free_sems.pop(), 0]
        ds = self.dmasem[key]
        ds[1] += 16
        e.prog.append(("dma_start", (), dict(out=out, in_=in_), (ds[0], 16)))
        self._record((key, ds[1]), reads, writes)

    def dma_split(self, qname, key, out, in_, outs, nk, step=4):
        for k0 in range(0, nk, step):
            k1 = min(nk, k0 + step)
            self.dma(qname, key, out[:, k0:k1, :], in_[:, k0:k1, :], outs=outs)

    def flush(self):
        decs = dict(pe=self.block.tensor, act=self.block.scalar, dve=self.block.vector,
                    pool=self.block.gpsimd, sp=self.block.sync)
        for name in self.ENG:
            e = self.E[name]
            if not e.prog:
                continue
            prog = e.prog
            e.prog = []

            def body(eng, prog=prog):
                for method, args, kwargs, inc in prog:
                    ins = getattr(eng, method)(*args, **kwargs)
                    if inc is not None:
                        ins.then_inc(inc[0], inc[1])
            decs[name](body)

    def barrier(self):
        evs = [(n, self.E[n].n) for n in self.ENG if self.E[n].n > 0]
        evs += [(k, v[1]) for k, v in self.dmasem.items()]
        for n in self.ENG:
            for ev in evs:
                if ev[0] != n:
                    self._wait(self.E[n], ev)
        self.flush()

    def mm(self, out, lhsT, rhs, start=True, stop=True, signal=True):
        self.op("pe", "matmul", [out], [lhsT, rhs], (out.ap, lhsT.ap, rhs.ap),
                dict(start=start, stop=stop), signal=signal)

    def tr(self, out, in_, ident, signal=True):
        self.op("pe", "transpose", [out], [in_, ident], (out.ap, in_.ap, ident.ap), signal=signal)

    def act(self, out, in_, func, bias=None, scale=None):
        kw = {}
        ins = [in_]
        if bias is not None:
            kw["bias"] = bias.ap if isinstance(bias, V) else bias
            ins.append(bias)
        if scale is not None:
            kw["scale"] = scale.ap if isinstance(scale, V) else scale
            ins.append(scale)
        self.op("act", "activation", [out], ins, (out.ap, in_.ap, func), kw)

    def tt(self, eng, out, in0, in1, op):
        self.op(eng, "tensor_tensor", [out], [in0, in1], (out.ap, in0.ap, in1.ap, op))

    def ts(self, eng, out, in0, s1, s2=None, op0=ALU.mult, op1=None):
        a1 = s1.ap if isinstance(s1, V) else s1
        a2 = s2.ap if isinstance(s2, V) else s2
        kw = {}
        if op1 is not None:
            kw["op1"] = op1
        self.op(eng, "tensor_scalar", [out], [in0, s1, s2], (out.ap, in0.ap, a1, a2, op0), kw)

    def stt(self, out, in0, scalar, in1, op0, op1):
        sc = scalar.ap if isinstance(scalar, V) else scalar
        self.op("dve", "scalar_tensor_tensor", [out], [in0, scalar, in1],
                (out.ap, in0.ap, sc, in1.ap, op0, op1))

    def copy(self, eng, out, in_):
        if eng == "act":
            self.op("act", "copy", [out], [in_], (out.ap, in_.ap))
        else:
            self.op(eng, "tensor_copy", [out], [in_], (out.ap, in_.ap))

    def memset(self, eng, out, val):
        self.op(eng, "memset", [out], [], (out.ap, val))


def col_blocks():
    blocks = []
    for i in range(8):
        blocks.append(("r%d" % i, [(128 * i, 128)]))
    for i in range(8):
        blocks.append(("k%d" % i, [(1024 + 128 * i, 128)]))
    for i in range(8):
        blocks.append(("v%d" % i, [(2048 + 128 * i, 128)]))
    blocks.append(("wa", [(3072, 128)]))
    blocks.append(("g0", [(3200, 128)]))
    blocks.append(("g1", [(3328, 32)]))
    for i in range(8):
        blocks.append(("q%d" % i, [(3360 + 128 * i, 128)]))
    blocks.append(("ka0", [(4384, 64), (4384, 64)]))
    blocks.append(("ka1", [(4448, 64), (4448, 64)]))
    blocks.append(("va", [(4512, 128)]))
    for i in range(32):
        blocks.append(("gt%d" % i, [(4640 + 128 * i, 128)]))
    return blocks


CB = col_blocks()
CBI = {name: i for i, (name, _) in enumerate(CB)}
NCB = len(CB)

HEAD_ORDER = [8 * g + 2 * j + par for g in range(2) for par in range(2) for j in range(4)]
SLOPES = [2.0 ** (-8.0 * (h + 1) / 16.0) for h in range(16)]


def pmaj(vec, n):
    return np.ascontiguousarray(np.asarray(vec, np.float32).reshape(n, 128).T)


def make_consts(TB):
    p = np.arange(128)[:, None]
    f = np.arange(128)[None, :]
    c = {}
    c["ident"] = (p == f).astype(np.float32)
    msu = (f > p).astype(np.float32)
    mu = (f >= p).astype(np.float32)
    msl = (p > f).astype(np.float32)
    bd = ((p // 32) == (f // 32)).astype(np.float32)
    c["gm"] = np.ascontiguousarray(np.concatenate([msu, mu, msu, mu], 1))
    c["gmn"] = np.ascontiguousarray(np.concatenate([msu * bd, mu, msu * bd, mu], 1))
    c["msl2d"] = np.ascontiguousarray(np.concatenate([msl * bd, msl * bd], 1))
    c["msl2o"] = np.ascontiguousarray(np.concatenate([msl * (1 - bd), msl * (1 - bd)], 1))
    cm = np.ones((128, TB), np.float32)
    cm[:, ::128] = 0.0
    c["chunkmask"] = cm
    c["identfold"] = ((p % 64) == np.arange(64)[None, :]).astype(np.float32)
    c["headsel"] = ((p // 64) == np.arange(2)[None, :]).astype(np.float32)
    c["blkones"] = ((p // 64) == (f // 64)).astype(np.float32)
    dist_prev = f + 128 - p
    c["negdp"] = np.where(p > f, -dist_prev, NEG).astype(np.float32)
    dist_cur = f - p
    c["negdc"] = np.where(p <= f, -dist_cur, NEG).astype(np.float32)
    c["onesdiv"] = np.full((128, 128), 1.0 / D, np.float32)
    return c


def host_inputs(inp, NT):
    TB = 128 * NT
    x = np.asarray(inp["x"], np.float32)
    B, T, _ = x.shape
    assert T == 8 * TB and B == 2
    shared = {}
    shared.update(make_consts(TB))
    shared["w_ada"] = np.ascontiguousarray(inp["w_ada"][0])
    shared["w_in"] = np.ascontiguousarray(inp["w_in"][0])
    shared["w_br"] = np.ascontiguousarray(inp["w_branch_rwkv"][0])
    shared["w_ba"] = np.ascontiguousarray(inp["w_branch_attn"][0])
    shared["w_out"] = np.ascontiguousarray(inp["w_out"][0])
    shared["w_up"] = np.ascontiguousarray(inp["w_up"][0])
    shared["w_down"] = np.ascontiguousarray(inp["w_down"][0])
    shared["b_ada_l"] = pmaj(inp["b_ada"][0], 96)
    shared["gain1_l"] = pmaj(inp["norm1_gain"][0], 16)
    shared["gain2_l"] = pmaj(inp["norm2_gain"][0], 16)
    shared["gainf_l"] = pmaj(inp["final_gain"], 16)
    b_in = np.asarray(inp["b_in"][0], np.float32)
    bl = np.zeros((128, NCB), np.float32)
    for i, (_, pieces) in enumerate(CB):
        off = 0
        for st, wd in pieces:
            bl[off:off + wd, i] = b_in[st:st + wd]
            off += wd
    shared["b_in_l"] = bl
    mix = np.asarray(inp["rwkv_mix"][0], np.float32)
    ml = np.zeros((128, 27), np.float32)
    for i in range(27):
        st, wd = CB[i][1][0]
        ml[:wd, i] = mix[st:st + wd]
    shared["mix_l"] = ml
    shared["w0_l"] = pmaj(inp["rwkv_w0"][0], 8)
    shared["a0_l"] = pmaj(inp["rwkv_a0"][0], 8)
    shared["kk_l"] = pmaj(inp["rwkv_k_k"][0], 8)
    shared["ka_l"] = pmaj(inp["rwkv_k_a"][0], 8)
    shared["rk_l"] = pmaj(np.asarray(inp["rwkv_r_k"][0]).reshape(-1), 8)
    z64 = np.zeros((64, 1024), np.float32)
    shared["w2z"] = np.ascontiguousarray(np.concatenate([np.asarray(inp["rwkv_w2"][0], np.float32), z64], 0))
    shared["a2z"] = np.ascontiguousarray(np.concatenate([z64, np.asarray(inp["rwkv_a2"][0], np.float32)], 0))
    g2 = np.asarray(inp["rwkv_g2"][0], np.float32)
    shared["g2a"] = np.ascontiguousarray(g2[:128])
    shared["g2b"] = np.ascontiguousarray(g2[128:160])
    shared["lnw_b"] = np.ascontiguousarray(np.broadcast_to(np.asarray(inp["rwkv_ln_w"][0], np.float32)[None], (128, 1024)))
    shared["lnb_b"] = np.ascontiguousarray(np.broadcast_to(np.asarray(inp["rwkv_ln_b"][0], np.float32)[None], (128, 1024)))
    sinks = np.asarray(inp["attn_sinks"][0], np.float32)[HEAD_ORDER]
    shared["sinks_b"] = np.ascontiguousarray(np.broadcast_to(sinks[None], (128, 16)))
    maps = []
    for cid in range(8):
        b, q = cid // 4, cid % 4
        m = dict(shared)
        xb = np.zeros((NBLK_MIX, TB, D), np.float32)
        fl = np.zeros((128, 8), np.float32)
        for j in range(NBLK_MIX):
            st = (2 * q - 6 + j) * TB
            if st >= 0:
                xb[j] = x[b, st:st + TB]
                fl[:, j] = 1.0
        m["xb"] = xb
        m["flags"] = fl
        m["cvec"] = pmaj(inp["c"][b], 16)
        maps.append(m)
    return maps


class StopBuild(Exception):
    pass


class StopInner(Exception):
    pass


def build(NT, stop=None):
    TB = 128 * NT

    stopflag = [False]

    def chki(tag):
        print("chk", tag, S.ninst)
        if stop is not None and stop == tag:
            raise StopInner()

    def chk(tag, soft=False):
        if stop is not None and stop == tag:
            if soft:
                stopflag[0] = True
                return True
            raise StopBuild()
        return False

    TO = 2 * TB
    GS = min(512, TO)
    NG = TO // GS
    nc = bass.Bass("TRN2", target_bir_lowering=False)

    def din(name, shape):
        return nc.dram_tensor(name, list(shape), F32, kind="ExternalInput").ap()

    xb = din("xb", [NBLK_MIX, TB, D])
    d_flags = din("flags", [128, 8])
    d_cvec = din("cvec", [128, 16])
    w_ada = din("w_ada", [D, 6 * D])
    w_in = din("w_in", [D, 8736])
    w_br = din("w_br", [1024, D])
    w_ba = din("w_ba", [1024, D])
    w_out = din("w_out", [D, D])
    w_up = din("w_up", [D, 4 * D])
    w_down = din("w_down", [4 * D, D])
    small = {}
    for name, shape in [("b_ada_l", [128, 96]), ("gain1_l", [128, 16]), ("gain2_l", [128, 16]),
                        ("gainf_l", [128, 16]), ("b_in_l", [128, NCB]), ("mix_l", [128, 27]),
                        ("w0_l", [128, 8]), ("a0_l", [128, 8]), ("kk_l", [128, 8]), ("ka_l", [128, 8]),
                        ("rk_l", [128, 8]), ("sinks_b", [128, 16]),
                        ("ident", [128, 128]), ("gm", [128, 512]), ("gmn", [128, 512]), ("msl2d", [128, 256]),
                        ("msl2o", [128, 256]),
                        ("chunkmask", [128, TB]), ("identfold", [128, 64]), ("headsel", [128, 2]),
                        ("blkones", [128, 128]), ("negdp", [128, 128]), ("negdc", [128, 128]),
                        ("onesdiv", [128, 128])]:
        small[name] = (din(name, shape), shape)
    d_w2z = din("w2z", [128, 1024])
    d_a2z = din("a2z", [128, 1024])
    d_g2a = din("g2a", [128, 1024])
    d_g2b = din("g2b", [32, 1024])
    d_lnw = din("lnw_b", [128, 1024])
    d_lnb = din("lnb_b", [128, 1024])
    out = nc.dram_tensor("out", [TO, D], F32, kind="ExternalOutput").ap()

    win_v = w_in.rearrange("(kc p) n -> p kc n", p=128)

    with ExitStack() as es:
        sems = [es.enter_context(nc.semaphore("s%d" % i)) for i in range(40)]
        S = Sync(nc, sems)
        if isinstance(stop, str) and stop.startswith("n"):
            S.limit = int(stop[1:])
        S.block = es.enter_context(nc.Block())

        def sb(st, name, shape, dt=F32, slots=0):
            return TT(st.enter_context(nc.sbuf_tensor("sb_" + name, list(shape), dt)), slots)

        psf = [TT(es.enter_context(nc.psum_tensor("psf%d" % i, [128, 512], F32))) for i in range(6)]
        psb = [TT(es.enter_context(nc.psum_tensor("psb%d" % i, [128, 1024], BF16))) for i in range(2)]
        pcnt = [0, 0]

        def PF():
            pcnt[0] += 1
            return psf[pcnt[0] % 6]

        def PB():
            pcnt[1] += 1
            return psb[pcnt[1] % 2]

        C = {}
        for name, (ap, shape) in small.items():
            C[name] = sb(es, "c_" + name, shape)
        for name, (ap, shape) in small.items():
            S.dma("sp", "const", C[name].ap, ap, outs=[C[name].v])
        identb = sb(es, "identb", [128, 128], BF16)
        blkones_b = sb(es, "blkonesb", [128, 128], BF16)
        headsel_b = sb(es, "headselb", [128, 2], BF16)
        flags = sb(es, "flags", [128, 8])
        cvec = sb(es, "cvec", [128, 16])
        S.dma("sp", "const", flags.ap, d_flags, outs=[flags.v])
        S.dma("sp", "const", cvec.ap, d_cvec, outs=[cvec.v])
        fin = ("const", S.dmasem["const"][1])
        for t in list(C.values()) + [flags, cvec]:
            t.bufs[0].w = fin
        S.copy("dve", identb.v, C["ident"].v)
        S.copy("dve", blkones_b.v, C["blkones"].v)
        S.copy("dve", headsel_b.v, C["headsel"].v)
        identf = C["ident"]
        yT = sb(es, "yT", [128, 16, TO], BF16, slots=16)
        ST = sb(es, "ST", [64, 16, 64])
        S.memset("pool", ST.v, 0.0)
        mod = sb(es, "mod", [128, 96])
        g1 = sb(es, "g1", [128, 16])
        g2 = sb(es, "g2", [128, 16])
        eps_n = sb(es, "eps_n", [128, 1])
        eps_g = sb(es, "eps_g", [128, 1])
        S.memset("pool", eps_n.v, 1e-6)
        S.memset("pool", eps_g.v, 64e-5)
        omka = sb(es, "omka", [128, 8])
        S.ts("dve", omka.v, C["ka_l"].v, -1.0, 1.0, op0=ALU.mult, op1=ALU.add)
        bq8 = sb(es, "bq8", [128, 8])
        S.ts("dve", bq8.v, C["b_in_l"][:, CBI["q0"]:CBI["q0"] + 8], 0.125, None, op0=ALU.mult)
        esink = sb(es, "esink", [128, 16])
        S.act(esink.v, C["sinks_b"].v, AF.Exp)
        hb = sb(es, "hb", [128, 1])
        S.ts("dve", hb.v, flags[:, 5:6], -1.0, 30000.0, op0=ALU.add, op1=ALU.mult)

        try:
            chk("c0")
            with ExitStack() as p0:
                cact = sb(p0, "cact", [128, 16])
                S.act(cact.v, cvec.v, AF.Silu)
                wsl = [sb(p0, "wada%d" % i, [128, 16, 512]) for i in range(2)]
                wada_v = w_ada.rearrange("(kc p) n -> p kc n", p=128)
                modps = PF()
                for s in range(24):
                    sl = wsl[s % 2]
                    for k4 in range(4):
                        S.dma("sp", "wada%d" % (s % 2), sl.ap[:, 4 * k4:4 * k4 + 4, :],
                              wada_v[:, 4 * k4:4 * k4 + 4, s * 512:(s + 1) * 512], outs=[sl.v])
                    for j in range(4):
                        col = 4 * s + j
                        for kc in range(16):
                            S.mm(modps[:, col:col + 1], sl[:, kc, j * 128:(j + 1) * 128], cact[:, kc:kc + 1],
                                 start=(kc == 0), stop=(kc == 15), signal=(kc == 15 and j == 3))
                S.tt("dve", mod.v, modps[:, 0:96], C["b_ada_l"].v, ALU.add)
                S.stt(g1.v, mod[:, 16:32], 1.0, C["gain1_l"].v, ALU.add, ALU.mult)
                S.stt(g2.v, mod[:, 64:80], 1.0, C["gain2_l"].v, ALU.add, ALU.mult)
                S.barrier()
            sh1, gt1, sh2, gt2 = mod[:, 0:16], mod[:, 32:48], mod[:, 48:64], mod[:, 80:96]
            chk("p0")

            with ExitStack() as p1:
                arena = sb(p1, "arena", [128, 16, TB], F32, slots=16)
                hTb = sb(p1, "hTb", [128, 16, TB], BF16, slots=16)
                pslab = [sb(p1, "pslab%d" % i, [128, 16, 384], BF16) for i in range(1)]
                mslab = [sb(p1, "mslab%d" % i, [128, 16, 128], BF16) for i in range(1)]
                xs = sb(p1, "xs", [128, D // 2])
                sqt = [sb(p1, "sqt%d" % i, [128, TB]) for i in range(2)]
                rstd = sb(p1, "rstd", [128, TB])
                ptmp = sb(p1, "ptmp", [128, TB + 1])
                carry = sb(p1, "carry", [128, 27])
                S.memset("pool", carry.v, 0.0)


                pass
                FB = []
                for par_ in range(2):
                    n_ = "_%d" % par_
                    fb = [sb(p1, "btm" + n_, [128, 2, TB], BF16), sb(p1, "ktm" + n_, [128, 2, TB], BF16),
                          sb(p1, "AR" + n_, [128, NT, 256], BF16), sb(p1, "Bh" + n_, [128, TB], BF16),
                          sb(p1, "Kh" + n_, [128, TB], BF16), sb(p1, "vbf" + n_, [128, TB], BF16),
                          sb(p1, "rkb" + n_, [128, TB], BF16), sb(p1, "dgh" + n_, [128, 2, NT, 64], BF16),
                          sb(p1, "dgl" + n_, [128, 2, NT, 64], BF16), sb(p1, "gL" + n_, [128, NT]),
                          sb(p1, "dg" + n_, [128, NT, 64]), sb(p1, "dgt" + n_, [128, NT, 64]),
                          sb(p1, "w2z" + n_, [128, 128], BF16), sb(p1, "a2z" + n_, [128, 128], BF16),
                          sb(p1, "g2a" + n_, [128, 128], BF16), sb(p1, "g2b" + n_, [32, 128], BF16),
                          sb(p1, "lnw" + n_, [128, 128]), sb(p1, "lnb" + n_, [128, 128])]
                    for t_ in (fb[0], fb[1], fb[7], fb[8]):
                        S.memset("pool", t_.v, 0.0)
                    FB.append(fb)
                Bh, Kh = FB[0][3], FB[0][4]
                NCH = 2
                wa_bf = sb(p1, "wa_bf", [128, TB], BF16)
                sgd0 = sb(p1, "sgd0", [128, TB], BF16)
                sgd1 = sb(p1, "sgd1", [32, TB], BF16)
                kk2 = sb(p1, "kk2", [128, TB], BF16)
                identfold_b = sb(p1, "identfoldb", [128, 64], BF16)
                S.copy("dve", identfold_b.v, C["identfold"].v)
                TOK = sb(p1, "TOK", [128, NT, 512], BF16, slots=NT)
                nbsb = [sb(p1, "nbsb%d" % i, [128, 512], BF16) for i in range(2)]
                kbsb = [sb(p1, "kbsb%d" % i, [128, 512], BF16) for i in range(2)]
                gsb = [sb(p1, "gsb%d" % i, [128, 256], BF16) for i in range(2)]
                ngb = [[sb(p1, "ngb%d_%d" % (i, j), [128, 512], BF16) for j in range(2)] for i in range(2)]
                rxb = [[sb(p1, "rxb%d_%d" % (i, j), [128, 512], BF16) for j in range(2)] for i in range(2)]
                PQ = sb(p1, "PQ", [64, NT, 256], F32, slots=NT)
                RP = sb(p1, "RP", [64, NT, 256], BF16, slots=NT)
                YP = sb(p1, "YP", [128, NT, 128], F32, slots=NT)
                STc = sb(p1, "STc", [64, NT, 128], BF16, slots=NT)
                yt = sb(p1, "yt", [128, 128])
                ysq = sb(p1, "ysq", [128, 128])
                ytb = sb(p1, "ytb", [128, 128], BF16)
                st1 = sb(p1, "st1", [128, 2])
                st2 = sb(p1, "st2", [128, 2])
                mean = sb(p1, "mean", [128, 2])
                msq = sb(p1, "msq", [128, 2])
                var = sb(p1, "var", [128, 2])
                bon = sb(p1, "bon", [128, 2])
                qT = sb(p1, "qT", [128, 8, TB], BF16, slots=8)
                kdup = [[sb(p1, "kdup%d_%d" % (g, par), [128, 128 + TB], BF16) for par in range(2)] for g in range(2)]
                vaT = kk2
                vaug = sb(p1, "vaug", [128, NT + 1, 2, 65], BF16)
                S.memset("pool", vaug.v, 1.0)
                for g in range(2):
                    for par in range(2):
                        S.memset("pool", kdup[g][par].v, 0.0)
                et = sqt if TB == 512 else [sb(p1, "et%d" % i, [128, 512]) for i in range(2)]
                PT = [Bh, Kh] if TB == 512 else [sb(p1, "PT%d" % i, [128, 512], BF16) for i in range(2)]
                den = sb(p1, "den", [128, 4])
                otok = sb(p1, "otok", [128, 1024], BF16)

                def A(i):
                    return arena.s(i)

                def load_cb(slab_v, key, name):
                    off = 0
                    for (st_, wd) in CB[CBI[name]][1]:
                        S.dma_split("pool", key, slab_v.ap[:, :, off:off + wd], win_v[:, :, st_:st_ + wd], [slab_v], 16)
                        off += wd
                    return off

                def proj_mix(slab_v, width):
                    ps = PF()
                    for kc in range(16):
                        S.mm(ps[0:width, 0:TB], slab_v[:, kc, 0:width], hTb.s(kc), start=(kc == 0), stop=(kc == 15),
                             signal=(kc == 15))
                    return ps[0:width, 0:TB]

                mcnt = [0]

                def misc_proj(name):
                    mcnt[0] += 1
                    sl = mslab[0]
                    wdt = load_cb(sl.v, "mslab0", name)
                    return proj_mix(sl.v, wdt), wdt

                tmp1 = sb(p1, "tmp1", [128, 1])

                def carry_last(slab_v, width, cbi, blk_):
                    ps = PF()
                    for kc in range(16):
                        S.mm(ps[0:width, 0:1], slab_v[:, kc, 0:width], hTb.s(kc)[:, TB - 1:TB], start=(kc == 0),
                             stop=(kc == 15))
                    S.act(tmp1[0:width, :], ps[0:width, 0:1], AF.Identity, bias=C["b_in_l"][0:width, cbi:cbi + 1])
                    S.ts("dve", carry[0:width, cbi:cbi + 1], tmp1[0:width, :], flags[0:width, blk_:blk_ + 1], None,
                         op0=ALU.mult)

                try:
                    for blk in range(NBLK_MIX):
                        own = blk >= 6
                        halo = blk == 5
                        ob = blk - 6
                        vflag = flags[:, blk:blk + 1]

                        def lerp(psv, cbi, width, out_v):
                            S.act(ptmp[0:width, 1:TB + 1], psv, AF.Identity, bias=C["b_in_l"][0:width, cbi:cbi + 1])
                            S.copy("pool", ptmp[0:width, 0:1], carry[0:width, cbi:cbi + 1])
                            dt_ = A(13)[0:width, :]
                            S.tt("dve", dt_, ptmp[0:width, 0:TB], ptmp[0:width, 1:TB + 1], ALU.subtract)
                            S.stt(out_v, dt_, C["mix_l"][0:width, cbi:cbi + 1], ptmp[0:width, 1:TB + 1], ALU.mult, ALU.add)
                            S.ts("dve", carry[0:width, cbi:cbi + 1], ptmp[0:width, TB:TB + 1], flags[0:width, blk:blk + 1],
                                 None, op0=ALU.mult)

                        for n in range(NT):
                            for g4 in range(4):
                                if g4 % 2 == 0:
                                    S.dma("sp", "xs", xs.ap, xb[blk, n * 128:(n + 1) * 128, g4 * 512:g4 * 512 + 1024], outs=[xs.v])
                                ps = PF()
                                for j in range(4):
                                    dc = 4 * g4 + j
                                    dl = dc % 8
                                    S.tr(ps[:, j * 128:(j + 1) * 128], xs[:, dl * 128:(dl + 1) * 128], identf.v,
                                         signal=(j == 3))
                                dst = arena.s(4 * g4, 4)[:, :, n * 128:(n + 1) * 128]
                                S.copy("act" if g4 % 2 else "dve", dst, ps[:, 0:512].re("p (j t) -> p j t", j=4))
                        if blk == 0:
                            chki("b0a")
                        msps = PF()
                        for dc in range(16):
                            sq = sqt[dc % 2]
                            S.act(sq.v, A(dc), AF.Square)
                            S.mm(msps[:, 0:TB], C["onesdiv"].v, sq.v, start=(dc == 0), stop=(dc == 15), signal=(dc == 15))
                        S.act(rstd.v, msps[:, 0:TB], AF.Sqrt, bias=eps_n.v)
                        S.op("dve", "reciprocal", [rstd.v], [rstd.v], (rstd.ap, rstd.ap))
                        for dc in range(16):
                            tmp = sqt[dc % 2]
                            S.tt("dve", tmp.v, A(dc), rstd.v, ALU.mult)
                            S.act(hTb.s(dc), tmp.v, AF.Identity, bias=sh1[:, dc:dc + 1], scale=g1[:, dc:dc + 1])
                        if blk == 0:
                            chki("b0b")
                        psv, _ = misc_proj("wa")
                        lerp(psv, CBI["wa"], 128, A(15))
                        S.act(wa_bf[0:64, :], A(15)[0:64, :], AF.Tanh)
                        S.copy("act", wa_bf[64:128, :], A(15)[64:128, :])
                        if halo:
                            for nm_ in ("g0", "g1"):
                                mcnt[0] += 1
                                sl_ = mslab[0]
                                wdt_ = load_cb(sl_.v, "mslab0", nm_)
                                carry_last(sl_.v, wdt_, CBI[nm_], blk)
                        if own:
                            psv, _ = misc_proj("g0")
                            lerp(psv, CBI["g0"], 128, A(15))
                            S.act(sgd0.v, A(15), AF.Sigmoid)
                            psv, _ = misc_proj("g1")
                            lerp(psv, CBI["g1"], 32, A(15)[0:32, :])
                            S.act(sgd1.v, A(15)[0:32, :], AF.Sigmoid)
                        if blk == 0:
                            chki("b0c")
                        def front(hp, par):
                            btm, ktm, AR, Bh, Kh, vbf, rkb, dgh, dgl, gL, dg, dgt, w2z, a2z, g2a, g2b, lnw, lnb = FB[par]
                            yield
                            sl = pslab[0]
                            yield
                            key = "pslab0"
                            yield
                            hc = slice(hp * 128, (hp + 1) * 128)
                            yield
                            if own or halo:
                                S.dma_split("pool", key, sl.ap[:, :, 0:128], win_v[:, :, hp * 128:(hp + 1) * 128], [sl.v], 16)
                            yield
                            S.dma_split("pool", key, sl.ap[:, :, 128:256], win_v[:, :, 1024 + hp * 128:1024 + (hp + 1) * 128], [sl.v], 16)
                            yield
                            S.dma_split("pool", key, sl.ap[:, :, 256:384], win_v[:, :, 2048 + hp * 128:2048 + (hp + 1) * 128], [sl.v], 16)
                            yield
                            if halo:
                                carry_last(sl[:, :, 0:128], 128, hp, blk)
                            yield
                            if own:
                                lerp(proj_mix(sl[:, :, 0:128], 128), hp, 128, A(0))
                            yield
                            lerp(proj_mix(sl[:, :, 128:256], 128), 8 + hp, 128, A(1))
                            yield
                            lerp(proj_mix(sl[:, :, 256:384], 128), 16 + hp, 128, A(2))
                            yield
                            S.dma("pool", "w2z%d" % par, w2z.ap, d_w2z[:, hc], outs=[w2z.v])
                            yield
                            S.dma("pool", "a2z%d" % par, a2z.ap, d_a2z[:, hc], outs=[a2z.v])
                            yield
                            if own:
                                S.dma("pool", "g2a%d" % par, g2a.ap, d_g2a[:, hc], outs=[g2a.v])
                                S.dma("pool", "g2b%d" % par, g2b.ap, d_g2b[:, hc], outs=[g2b.v])
                            yield
                            ps = PF()
                            yield
                            S.mm(ps[:, 0:TB], w2z.v, wa_bf.v)
                            yield
                            S.act(A(3), ps[:, 0:TB], AF.Sigmoid, bias=C["w0_l"][:, hp:hp + 1])
                            yield
                            ps = PF()
                            yield
                            S.mm(ps[:, 0:TB], a2z.v, wa_bf.v)
                            yield
                            S.act(A(5), ps[:, 0:TB], AF.Sigmoid, bias=C["a0_l"][:, hp:hp + 1])
                            yield
                            S.op("dve", "tensor_tensor_scan", [A(4)], [C["chunkmask"].v, A(3)],
                                 (A(4).ap, C["chunkmask"].ap, A(3).ap, 0.0, ALU.mult, ALU.add))
                            yield
                            c3 = "p (c t) -> p c t"
                            yield
                            S.act(A(9), A(4), AF.Exp, scale=-CDEC)
                            yield
                            if own:
                                S.tt("pool", AR[:, :, 128:256], A(0).re(c3, c=NT), A(9).re(c3, c=NT), ALU.mult)
                            yield
                            S.tt("pool", A(14), A(4), A(3), ALU.subtract)
                            yield
                            S.act(A(10), A(14), AF.Exp, scale=-CDEC)
                            yield
                            S.act(gL.v, A(4).re(c3, c=NT)[:, :, 127], AF.Exp, scale=-CDEC)
                            yield
                            S.act(A(9), A(4), AF.Exp, scale=CDEC)
                            yield
                            S.ts("pool", A(6), A(1), C["kk_l"][:, hp:hp + 1], None, op0=ALU.mult)
                            yield
                            S.act(kk2.v, A(6), AF.Square)
                            yield
                            ps = PF()
                            yield
                            S.mm(ps[:, 0:TB], blkones_b.v, kk2.v)
                            yield
                            S.act(A(14), ps[:, 0:TB], AF.Sqrt)
                            yield
                            S.ts("dve", A(14), A(14), 1e-12, None, op0=ALU.max)
                            yield
                            S.op("dve", "reciprocal", [A(14)], [A(14)], (A(14).ap, A(14).ap))
                            yield
                            S.tt("dve", A(6), A(6), A(14), ALU.mult)
                            yield
                            S.ts("dve", A(7), A(5), C["ka_l"][:, hp:hp + 1], omka[:, hp:hp + 1], op0=ALU.mult, op1=ALU.add)
                            yield
                            S.tt("dve", A(7), A(1), A(7), ALU.mult)
                            yield
                            S.tt("pool", A(8), A(6), A(5), ALU.mult)
                            yield
                            S.stt(AR[:, :, 0:128], A(6).re(c3, c=NT), -1.0, A(10).re(c3, c=NT), ALU.mult, ALU.mult)
                            yield
                            S.tt("dve", A(11), A(8), A(9), ALU.mult)
                            yield
                            S.tt("pool", A(12), A(7), A(9), ALU.mult)
                            yield
                            for h in range(2):
                                hs = slice(64 * h, 64 * h + 64)
                                S.copy("act", btm[hs, h, :], A(11)[hs, :])
                                S.copy("act", ktm[hs, h, :], A(12)[hs, :])
                            yield
                            for c in range(NT):
                                cs_ = slice(c * 128, (c + 1) * 128)
                                S.ts("pool", Bh[:, cs_], A(11)[:, cs_], gL[:, c:c + 1], None, op0=ALU.mult)
                                S.ts("pool", Kh[:, cs_], A(12)[:, cs_], gL[:, c:c + 1], None, op0=ALU.mult)
                                S.ts("pool", dg[:, c, :], C["identfold"].v, gL[:, c:c + 1], None, op0=ALU.mult)
                            yield
                            for h in range(2):
                                hs = slice(64 * h, 64 * h + 64)
                                S.copy("act", dgh[hs, h], dg[hs])
                                S.tt("pool", dgt[hs], dg[hs], dgh[hs, h], ALU.subtract)
                                S.copy("act", dgl[hs, h], dgt[hs])
                            yield
                            S.ts("dve", vbf.v, A(2), vflag, None, op0=ALU.mult)
                            yield
                            if own:
                                S.stt(rkb.v, A(0), C["rk_l"][:, hp:hp + 1], A(7), ALU.mult, ALU.mult)
                        def back(hp, par):
                            btm, ktm, AR, Bh, Kh, vbf, rkb, dgh, dgl, gL, dg, dgt, w2z, a2z, g2a, g2b, lnw, lnb = FB[par]
                            hc = slice(hp * 128, (hp + 1) * 128)
                            yield
                            W = 256 if own else 128
                            yield
                            h3 = "p (h f) -> p h f"
                            yield
                            h4 = "p (h f) -> p h f"
                            yield
                            csl = [slice(c * 128, (c + 1) * 128) for c in range(NT)]
                            yield
                            for cg in range(0, NT, NCH):
                                yield
                                CR = range(cg, min(NT, cg + NCH))
                                yield
                                for c in CR:
                                    yield
                                    pb = PB()
                                    yield
                                    S.tr(pb[:, 0:128], Bh[:, csl[c]], identb.v)
                                    yield
                                    S.tr(pb[:, 128:256], Kh[:, csl[c]], identb.v)
                                    yield
                                    S.tr(pb[:, 256:384], AR[:, c, 0:128], identb.v)
                                    yield
                                    S.tr(pb[:, 384:512], vbf[:, csl[c]], identb.v)
                                    yield
                                    S.copy("act", TOK.s(c), pb[:, 0:512])
                                yield
                                for c in CR:
                                    yield
                                    psg, psk, psn = PF(), PF(), PF()
                                    yield
                                    for h in range(2):
                                        S.mm(psg[:, h * 256:h * 256 + W], btm[:, h, csl[c]], AR[:, c, 0:W])
                                        S.mm(psk[:, h * 256:h * 256 + W], ktm[:, h, csl[c]], AR[:, c, 0:W])
                                        S.mm(psn[:, h * 128:(h + 1) * 128], AR[:, c, 0:128], btm[:, h, csl[c]])
                                    yield
                                    NBs, KBs, Gs = nbsb[c - cg], kbsb[c - cg], gsb[c - cg]
                                    yield
                                    S.tt("dve", NBs.v.re(h3, h=2)[:, :, 0:W], psg[:, 0:512].re(h3, h=2)[:, :, 0:W],
                                         C["gmn"].v.re(h3, h=2)[:, :, 0:W], ALU.mult)
                                    yield
                                    S.tt("dve", KBs.v.re(h3, h=2)[:, :, 0:W], psk[:, 0:512].re(h3, h=2)[:, :, 0:W],
                                         C["gm"].v.re(h3, h=2)[:, :, 0:W], ALU.mult)
                                    yield
                                    S.tt("dve", Gs.v, psn[:, 0:256], C["msl2d"].v, ALU.mult)
                                    yield
                                    S.tt("dve", rxb[c - cg][0].v.re(h4, h=2)[:, :, 128:256], psn[:, 0:256].re(h4, h=2),
                                         C["msl2o"].v.re(h4, h=2), ALU.mult)
                                yield
                                for c in CR:
                                    yield
                                    tok = TOK.s(c)
                                    yield
                                    RX = rxb[c - cg][0]
                                    yield
                                    psz = PF()
                                    yield
                                    for h in range(2):
                                        S.mm(psz[:, h * 64:(h + 1) * 64], kbsb[c - cg][:, h * 256:h * 256 + 128],
                                             tok[:, 384 + 64 * h:384 + 64 * h + 64])
                                    yield
                                    for h in range(2):
                                        S.copy("pool", RX[:, h * 256:h * 256 + 64], tok[:, 256 + 64 * h:256 + 64 * h + 64])
                                    yield
                                    S.copy("act", RX.v.re(h4, h=2)[:, :, 64:128], psz[:, 0:128].re(h4, h=2))
                                yield
                                Nv = {c: [nbsb[c - cg][:, h * 256:h * 256 + 128] for h in range(2)] for c in CR}
                                yield
                                Gv = {c: [gsb[c - cg][:, h * 128:(h + 1) * 128] for h in range(2)] for c in CR}
                                yield
                                rxi = {c: 0 for c in CR}
                                yield
                                for k in range(5):
                                    yield
                                    for c in CR:
                                        RX = rxb[c - cg][rxi[c]]
                                        psr = PF()
                                        for h in range(2):
                                            S.mm(psr[:, h * 256:(h + 1) * 256], Nv[c][h], RX[:, h * 256:(h + 1) * 256])
                                        rxi[c] ^= 1
                                        S.tt("dve", rxb[c - cg][rxi[c]].v, psr[:, 0:512], RX.v, ALU.add)
                                    yield
                                    if k < 4:
                                        for c in CR:
                                            pss = PF()
                                            for h in range(2):
                                                S.mm(pss[:, h * 128:(h + 1) * 128], Gv[c][h], Nv[c][h])
                                                S.mm(pss[:, 256 + h * 128:256 + (h + 1) * 128], Nv[c][h], Gv[c][h])
                                            NGn = ngb[c - cg][k % 2]
                                            S.copy("act", NGn.v, pss[:, 0:512])
                                            Nv[c] = [NGn[:, h * 128:(h + 1) * 128] for h in range(2)]
                                            Gv[c] = [NGn[:, 256 + h * 128:256 + (h + 1) * 128] for h in range(2)]
                                yield
                                ETs = {c: ngb[c - cg][0][:, 0:256] for c in CR}
                                yield
                                E2T = {c: ngb[c - cg][0][:, 256:512] for c in CR}
                                yield
                                R2s = {c: ngb[c - cg][1][:, 0:256] for c in CR}
                                yield
                                Rs = {c: ngb[c - cg][1][:, 256:512] for c in CR}
                                yield
                                RXf = {c: rxb[c - cg][rxi[c]] for c in CR}
                                yield
                                for c in CR:
                                    yield
                                    pbe = PB()
                                    yield
                                    for h in range(2):
                                        S.tr(pbe[:, h * 128:(h + 1) * 128], RXf[c][:, h * 256 + 128:h * 256 + 256], identb.v)
                                    yield
                                    S.copy("act", ETs[c], pbe[:, 0:256])
                                yield
                                for c in CR:
                                    yield
                                    psr = PF()
                                    yield
                                    for h in range(2):
                                        S.mm(psr[:, h * 128:(h + 1) * 128], ETs[c][:, h * 128:(h + 1) * 128], RXf[c][:, h * 256:h * 256 + 128])
                                    yield
                                    S.tt("dve", R2s[c].re(h4, h=2), psr[:, 0:256].re(h4, h=2), RXf[c].v.re(h4, h=2)[:, :, 0:128], ALU.add)
                                    yield
                                    pse = PF()
                                    yield
                                    for h in range(2):
                                        S.mm(pse[:, h * 128:(h + 1) * 128], RXf[c][:, h * 256 + 128:h * 256 + 256], ETs[c][:, h * 128:(h + 1) * 128])
                                    yield
                                    S.copy("act", E2T[c], pse[:, 0:256])
                                yield
                                for c in CR:
                                    yield
                                    psr = PF()
                                    yield
                                    for h in range(2):
                                        S.mm(psr[:, h * 128:(h + 1) * 128], E2T[c][:, h * 128:(h + 1) * 128], R2s[c][:, h * 128:(h + 1) * 128])
                                    yield
                                    S.tt("dve", Rs[c], psr[:, 0:256], R2s[c], ALU.add)
                                yield
                                for c in CR:
                                    yield
                                    tok = TOK.s(c)
                                    yield
                                    R = Rs[c]
                                    yield
                                    NBs, KBs = nbsb[c - cg], kbsb[c - cg]
                                    yield
                                    pspq = PF()
                                    yield
                                    for h in range(2):
                                        o = h * 128
                                        S.mm(pspq[0:64, o:o + 64], R[:, o:o + 64], tok[:, 64 * h:64 * h + 64], start=True, stop=False)
                                        S.mm(pspq[0:64, o:o + 64], identfold_b.v, dgh[:, h, c, :], start=False, stop=False)
                                        S.mm(pspq[0:64, o:o + 64], identfold_b.v, dgl[:, h, c, :], start=False, stop=True)
                                        S.mm(pspq[0:64, o + 64:o + 128], tok[:, 64 * h:64 * h + 64], R[:, o + 64:o + 128], start=True, stop=False)
                                        S.mm(pspq[0:64, o + 64:o + 128], tok[:, 128 + 64 * h:128 + 64 * h + 64],
                                             tok[:, 384 + 64 * h:384 + 64 * h + 64], start=False, stop=True)
                                    yield
                                    S.copy("act", PQ.s(c), pspq[0:64, 0:256])
                                    yield
                                    if own:
                                        psrp = PF()
                                        for h in range(2):
                                            o = h * 128
                                            S.mm(psrp[0:64, o:o + 128], R[:, o:o + 64], NBs[:, h * 256 + 128:h * 256 + 256], start=True, stop=False)
                                            S.mm(psrp[0:64, o:o + 128], identb[:, 64 * h:64 * h + 64], AR[:, c, 128:256], start=False, stop=True)
                                        S.copy("act", RP.s(c), psrp[0:64, 0:256])
                                        psyp = PF()
                                        for h in range(2):
                                            o = h * 128
                                            S.mm(psyp[:, h * 64:(h + 1) * 64], NBs[:, h * 256 + 128:h * 256 + 256], R[:, o + 64:o + 128],
                                                 start=True, stop=False)
                                            S.mm(psyp[:, h * 64:(h + 1) * 64], KBs[:, h * 256 + 128:h * 256 + 256],
                                                 tok[:, 384 + 64 * h:384 + 64 * h + 64], start=False, stop=True)
                                        S.copy("dve", YP.s(c), psyp[:, 0:128])
                            yield
                            stv = ST[:, 2 * hp:2 * hp + 2, :]
                            yield
                            for c in range(NT):
                                yield
                                if own:
                                    yield
                                    S.copy("pool", STc.s(c).re("p (h i) -> p h i", h=2), stv)
                                yield
                                pst = PF()
                                yield
                                for h in range(2):
                                    yield
                                    S.mm(pst[:, h * 64:(h + 1) * 64], PQ.s(c)[:, h * 128:h * 128 + 128],
                                         ST[:, 2 * hp + h, :], signal=(h == 1))
                                yield
                                S.tt("dve", stv, pst[0:64, 0:128].re("p (h i) -> p h i", h=2),
                                     PQ.s(c).re("p (h f) -> p h f", h=2)[:, :, 64:128], ALU.add)
                            yield
                            if own:
                                yield
                                S.dma("sp", "lnw%d" % par, lnw.ap, d_lnw[:, hc], outs=[lnw.v])
                                yield
                                S.dma("sp", "lnb%d" % par, lnb.ap, d_lnb[:, hc], outs=[lnb.v])
                                yield
                                for c in range(NT):
                                    yield
                                    cs_ = slice(c * 128, (c + 1) * 128)
                                    yield
                                    tok = TOK.s(c)
                                    yield
                                    psy = PF()
                                    yield
                                    for h in range(2):
                                        S.mm(psy[:, h * 64:(h + 1) * 64], RP.s(c)[:, h * 128:(h + 1) * 128],
                                             STc.s(c)[:, h * 64:(h + 1) * 64], signal=(h == 1))
                                    yield
                                    S.tt("dve", yt.v, psy[:, 0:128], YP.s(c), ALU.add)
                                    yield
                                    y3 = yt.v.re("p (h i) -> p h i", h=2)
                                    yield
                                    S.op("dve", "tensor_reduce", [st1.v], [yt.v], (st1.ap, y3.ap), dict(axis=AX.X, op=ALU.add))
                                    yield
                                    S.act(ysq.v, yt.v, AF.Square)
                                    yield
                                    S.op("dve", "tensor_reduce", [st2.v], [ysq.v],
                                         (st2.ap, ysq.v.re("p (h i) -> p h i", h=2).ap), dict(axis=AX.X, op=ALU.add))
                                    yield
                                    S.ts("dve", mean.v, st1.v, 1.0 / 64, None, op0=ALU.mult)
                                    yield
                                    S.tt("dve", msq.v, mean.v, mean.v, ALU.mult)
                                    yield
                                    S.stt(var.v, st2.v, 1.0 / 64, msq.v, ALU.mult, ALU.subtract)
                                    yield
                                    S.act(var.v, var.v, AF.Sqrt, bias=eps_g.v)
                                    yield
                                    S.op("dve", "reciprocal", [var.v], [var.v], (var.ap, var.ap))
                                    yield
                                    for h in range(2):
                                        S.ts("dve", yt[:, h * 64:(h + 1) * 64], yt[:, h * 64:(h + 1) * 64], mean[:, h:h + 1],
                                             var[:, h:h + 1], op0=ALU.subtract, op1=ALU.mult)
                                    yield
                                    S.tt("pool", yt.v, yt.v, lnw.v, ALU.mult)
                                    yield
                                    S.tt("pool", yt.v, yt.v, lnb.v, ALU.add)
                                    yield
                                    psb_ = PF()
                                    yield
                                    S.mm(psb_[:, 0:2], rkb[:, cs_], headsel_b.v)
                                    yield
                                    S.copy("act", bon.v, psb_[:, 0:2])
                                    yield
                                    for h in range(2):
                                        S.stt(yt[:, h * 64:(h + 1) * 64], tok[:, 384 + 64 * h:384 + 64 * h + 64], bon[:, h:h + 1],
                                              yt[:, h * 64:(h + 1) * 64], ALU.mult, ALU.add)
                                    yield
                                    psg_ = PF()
                                    yield
                                    S.mm(psg_[:, 0:128], sgd0[:, cs_], g2a.v, start=True, stop=False, signal=False)
                                    yield
                                    S.mm(psg_[:, 0:128], sgd1[0:32, cs_], g2b.v, start=False, stop=True)
                                    yield
                                    S.tt("dve", ytb.v, psg_[:, 0:128], yt.v, ALU.mult)
                                    yield
                                    pb = PB()
                                    yield
                                    S.tr(pb[:, 0:128], ytb.v, identb.v)
                                    yield
                                    S.copy("act", yT.s(hp)[:, ob * TB + c * 128:ob * TB + (c + 1) * 128], pb[:, 0:128])
                        def run_rr(gens):
                            gens = list(gens)
                            while gens:
                                for g_ in list(gens):
                                    try:
                                        next(g_)
                                    except StopIteration:
                                        gens.remove(g_)

                        run_rr([front(0, 0)])
                        for hp in range(8):
                            gl = [back(hp, hp % 2)]
                            if hp + 1 < 8:
                                gl.append(front(hp + 1, (hp + 1) % 2))
                            run_rr(gl)
                        if blk == 0:
                            chki("b0f")
                        if halo or own:
                            for g in range(2):
                                psv, _ = misc_proj("ka%d" % g)
                                for par in range(2):
                                    pl = slice(64 * par, 64 * par + 64)
                                    S.act(kdup[g][par][pl, 128:128 + TB], psv[pl, :], AF.Identity,
                                          bias=C["b_in_l"][pl, CBI["ka%d" % g]:CBI["ka%d" % g] + 1])
                            psv, _ = misc_proj("va")
                            S.act(vaT.v, psv, AF.Identity, bias=C["b_in_l"][:, CBI["va"]:CBI["va"] + 1])
                            for n in range(NT):
                                pb = PB()
                                S.tr(pb[:, 0:128], vaT[:, n * 128:(n + 1) * 128], identb.v)
                                S.copy("dve", vaug[:, 1 + n, :, 0:64], pb[:, 0:128].re("p (g d) -> p g d", g=2))
                        if own:
                            for i in range(8):
                                psv, _ = misc_proj("q%d" % i)
                                S.act(qT.s(i), psv, AF.Identity, bias=bq8[:, i:i + 1], scale=0.125)
                            for n in range(NT):
                                for g in range(2):
                                    for par in range(2):
                                        pl = slice(64 * par, 64 * par + 64)
                                        ps0, ps1 = PF(), PF()
                                        qv = qT[:, 4 * g:4 * g + 4, n * 128:(n + 1) * 128]
                                        S.mm(ps0[:, 0:512].re("p (j t) -> p j t", j=4), kdup[g][par][:, n * 128:(n + 1) * 128], qv)
                                        S.mm(ps1[:, 0:512].re("p (j t) -> p j t", j=4), kdup[g][par][:, (n + 1) * 128:(n + 2) * 128], qv)
                                        for j in range(4):
                                            h = 8 * g + 2 * j + par
                                            js = slice(j * 128, (j + 1) * 128)
                                            S.stt(et[0][:, js], C["negdp"].v, SLOPES[h], ps0[:, js], ALU.mult, ALU.add)
                                            S.stt(et[1][:, js], C["negdc"].v, SLOPES[h], ps1[:, js], ALU.mult, ALU.add)
                                        if blk == 6 and n == 0:
                                            S.act(PT[0].v, et[0].v, AF.Exp, bias=hb.v)
                                        else:
                                            S.act(PT[0].v, et[0].v, AF.Exp)
                                        S.act(PT[1].v, et[1].v, AF.Exp)
                                        pso = PF()
                                        for j in range(4):
                                            js = slice(j * 128, (j + 1) * 128)
                                            S.mm(pso[:, j * 65:(j + 1) * 65], PT[0][:, js], vaug[:, n, g, :],
                                                 start=True, stop=False, signal=False)
                                            S.mm(pso[:, j * 65:(j + 1) * 65], PT[1][:, js], vaug[:, n + 1, g, :],
                                                 start=False, stop=True, signal=(j == 3))
                                        i0 = g * 8 + par * 4
                                        S.tt("dve", den.v, pso[:, 0:260].re("p (j e) -> p j e", j=4)[:, :, 64],
                                             esink[:, i0:i0 + 4], ALU.add)
                                        S.op("dve", "reciprocal", [den.v], [den.v], (den.ap, den.ap))
                                        for j in range(4):
                                            h = 8 * g + 2 * j + par
                                            S.act(otok[:, h * 64:(h + 1) * 64], pso[:, j * 65:j * 65 + 64], AF.Identity,
                                                  scale=den[:, j:j + 1])
                                for i in range(8):
                                    pb = PB()
                                    S.tr(pb[:, 0:128], otok[:, i * 128:(i + 1) * 128], identb.v)
                                    S.copy("act" if i % 2 else "dve",
                                           yT.s(8 + i)[:, ob * TB + n * 128:ob * TB + (n + 1) * 128], pb[:, 0:128])
                        if halo or own:
                            for g in range(2):
                                for par in range(2):
                                    pl = slice(64 * par, 64 * par + 64)
                                    S.copy("pool", kdup[g][par][pl, 0:128], kdup[g][par][pl, TB:TB + 128])
                            S.copy("pool", vaug[:, 0, :, 0:64], vaug[:, NT, :, 0:64])
                        S.flush()
                        if chk("blk%d" % blk, soft=True):
                            break
                except StopInner:
                    stopflag[0] = True
                S.barrier()
            if stopflag[0]:
                raise StopBuild()
            chk("p1")

            with ExitStack() as p2:
                slabs = [sb(p2, "slab%d" % i, [128, 16, 256], BF16) for i in range(4)]
                mg = sb(p2, "mg", [128, 16, TO], BF16, slots=16)
                scnt = [0]

                def slab_load(src_v, nk):
                    scnt[0] += 1
                    i = scnt[0] % 4
                    S.dma_split("pool", "slab%d" % i, slabs[i].ap[:, 0:nk, :], src_v, [slabs[i].v], nk)
                    return slabs[i]

                def slab_load2(src_a, src_b):
                    scnt[0] += 1
                    i = scnt[0] % 4
                    S.dma_split("pool", "slab%d" % i, slabs[i].ap[:, 0:8, :], src_a, [slabs[i].v], 8)
                    S.dma_split("pool", "slab%d" % i, slabs[i].ap[:, 8:16, :], src_b, [slabs[i].v], 8)
                    return slabs[i]

                wbr_v = w_br.rearrange("(kc p) n -> p kc n", p=128)
                wba_v = w_ba.rearrange("(kc p) n -> p kc n", p=128)
                wout_v = w_out.rearrange("(kc p) n -> p kc n", p=128)
                wup_v = w_up.rearrange("(kc p) n -> p kc n", p=128)
                wdn_v = w_down.rearrange("(fc p) n -> p fc n", p=128)

                with ExitStack() as d1:
                    hT = sb(d1, "hT", [128, 16, TO], BF16, slots=16)
                    xs2 = sb(d1, "xs2", [128, D])
                    xTt = sb(d1, "xTt", [128, 16, 128])
                    sq2 = [sb(d1, "sq2%d" % i, [128, 128]) for i in range(2)]
                    rs2 = sb(d1, "rs2", [128, 128])
                    tg = [sb(d1, "tg%d" % i, [128, GS]) for i in range(4)]
                    for n in range(2 * NT):
                        ts_ = slice(n * 128, (n + 1) * 128)
                        S.dma("sp", "xs2", xs2.ap, xb[6 + n // NT, (n % NT) * 128:(n % NT + 1) * 128, :], outs=[xs2.v])
                        for g4 in range(4):
                            ps = PF()
                            for j in range(4):
                                dc = 4 * g4 + j
                                S.tr(ps[:, j * 128:(j + 1) * 128], xs2[:, dc * 128:(dc + 1) * 128], identf.v, signal=(j == 3))
                            S.copy("act" if g4 % 2 else "dve", xTt[:, 4 * g4:4 * g4 + 4, :],
                                   ps[:, 0:512].re("p (j t) -> p j t", j=4))
                        msps = PF()
                        for dc in range(16):
                            sq = sq2[dc % 2]
                            S.act(sq.v, xTt[:, dc, :], AF.Square)
                            S.mm(msps[:, 0:128], C["onesdiv"].v, sq.v, start=(dc == 0), stop=(dc == 15), signal=(dc == 15))
                        S.act(rs2.v, msps[:, 0:128], AF.Sqrt, bias=eps_n.v)
                        S.op("dve", "reciprocal", [rs2.v], [rs2.v], (rs2.ap, rs2.ap))
                        for dc in range(16):
                            tmp = sq2[dc % 2]
                            S.tt("dve", tmp.v, xTt[:, dc, :], rs2.v, ALU.mult)
                            S.act(hT.s(dc)[:, ts_], tmp.v, AF.Identity, bias=sh1[:, dc:dc + 1], scale=g1[:, dc:dc + 1])
                    gt0 = CBI["gt0"]
                    for d2 in range(8):
                        c0 = d2 * 256
                        s_gr = slab_load(win_v[:, :, 4640 + c0:4640 + c0 + 256], 16)
                        s_ga = slab_load(win_v[:, :, 4640 + 2048 + c0:4640 + 2048 + c0 + 256], 16)
                        s_b = slab_load2(wbr_v[:, :, c0:c0 + 256], wba_v[:, :, c0:c0 + 256])
                        for jj in range(2):
                            dblk = 2 * d2 + jj
                            js = slice(jj * 128, (jj + 1) * 128)
                            for grp in range(NG):
                                gs = slice(grp * GS, (grp + 1) * GS)
                                ps = PF()
                                for kc in range(16):
                                    S.mm(ps[:, 0:GS], s_gr[:, kc, js], hT.s(kc)[:, gs], start=(kc == 0), stop=(kc == 15),
                                         signal=(kc == 15))
                                S.act(tg[0].v, ps[:, 0:GS], AF.Sigmoid, bias=C["b_in_l"][:, gt0 + dblk:gt0 + dblk + 1])
                                ps = PF()
                                for kc in range(16):
                                    S.mm(ps[:, 0:GS], s_ga[:, kc, js], hT.s(kc)[:, gs], start=(kc == 0), stop=(kc == 15),
                                         signal=(kc == 15))
                                S.act(tg[1].v, ps[:, 0:GS], AF.Sigmoid,
                                      bias=C["b_in_l"][:, gt0 + 16 + dblk:gt0 + 16 + dblk + 1])
                                ps = PF()
                                for kc in range(8):
                                    S.mm(ps[:, 0:GS], s_b[:, kc, js], yT.s(kc)[:, gs], start=(kc == 0), stop=(kc == 7),
                                         signal=(kc == 7))
                                S.tt("dve", tg[2].v, ps[:, 0:GS], tg[0].v, ALU.mult)
                                ps = PF()
                                for kc in range(8):
                                    S.mm(ps[:, 0:GS], s_b[:, 8 + kc, js], yT.s(8 + kc)[:, gs], start=(kc == 0), stop=(kc == 7),
                                         signal=(kc == 7))
                                S.tt("dve", tg[3].v, ps[:, 0:GS], tg[1].v, ALU.mult)
                                S.tt("pool", mg.s(dblk)[:, gs], tg[2].v, tg[3].v, ALU.add)
                    S.barrier()

                with ExitStack() as d2s:
                    x1T = sb(d2s, "x1T", [128, 16, TO], F32, slots=16)
                    xs3 = sb(d2s, "xs3", [128, D])
                    sq3 = [sb(d2s, "sq3%d" % i, [128, TO]) for i in range(2)]
                    rs3 = sb(d2s, "rs3", [128, TO])
                    fin_t = sb(d2s, "fin_t", [128, 16, 128])
                    ostage = xs3
                    for n in range(2 * NT):
                        S.dma("sp", "xs3", xs3.ap, xb[6 + n // NT, (n % NT) * 128:(n % NT + 1) * 128, :], outs=[xs3.v])
                        for g4 in range(4):
                            ps = PF()
                            for j in range(4):
                                dc = 4 * g4 + j
                                S.tr(ps[:, j * 128:(j + 1) * 128], xs3[:, dc * 128:(dc + 1) * 128], identf.v, signal=(j == 3))
                            S.copy("act" if g4 % 2 else "dve", x1T.s(4 * g4, 4)[:, :, n * 128:(n + 1) * 128],
                                   ps[:, 0:512].re("p (j t) -> p j t", j=4))
                    for d2 in range(8):
                        sl = slab_load(wout_v[:, :, d2 * 256:(d2 + 1) * 256], 16)
                        for jj in range(2):
                            dblk = 2 * d2 + jj
                            js = slice(jj * 128, (jj + 1) * 128)
                            for grp in range(NG):
                                gs = slice(grp * GS, (grp + 1) * GS)
                                ps = PF()
                                for kc in range(16):
                                    S.mm(ps[:, 0:GS], sl[:, kc, js], mg.s(kc)[:, gs], start=(kc == 0), stop=(kc == 15),
                                         signal=(kc == 15))
                                S.stt(x1T.s(dblk)[:, gs], ps[:, 0:GS], gt1[:, dblk:dblk + 1], x1T.s(dblk)[:, gs],
                                      ALU.mult, ALU.add)

                    def rms_stats():
                        for grp in range(NG):
                            gs = slice(grp * GS, (grp + 1) * GS)
                            msps = PF()
                            for dc in range(16):
                                sq = sq3[dc % 2]
                                S.act(sq[:, gs], x1T.s(dc)[:, gs], AF.Square)
                                S.mm(msps[:, 0:GS], C["onesdiv"].v, sq[:, gs], start=(dc == 0), stop=(dc == 15),
                                     signal=(dc == 15))
                            S.act(rs3[:, gs], msps[:, 0:GS], AF.Sqrt, bias=eps_n.v)
                        S.op("dve", "reciprocal", [rs3.v], [rs3.v], (rs3.ap, rs3.ap))

                    rms_stats()
                    h2T = yT
                    for dc in range(16):
                        tmp = sq3[dc % 2]
                        S.tt("dve", tmp.v, x1T.s(dc), rs3.v, ALU.mult)
                        S.act(h2T.s(dc), tmp.v, AF.Identity, bias=sh2[:, dc:dc + 1], scale=g2[:, dc:dc + 1])
                    for fg in range(4):
                        for f2 in range(8):
                            c0 = fg * 2048 + f2 * 256
                            sl = slab_load(wup_v[:, :, c0:c0 + 256], 16)
                            for jj in range(2):
                                fb = 2 * f2 + jj
                                js = slice(jj * 128, (jj + 1) * 128)
                                for grp in range(NG):
                                    gs = slice(grp * GS, (grp + 1) * GS)
                                    ps = PF()
                                    for kc in range(16):
                                        S.mm(ps[:, 0:GS], sl[:, kc, js], h2T.s(kc)[:, gs], start=(kc == 0), stop=(kc == 15),
                                             signal=(kc == 15))
                                    rl = sq3[(fb + grp) % 2]
                                    S.act(rl[:, 0:GS], ps[:, 0:GS], AF.Relu)
                                    S.tt("pool", mg.s(fb)[:, gs], rl[:, 0:GS], rl[:, 0:GS], ALU.mult)
                        for d2 in range(8):
                            sl = slab_load(wdn_v[:, fg * 16:(fg + 1) * 16, d2 * 256:(d2 + 1) * 256], 16)
                            for jj in range(2):
                                dblk = 2 * d2 + jj
                                js = slice(jj * 128, (jj + 1) * 128)
                                for grp in range(NG):
                                    gs = slice(grp * GS, (grp + 1) * GS)
                                    ps = PF()
                                    for fb in range(16):
                                        S.mm(ps[:, 0:GS], sl[:, fb, js], mg.s(fb)[:, gs], start=(fb == 0), stop=(fb == 15),
                                             signal=(fb == 15))
                                    S.stt(x1T.s(dblk)[:, gs], ps[:, 0:GS], gt2[:, dblk:dblk + 1], x1T.s(dblk)[:, gs],
                                          ALU.mult, ALU.add)
                    rms_stats()
                    outb = Buf()
                    for n in range(2 * NT):
                        ts_ = slice(n * 128, (n + 1) * 128)
                        for dc in range(16):
                            S.stt(fin_t[:, dc, :], x1T.s(dc)[:, ts_], C["gainf_l"][:, dc:dc + 1], rs3[:, ts_],
                                  ALU.mult, ALU.mult)
                        for g4 in range(4):
                            ps = PF()
                            for j in range(4):
                                dc = 4 * g4 + j
                                S.tr(ps[:, j * 128:(j + 1) * 128], fin_t[:, dc, :], identf.v, signal=(j == 3))
                            S.copy("act" if g4 % 2 else "dve", ostage[:, g4 * 512:(g4 + 1) * 512], ps[:, 0:512])
                        S.dma("sp", "ostore", out[n * 128:(n + 1) * 128, :], ostage.ap, outs=[V(None, [outb])], ins=[ostage.v])
                    S._wait(S.E["sp"], outb.w)
                    S.barrier()
        except StopBuild:
            S.barrier()
    return nc


_NC_CACHE = {}
_DBG = {"stop": None, "cores": None}


def kernel(**inputs):
    NT = inputs["x"].shape[1] // (8 * 128)
    if NT not in _NC_CACHE:
        _NC_CACHE[NT] = build(NT, _DBG["stop"])
    nc = _NC_CACHE[NT]
    maps = host_inputs(inputs, NT)
    if _DBG["cores"] is not None:
        sel = _DBG["cores"]
        res = run_bass_kernel_spmd(nc, [maps[c] for c in sel], core_ids=list(range(len(sel))))
        return {c: np.asarray(res.results[i]["out"], np.float32) for i, c in enumerate(sel)}
    res = run_bass_kernel_spmd(nc, maps, core_ids=list(range(8)))
    TB = 128 * NT
    B = inputs["x"].shape[0]
    outp = np.zeros((B, 8 * TB, D), np.float32)
    for cid in range(8):
        b, q = cid // 4, cid % 4
        outp[b, q * 2 * TB:(q + 1) * 2 * TB] = np.asarray(res.results[cid]["out"], np.float32)
    return outp
```

```python
import math
from contextlib import ExitStack

import numpy as np
import concourse.bass as bass
import concourse.mybir as mybir
from concourse.bass_utils import run_bass_kernel_spmd

F32 = mybir.dt.float32
BF16 = mybir.dt.bfloat16
AF = mybir.ActivationFunctionType
ALU = mybir.AluOpType
AX = mybir.AxisListType

D = 2048
HD = 64
NBLK_MIX = 8
CDEC = math.exp(-0.5)
NEG = -1.0e6


class Buf:
    __slots__ = ("w", "r")

    def __init__(self):
        self.w = None
        self.r = []


class V:
    __slots__ = ("ap", "bufs")

    def __init__(self, ap, bufs):
        self.ap, self.bufs = ap, bufs

    def __getitem__(self, idx):
        return V(self.ap[idx], self.bufs)

    def re(self, pat, **kw):
        return V(self.ap.rearrange(pat, **kw), self.bufs)


class TT:
    def __init__(self, t, slots=0):
        self.t = t
        self.ap = t[:] if not isinstance(t, bass.AP) else t
        self.slots = slots
        self.bufs = [Buf() for _ in range(max(1, slots))]

    @property
    def v(self):
        return V(self.ap, self.bufs)

    def __getitem__(self, idx):
        return V(self.ap[idx], self.bufs)

    def s(self, i, n=1):
        if n == 1:
            return V(self.ap[:, i], self.bufs[i:i + 1])
        return V(self.ap[:, i:i + n], self.bufs[i:i + n])


class Eng:
    def __init__(self, name, sem):
        self.name, self.sem = name, sem
        self.n = 0
        self.seen = {}
        self.prog = []


class Sync:
    ENG = ("pe", "act", "dve", "pool", "sp")

    def __init__(self, nc, sems):
        self.nc = nc
        self.E = {n: Eng(n, s) for n, s in zip(self.ENG, sems[:5])}
        self.free_sems = list(sems[5:])
        self.dmasem = {}
        self.block = None
        self.ninst = 0
        self.limit = None

    def _wait(self, e, ev):
        if ev is None:
            return
        key, cnt = ev
        if key == e.name and (key == "pe" or cnt > e.n):
            return
        if e.seen.get(key, 0) >= cnt:
            return
        e.seen[key] = cnt
        sem = self.E[key].sem if key in self.E else self.dmasem[key][0]
        e.prog.append(("wait_ge", (sem, cnt), {}, None))

    def _deps(self, e, reads, writes):
        for b in reads:
            self._wait(e, b.w)
        for b in writes:
            self._wait(e, b.w)
            for ev in b.r:
                self._wait(e, ev)

    def _record(self, ev, reads, writes):
        for b in reads:
            b.r.append(ev)
            if len(b.r) > 16:
                m = {}
                for k, c in b.r:
                    m[k] = max(m.get(k, 0), c)
                b.r = list(m.items())
        for b in writes:
            b.w = ev
            b.r = []

    def op(self, ename, method, outs, ins, args=(), kwargs=None, signal=True):
        if self.limit is not None and self.ninst >= self.limit:
            return
        e = self.E[ename]
        reads = [b for v in ins if isinstance(v, V) for b in v.bufs]
        writes = [b for v in outs for b in v.bufs]
        self._deps(e, reads, writes)
        if signal or True:
            e.n += 1
            ev = (ename, e.n)
            e.prog.append((method, args, kwargs or {}, (e.sem, 1)))
        else:
            ev = (ename, e.n + 1)
            e.prog.append((method, args, kwargs or {}, None))
        self._record(ev, reads, writes)
        self.ninst += 1

    def dma(self, qname, key, out, in_, outs=(), ins=()):
        e = self.E[qname]
        reads = [b for v in ins for b in v.bufs]
        writes = [b for v in outs for b in v.bufs]
        self._deps(e, reads, writes)
        if key not in self.dmasem:
            self.dmasem[key] = [self.free_sems.pop(), 0]
        ds = self.dmasem[key]
        ds[1] += 16
        e.prog.append(("dma_start", (), dict(out=out, in_=in_), (ds[0], 16)))
        self._record((key, ds[1]), reads, writes)

    def dma_split(self, qname, key, out, in_, outs, nk, step=4):
        for k0 in range(0, nk, step):
            k1 = min(nk, k0 + step)
            self.dma(qname, key, out[:, k0:k1, :], in_[:, k0:k1, :], outs=outs)

    def flush(self):
        decs = dict(pe=self.block.tensor, act=self.block.scalar, dve=self.block.vector,
                    pool=self.block.gpsimd, sp=self.block.sync)
        for name in self.ENG:
            e = self.E[name]
            if not e.prog:
                continue
            prog = e.prog
            e.prog = []

            def body(eng, prog=prog):
                for method, args, kwargs, inc in prog:
                    ins = getattr(eng, method)(*args, **kwargs)
                    if inc is not None:
                        ins.then_inc(inc[0], inc[1])
            decs[name](body)

    def barrier(self):
        evs = [(n, self.E[n].n) for n in self.ENG if self.E[n].n > 0]
        evs += [(k, v[1]) for k, v in self.dmasem.items()]
        for n in self.ENG:
            for ev in evs:
                if ev[0] != n:
                    self._wait(self.E[n], ev)
        self.flush()

    def mm(self, out, lhsT, rhs, start=True, stop=True, signal=True):
        self.op("pe", "matmul", [out], [lhsT, rhs], (out.ap, lhsT.ap, rhs.ap),
                dict(start=start, stop=stop), signal=signal)

    def tr(self, out, in_, ident, signal=True):
        self.op("pe", "transpose", [out], [in_, ident], (out.ap, in_.ap, ident.ap), signal=signal)

    def act(self, out, in_, func, bias=None, scale=None):
        kw = {}
        ins = [in_]
        if bias is not None:
            kw["bias"] = bias.ap if isinstance(bias, V) else bias
            ins.append(bias)
        if scale is not None:
            kw["scale"] = scale.ap if isinstance(scale, V) else scale
            ins.append(scale)
        self.op("act", "activation", [out], ins, (out.ap, in_.ap, func), kw)

    def tt(self, eng, out, in0, in1, op):
        self.op(eng, "tensor_tensor", [out], [in0, in1], (out.ap, in0.ap, in1.ap, op))

    def ts(self, eng, out, in0, s1, s2=None, op0=ALU.mult, op1=None):
        a1 = s1.ap if isinstance(s1, V) else s1
        a2 = s2.ap if isinstance(s2, V) else s2
        kw = {}
        if op1 is not None:
            kw["op1"] = op1
        self.op(eng, "tensor_scalar", [out], [in0, s1, s2], (out.ap, in0.ap, a1, a2, op0), kw)

    def stt(self, out, in0, scalar, in1, op0, op1):
        sc = scalar.ap if isinstance(scalar, V) else scalar
        self.op("dve", "scalar_tensor_tensor", [out], [in0, scalar, in1],
                (out.ap, in0.ap, sc, in1.ap, op0, op1))

    def copy(self, eng, out, in_):
        if eng == "act":
            self.op("act", "copy", [out], [in_], (out.ap, in_.ap))
        else:
            self.op(eng, "tensor_copy", [out], [in_], (out.ap, in_.ap))

    def memset(self, eng, out, val):
        self.op(eng, "memset", [out], [], (out.ap, val))


def col_blocks():
    blocks = []
    for i in range(8):
        blocks.append(("r%d" % i, [(128 * i, 128)]))
    for i in range(8):
        blocks.append(("k%d" % i, [(1024 + 128 * i, 128)]))
    for i in range(8):
        blocks.append(("v%d" % i, [(2048 + 128 * i, 128)]))
    blocks.append(("wa", [(3072, 128)]))
    blocks.append(("g0", [(3200, 128)]))
    blocks.append(("g1", [(3328, 32)]))
    for i in range(8):
        blocks.append(("q%d" % i, [(3360 + 128 * i, 128)]))
    blocks.append(("ka0", [(4384, 64), (4384, 64)]))
    blocks.append(("ka1", [(4448, 64), (4448, 64)]))
    blocks.append(("va", [(4512, 128)]))
    for i in range(32):
        blocks.append(("gt%d" % i, [(4640 + 128 * i, 128)]))
    return blocks


CB = col_blocks()
CBI = {name: i for i, (name, _) in enumerate(CB)}
NCB = len(CB)

HEAD_ORDER = [8 * g + 2 * j + par for g in range(2) for par in range(2) for j in range(4)]
SLOPES = [2.0 ** (-8.0 * (h + 1) / 16.0) for h in range(16)]


def pmaj(vec, n):
    return np.ascontiguousarray(np.asarray(vec, np.float32).reshape(n, 128).T)


def make_consts(TB):
    p = np.arange(128)[:, None]
    f = np.arange(128)[None, :]
    c = {}
    c["ident"] = (p == f).astype(np.float32)
    msu = (f > p).astype(np.float32)
    mu = (f >= p).astype(np.float32)
    msl = (p > f).astype(np.float32)
    bd = ((p // 32) == (f // 32)).astype(np.float32)
    c["gm"] = np.ascontiguousarray(np.concatenate([msu, mu, msu, mu], 1))
    c["gmn"] = np.ascontiguousarray(np.concatenate([msu * bd, mu, msu * bd, mu], 1))
    c["msl2d"] = np.ascontiguousarray(np.concatenate([msl * bd, msl * bd], 1))
    c["msl2o"] = np.ascontiguousarray(np.concatenate([msl * (1 - bd), msl * (1 - bd)], 1))
    cm = np.ones((128, TB), np.float32)
    cm[:, ::128] = 0.0
    c["chunkmask"] = cm
    c["identfold"] = ((p % 64) == np.arange(64)[None, :]).astype(np.float32)
    c["headsel"] = ((p // 64) == np.arange(2)[None, :]).astype(np.float32)
    c["blkones"] = ((p // 64) == (f // 64)).astype(np.float32)
    dist_prev = f + 128 - p
    c["negdp"] = np.where(p > f, -dist_prev, NEG).astype(np.float32)
    dist_cur = f - p
    c["negdc"] = np.where(p <= f, -dist_cur, NEG).astype(np.float32)
    c["onesdiv"] = np.full((128, 128), 1.0 / D, np.float32)
    return c


def host_inputs(inp, NT):
    TB = 128 * NT
    x = np.asarray(inp["x"], np.float32)
    B, T, _ = x.shape
    assert T == 8 * TB and B == 2
    shared = {}
    shared.update(make_consts(TB))
    shared["w_ada"] = np.ascontiguousarray(inp["w_ada"][0])
    shared["w_in"] = np.ascontiguousarray(inp["w_in"][0])
    shared["w_br"] = np.ascontiguousarray(inp["w_branch_rwkv"][0])
    shared["w_ba"] = np.ascontiguousarray(inp["w_branch_attn"][0])
    shared["w_out"] = np.ascontiguousarray(inp["w_out"][0])
    shared["w_up"] = np.ascontiguousarray(inp["w_up"][0])
    shared["w_down"] = np.ascontiguousarray(inp["w_down"][0])
    shared["b_ada_l"] = pmaj(inp["b_ada"][0], 96)
    shared["gain1_l"] = pmaj(inp["norm1_gain"][0], 16)
    shared["gain2_l"] = pmaj(inp["norm2_gain"][0], 16)
    shared["gainf_l"] = pmaj(inp["final_gain"], 16)
    b_in = np.asarray(inp["b_in"][0], np.float32)
    bl = np.zeros((128, NCB), np.float32)
    for i, (_, pieces) in enumerate(CB):
        off = 0
        for st, wd in pieces:
            bl[off:off + wd, i] = b_in[st:st + wd]
            off += wd
    shared["b_in_l"] = bl
    mix = np.asarray(inp["rwkv_mix"][0], np.float32)
    ml = np.zeros((128, 27), np.float32)
    for i in range(27):
        st, wd = CB[i][1][0]
        ml[:wd, i] = mix[st:st + wd]
    shared["mix_l"] = ml
    shared["w0_l"] = pmaj(inp["rwkv_w0"][0], 8)
    shared["a0_l"] = pmaj(inp["rwkv_a0"][0], 8)
    shared["kk_l"] = pmaj(inp["rwkv_k_k"][0], 8)
    shared["ka_l"] = pmaj(inp["rwkv_k_a"][0], 8)
    shared["rk_l"] = pmaj(np.asarray(inp["rwkv_r_k"][0]).reshape(-1), 8)
    z64 = np.zeros((64, 1024), np.float32)
    shared["w2z"] = np.ascontiguousarray(np.concatenate([np.asarray(inp["rwkv_w2"][0], np.float32), z64], 0))
    shared["a2z"] = np.ascontiguousarray(np.concatenate([z64, np.asarray(inp["rwkv_a2"][0], np.float32)], 0))
    g2 = np.asarray(inp["rwkv_g2"][0], np.float32)
    shared["g2a"] = np.ascontiguousarray(g2[:128])
    shared["g2b"] = np.ascontiguousarray(g2[128:160])
    shared["lnw_b"] = np.ascontiguousarray(np.broadcast_to(np.asarray(inp["rwkv_ln_w"][0], np.float32)[None], (128, 1024)))
    shared["lnb_b"] = np.ascontiguousarray(np.broadcast_to(np.asarray(inp["rwkv_ln_b"][0], np.float32)[None], (128, 1024)))
    sinks = np.asarray(inp["attn_sinks"][0], np.float32)[HEAD_ORDER]
    shared["sinks_b"] = np.ascontiguousarray(np.broadcast_to(sinks[None], (128, 16)))
    maps = []
    for cid in range(8):
        b, q = cid // 4, cid % 4
        m = dict(shared)
        xb = np.zeros((NBLK_MIX, TB, D), np.float32)
        fl = np.zeros((128, 8), np.float32)
        for j in range(NBLK_MIX):
            st = (2 * q - 6 + j) * TB
            if st >= 0:
                xb[j] = x[b, st:st + TB]
                fl[:, j] = 1.0
        m["xb"] = xb
        m["flags"] = fl
        m["cvec"] = pmaj(inp["c"][b], 16)
        maps.append(m)
    return maps


class StopBuild(Exception):
    pass


class StopInner(Exception):
    pass


def build(NT, stop=None):
    TB = 128 * NT

    stopflag = [False]

    def chki(tag):
        print("chk", tag, S.ninst)
        if stop is not None and stop == tag:
            raise StopInner()

    def chk(tag, soft=False):
        if stop is not None and stop == tag:
            if soft:
                stopflag[0] = True
                return True
            raise StopBuild()
        return False

    TO = 2 * TB
    GS = min(512, TO)
    NG = TO // GS
    nc = bass.Bass("TRN2", target_bir_lowering=False)

    def din(name, shape):
        return nc.dram_tensor(name, list(shape), F32, kind="ExternalInput").ap()

    xb = din("xb", [NBLK_MIX, TB, D])
    d_flags = din("flags", [128, 8])
    d_cvec = din("cvec", [128, 16])
    w_ada = din("w_ada", [D, 6 * D])
    w_in = din("w_in", [D, 8736])
    w_br = din("w_br", [1024, D])
    w_ba = din("w_ba", [1024, D])
    w_out = din("w_out", [D, D])
    w_up = din("w_up", [D, 4 * D])
    w_down = din("w_down", [4 * D, D])
    small = {}
    for name, shape in [("b_ada_l", [128, 96]), ("gain1_l", [128, 16]), ("gain2_l", [128, 16]),
                        ("gainf_l", [128, 16]), ("b_in_l", [128, NCB]), ("mix_l", [128, 27]),
                        ("w0_l", [128, 8]), ("a0_l", [128, 8]), ("kk_l", [128, 8]), ("ka_l", [128, 8]),
                        ("rk_l", [128, 8]), ("sinks_b", [128, 16]),
                        ("ident", [128, 128]), ("gm", [128, 512]), ("gmn", [128, 512]), ("msl2d", [128, 256]),
                        ("msl2o", [128, 256]),
                        ("chunkmask", [128, TB]), ("identfold", [128, 64]), ("headsel", [128, 2]),
                        ("blkones", [128, 128]), ("negdp", [128, 128]), ("negdc", [128, 128]),
                        ("onesdiv", [128, 128])]:
        small[name] = (din(name, shape), shape)
    d_w2z = din("w2z", [128, 1024])
    d_a2z = din("a2z", [128, 1024])
    d_g2a = din("g2a", [128, 1024])
    d_g2b = din("g2b", [32, 1024])
    d_lnw = din("lnw_b", [128, 1024])
    d_lnb = din("lnb_b", [128, 1024])
    out = nc.dram_tensor("out", [TO, D], F32, kind="ExternalOutput").ap()

    win_v = w_in.rearrange("(kc p) n -> p kc n", p=128)

    with ExitStack() as es:
        sems = [es.enter_context(nc.semaphore("s%d" % i)) for i in range(40)]
        S = Sync(nc, sems)
        if isinstance(stop, str) and stop.startswith("n"):
            S.limit = int(stop[1:])
        S.block = es.enter_context(nc.Block())

        def sb(st, name, shape, dt=F32, slots=0):
            return TT(st.enter_context(nc.sbuf_tensor("sb_" + name, list(shape), dt)), slots)

        psf = [TT(es.enter_context(nc.psum_tensor("psf%d" % i, [128, 512], F32))) for i in range(6)]
        psb = [TT(es.enter_context(nc.psum_tensor("psb%d" % i, [128, 1024], BF16))) for i in range(2)]
        pcnt = [0, 0]

        def PF():
            pcnt[0] += 1
            return psf[pcnt[0] % 6]

        def PB():
            pcnt[1] += 1
            return psb[pcnt[1] % 2]

        C = {}
        for name, (ap, shape) in small.items():
            C[name] = sb(es, "c_" + name, shape)
        for name, (ap, shape) in small.items():
            S.dma("sp", "const", C[name].ap, ap, outs=[C[name].v])
        identb = sb(es, "identb", [128, 128], BF16)
        blkones_b = sb(es, "blkonesb", [128, 128], BF16)
        headsel_b = sb(es, "headselb", [128, 2], BF16)
        flags = sb(es, "flags", [128, 8])
        cvec = sb(es, "cvec", [128, 16])
        S.dma("sp", "const", flags.ap, d_flags, outs=[flags.v])
        S.dma("sp", "const", cvec.ap, d_cvec, outs=[cvec.v])
        fin = ("const", S.dmasem["const"][1])
        for t in list(C.values()) + [flags, cvec]:
            t.bufs[0].w = fin
        S.copy("dve", identb.v, C["ident"].v)
        S.copy("dve", blkones_b.v, C["blkones"].v)
        S.copy("dve", headsel_b.v, C["headsel"].v)
        identf = C["ident"]
        yT = sb(es, "yT", [128, 16, TO], BF16, slots=16)
        ST = sb(es, "ST", [64, 16, 64])
        S.memset("pool", ST.v, 0.0)
        mod = sb(es, "mod", [128, 96])
        g1 = sb(es, "g1", [128, 16])
        g2 = sb(es, "g2", [128, 16])
        eps_n = sb(es, "eps_n", [128, 1])
        eps_g = sb(es, "eps_g", [128, 1])
        S.memset("pool", eps_n.v, 1e-6)
        S.memset("pool", eps_g.v, 64e-5)
        omka = sb(es, "omka", [128, 8])
        S.ts("dve", omka.v, C["ka_l"].v, -1.0, 1.0, op0=ALU.mult, op1=ALU.add)
        bq8 = sb(es, "bq8", [128, 8])
        S.ts("dve", bq8.v, C["b_in_l"][:, CBI["q0"]:CBI["q0"] + 8], 0.125, None, op0=ALU.mult)
        esink = sb(es, "esink", [128, 16])
        S.act(esink.v, C["sinks_b"].v, AF.Exp)
        hb = sb(es, "hb", [128, 1])
        S.ts("dve", hb.v, flags[:, 5:6], -1.0, 30000.0, op0=ALU.add, op1=ALU.mult)

        try:
            chk("c0")
            with ExitStack() as p0:
                cact = sb(p0, "cact", [128, 16])
                S.act(cact.v, cvec.v, AF.Silu)
                wsl = [sb(p0, "wada%d" % i, [128, 16, 512]) for i in range(2)]
                wada_v = w_ada.rearrange("(kc p) n -> p kc n", p=128)
                modps = PF()
                for s in range(24):
                    sl = wsl[s % 2]
                    for k4 in range(4):
                        S.dma("sp", "wada%d" % (s % 2), sl.ap[:, 4 * k4:4 * k4 + 4, :],
                              wada_v[:, 4 * k4:4 * k4 + 4, s * 512:(s + 1) * 512], outs=[sl.v])
                    for j in range(4):
                        col = 4 * s + j
                        for kc in range(16):
                            S.mm(modps[:, col:col + 1], sl[:, kc, j * 128:(j + 1) * 128], cact[:, kc:kc + 1],
                                 start=(kc == 0), stop=(kc == 15), signal=(kc == 15 and j == 3))
                S.tt("dve", mod.v, modps[:, 0:96], C["b_ada_l"].v, ALU.add)
                S.stt(g1.v, mod[:, 16:32], 1.0, C["gain1_l"].v, ALU.add, ALU.mult)
                S.stt(g2.v, mod[:, 64:80], 1.0, C["gain2_l"].v, ALU.add, ALU.mult)
                S.barrier()
            sh1, gt1, sh2, gt2 = mod[:, 0:16], mod[:, 32:48], mod[:, 48:64], mod[:, 80:96]
            chk("p0")

            with ExitStack() as p1:
                arena = sb(p1, "arena", [128, 16, TB], F32, slots=16)
                hTb = sb(p1, "hTb", [128, 16, TB], BF16, slots=16)
                pslab = [sb(p1, "pslab%d" % i, [128, 16, 384], BF16) for i in range(1)]
                mslab = [sb(p1, "mslab%d" % i, [128, 16, 128], BF16) for i in range(2)]
                xs = sb(p1, "xs", [128, D // 2])
                sqt = [sb(p1, "sqt%d" % i, [128, TB]) for i in range(2)]
                rstd = sb(p1, "rstd", [128, TB])
                ptmp = sb(p1, "ptmp", [128, TB + 1])
                carry = sb(p1, "carry", [128, 27])
                S.memset("pool", carry.v, 0.0)
                w2z = sb(p1, "w2z", [128, 128], BF16)
                a2z = sb(p1, "a2z", [128, 128], BF16)
                g2a = sb(p1, "g2a", [128, 128], BF16)
                g2b = sb(p1, "g2b", [32, 128], BF16)
                lnw = sb(p1, "lnw", [128, 128])
                lnb = sb(p1, "lnb", [128, 128])


                pass
                wa_bf = sb(p1, "wa_bf", [128, TB], BF16)
                sgd0 = sb(p1, "sgd0", [128, TB], BF16)
                sgd1 = sb(p1, "sgd1", [32, TB], BF16)
                kk2 = sb(p1, "kk2", [128, TB], BF16)
                btm = sb(p1, "btm", [128, 2, TB], BF16)
                ktm = sb(p1, "ktm", [128, 2, TB], BF16)
                S.memset("pool", btm.v, 0.0)
                S.memset("pool", ktm.v, 0.0)
                dgh = sb(p1, "dgh", [128, 2, NT, 64], BF16)
                dgl = sb(p1, "dgl", [128, 2, NT, 64], BF16)
                dgt = sb(p1, "dgt", [128, NT, 64])
                S.memset("pool", dgh.v, 0.0)
                S.memset("pool", dgl.v, 0.0)
                identfold_b = sb(p1, "identfoldb", [128, 64], BF16)
                S.copy("dve", identfold_b.v, C["identfold"].v)
                Bh = sb(p1, "Bh", [128, TB], BF16)
                Kh = sb(p1, "Kh", [128, TB], BF16)
                vbf = sb(p1, "vbf", [128, TB], BF16)
                rkb = sb(p1, "rkb", [128, TB], BF16)
                AR = sb(p1, "AR", [128, NT, 256], BF16)
                gL = sb(p1, "gL", [128, NT])
                dg = sb(p1, "dg", [128, NT, 64])
                TOK = sb(p1, "TOK", [128, NT, 512], BF16, slots=NT)
                nbsb = [sb(p1, "nbsb%d" % i, [128, 512], BF16) for i in range(NT)]
                kbsb = [sb(p1, "kbsb%d" % i, [128, 512], BF16) for i in range(NT)]
                gsb = [sb(p1, "gsb%d" % i, [128, 256], BF16) for i in range(NT)]
                ngb = [[sb(p1, "ngb%d_%d" % (i, j), [128, 512], BF16) for j in range(2)] for i in range(NT)]
                rxb = [[sb(p1, "rxb%d_%d" % (i, j), [128, 512], BF16) for j in range(2)] for i in range(NT)]
                PQ = sb(p1, "PQ", [64, NT, 256], F32, slots=NT)
                RP = sb(p1, "RP", [64, NT, 256], BF16, slots=NT)
                YP = sb(p1, "YP", [128, NT, 128], F32, slots=NT)
                STc = sb(p1, "STc", [64, NT, 128], BF16, slots=NT)
                yt = sb(p1, "yt", [128, 128])
                ysq = sb(p1, "ysq", [128, 128])
                ytb = sb(p1, "ytb", [128, 128], BF16)
                st1 = sb(p1, "st1", [128, 2])
                st2 = sb(p1, "st2", [128, 2])
                mean = sb(p1, "mean", [128, 2])
                msq = sb(p1, "msq", [128, 2])
                var = sb(p1, "var", [128, 2])
                bon = sb(p1, "bon", [128, 2])
                qT = sb(p1, "qT", [128, 8, TB], BF16, slots=8)
                kdup = [[sb(p1, "kdup%d_%d" % (g, par), [128, 128 + TB], BF16) for par in range(2)] for g in range(2)]
                vaT = kk2
                vaug = sb(p1, "vaug", [128, NT + 1, 2, 65], BF16)
                S.memset("pool", vaug.v, 1.0)
                for g in range(2):
                    for par in range(2):
                        S.memset("pool", kdup[g][par].v, 0.0)
                et = sqt if TB == 512 else [sb(p1, "et%d" % i, [128, 512]) for i in range(2)]
                PT = [Bh, Kh] if TB == 512 else [sb(p1, "PT%d" % i, [128, 512], BF16) for i in range(2)]
                den = sb(p1, "den", [128, 4])
                otok = sb(p1, "otok", [128, 1024], BF16)

                def A(i):
                    return arena.s(i)

                def load_cb(slab_v, key, name):
                    off = 0
                    for (st_, wd) in CB[CBI[name]][1]:
                        S.dma_split("pool", key, slab_v.ap[:, :, off:off + wd], win_v[:, :, st_:st_ + wd], [slab_v], 16)
                        off += wd
                    return off

                def proj_mix(slab_v, width):
                    ps = PF()
                    for kc in range(16):
                        S.mm(ps[0:width, 0:TB], slab_v[:, kc, 0:width], hTb.s(kc), start=(kc == 0), stop=(kc == 15),
                             signal=(kc == 15))
                    return ps[0:width, 0:TB]

                mcnt = [0]

                def misc_proj(name):
                    mcnt[0] += 1
                    sl = mslab[mcnt[0] % 2]
                    wdt = load_cb(sl.v, "mslab%d" % (mcnt[0] % 2), name)
                    return proj_mix(sl.v, wdt), wdt

                tmp1 = sb(p1, "tmp1", [128, 1])

                def carry_last(slab_v, width, cbi, blk_):
                    ps = PF()
                    for kc in range(16):
                        S.mm(ps[0:width, 0:1], slab_v[:, kc, 0:width], hTb.s(kc)[:, TB - 1:TB], start=(kc == 0),
                             stop=(kc == 15))
                    S.act(tmp1[0:width, :], ps[0:width, 0:1], AF.Identity, bias=C["b_in_l"][0:width, cbi:cbi + 1])
                    S.ts("dve", carry[0:width, cbi:cbi + 1], tmp1[0:width, :], flags[0:width, blk_:blk_ + 1], None,
                         op0=ALU.mult)

                try:
                    for blk in range(NBLK_MIX):
                        own = blk >= 6
                        halo = blk == 5
                        ob = blk - 6
                        vflag = flags[:, blk:blk + 1]

                        def lerp(psv, cbi, width, out_v):
                            S.act(ptmp[0:width, 1:TB + 1], psv, AF.Identity, bias=C["b_in_l"][0:width, cbi:cbi + 1])
                            S.copy("pool", ptmp[0:width, 0:1], carry[0:width, cbi:cbi + 1])
                            dt_ = A(13)[0:width, :]
                            S.tt("dve", dt_, ptmp[0:width, 0:TB], ptmp[0:width, 1:TB + 1], ALU.subtract)
                            S.stt(out_v, dt_, C["mix_l"][0:width, cbi:cbi + 1], ptmp[0:width, 1:TB + 1], ALU.mult, ALU.add)
                            S.ts("dve", carry[0:width, cbi:cbi + 1], ptmp[0:width, TB:TB + 1], flags[0:width, blk:blk + 1],
                                 None, op0=ALU.mult)

                        for n in range(NT):
                            for g4 in range(4):
                                if g4 % 2 == 0:
                                    S.dma("sp", "xs", xs.ap, xb[blk, n * 128:(n + 1) * 128, g4 * 512:g4 * 512 + 1024], outs=[xs.v])
                                ps = PF()
                                for j in range(4):
                                    dc = 4 * g4 + j
                                    dl = dc % 8
                                    S.tr(ps[:, j * 128:(j + 1) * 128], xs[:, dl * 128:(dl + 1) * 128], identf.v,
                                         signal=(j == 3))
                                dst = arena.s(4 * g4, 4)[:, :, n * 128:(n + 1) * 128]
                                S.copy("act" if g4 % 2 else "dve", dst, ps[:, 0:512].re("p (j t) -> p j t", j=4))
                        if blk == 0:
                            chki("b0a")
                        msps = PF()
                        for dc in range(16):
                            sq = sqt[dc % 2]
                            S.act(sq.v, A(dc), AF.Square)
                            S.mm(msps[:, 0:TB], C["onesdiv"].v, sq.v, start=(dc == 0), stop=(dc == 15), signal=(dc == 15))
                        S.act(rstd.v, msps[:, 0:TB], AF.Sqrt, bias=eps_n.v)
                        S.op("dve", "reciprocal", [rstd.v], [rstd.v], (rstd.ap, rstd.ap))
                        for dc in range(16):
                            tmp = sqt[dc % 2]
                            S.tt("dve", tmp.v, A(dc), rstd.v, ALU.mult)
                            S.act(hTb.s(dc), tmp.v, AF.Identity, bias=sh1[:, dc:dc + 1], scale=g1[:, dc:dc + 1])
                        if blk == 0:
                            chki("b0b")
                        psv, _ = misc_proj("wa")
                        lerp(psv, CBI["wa"], 128, A(15))
                        S.act(wa_bf[0:64, :], A(15)[0:64, :], AF.Tanh)
                        S.copy("act", wa_bf[64:128, :], A(15)[64:128, :])
                        if halo:
                            for nm_ in ("g0", "g1"):
                                mcnt[0] += 1
                                sl_ = mslab[mcnt[0] % 2]
                                wdt_ = load_cb(sl_.v, "mslab%d" % (mcnt[0] % 2), nm_)
                                carry_last(sl_.v, wdt_, CBI[nm_], blk)
                        if own:
                            psv, _ = misc_proj("g0")
                            lerp(psv, CBI["g0"], 128, A(15))
                            S.act(sgd0.v, A(15), AF.Sigmoid)
                            psv, _ = misc_proj("g1")
                            lerp(psv, CBI["g1"], 32, A(15)[0:32, :])
                            S.act(sgd1.v, A(15)[0:32, :], AF.Sigmoid)
                        if blk == 0:
                            chki("b0c")
                        def pair_weights(hp_, blk_):
                            sl_ = pslab[0]
                            if blk_ >= 5:
                                S.dma_split("pool", "pslab0", sl_.ap[:, :, 0:128], win_v[:, :, hp_ * 128:(hp_ + 1) * 128], [sl_.v], 16)
                            S.dma_split("pool", "pslab0", sl_.ap[:, :, 128:256], win_v[:, :, 1024 + hp_ * 128:1024 + (hp_ + 1) * 128], [sl_.v], 16)
                            S.dma_split("pool", "pslab0", sl_.ap[:, :, 256:384], win_v[:, :, 2048 + hp_ * 128:2048 + (hp_ + 1) * 128], [sl_.v], 16)

                        for hp in range(8):
                            sl = pslab[0]
                            key = "pslab0"
                            hc = slice(hp * 128, (hp + 1) * 128)
                            if hp == 0 and blk == 0:
                                pair_weights(0, blk)
                            if halo:
                                carry_last(sl[:, :, 0:128], 128, hp, blk)
                            if own:
                                lerp(proj_mix(sl[:, :, 0:128], 128), hp, 128, A(0))
                            lerp(proj_mix(sl[:, :, 128:256], 128), 8 + hp, 128, A(1))
                            lerp(proj_mix(sl[:, :, 256:384], 128), 16 + hp, 128, A(2))
                            S.dma("pool", "w2z", w2z.ap, d_w2z[:, hc], outs=[w2z.v])
                            S.dma("pool", "a2z", a2z.ap, d_a2z[:, hc], outs=[a2z.v])
                            if own:
                                S.dma("pool", "g2a", g2a.ap, d_g2a[:, hc], outs=[g2a.v])
                                S.dma("pool", "g2b", g2b.ap, d_g2b[:, hc], outs=[g2b.v])
                            ps = PF()
                            S.mm(ps[:, 0:TB], w2z.v, wa_bf.v)
                            S.act(A(3), ps[:, 0:TB], AF.Sigmoid, bias=C["w0_l"][:, hp:hp + 1])
                            ps = PF()
                            S.mm(ps[:, 0:TB], a2z.v, wa_bf.v)
                            S.act(A(5), ps[:, 0:TB], AF.Sigmoid, bias=C["a0_l"][:, hp:hp + 1])
                            S.op("dve", "tensor_tensor_scan", [A(4)], [C["chunkmask"].v, A(3)],
                                 (A(4).ap, C["chunkmask"].ap, A(3).ap, 0.0, ALU.mult, ALU.add))
                            c3 = "p (c t) -> p c t"
                            S.act(A(9), A(4), AF.Exp, scale=-CDEC)
                            if own:
                                S.tt("pool", AR[:, :, 128:256], A(0).re(c3, c=NT), A(9).re(c3, c=NT), ALU.mult)
                            S.tt("pool", A(14), A(4), A(3), ALU.subtract)
                            S.act(A(10), A(14), AF.Exp, scale=-CDEC)
                            S.act(gL.v, A(4).re(c3, c=NT)[:, :, 127], AF.Exp, scale=-CDEC)
                            S.act(A(9), A(4), AF.Exp, scale=CDEC)
                            S.ts("pool", A(6), A(1), C["kk_l"][:, hp:hp + 1], None, op0=ALU.mult)
                            S.act(kk2.v, A(6), AF.Square)
                            ps = PF()
                            S.mm(ps[:, 0:TB], blkones_b.v, kk2.v)
                            S.act(A(14), ps[:, 0:TB], AF.Sqrt)
                            S.ts("dve", A(14), A(14), 1e-12, None, op0=ALU.max)
                            S.op("dve", "reciprocal", [A(14)], [A(14)], (A(14).ap, A(14).ap))
                            S.tt("dve", A(6), A(6), A(14), ALU.mult)
                            S.ts("dve", A(7), A(5), C["ka_l"][:, hp:hp + 1], omka[:, hp:hp + 1], op0=ALU.mult, op1=ALU.add)
                            S.tt("dve", A(7), A(1), A(7), ALU.mult)
                            S.tt("pool", A(8), A(6), A(5), ALU.mult)
                            S.stt(AR[:, :, 0:128], A(6).re(c3, c=NT), -1.0, A(10).re(c3, c=NT), ALU.mult, ALU.mult)
                            S.tt("dve", A(11), A(8), A(9), ALU.mult)
                            S.tt("pool", A(12), A(7), A(9), ALU.mult)
                            for h in range(2):
                                hs = slice(64 * h, 64 * h + 64)
                                S.copy("act", btm[hs, h, :], A(11)[hs, :])
                                S.copy("act", ktm[hs, h, :], A(12)[hs, :])
                            for c in range(NT):
                                cs_ = slice(c * 128, (c + 1) * 128)
                                S.ts("pool", Bh[:, cs_], A(11)[:, cs_], gL[:, c:c + 1], None, op0=ALU.mult)
                                S.ts("pool", Kh[:, cs_], A(12)[:, cs_], gL[:, c:c + 1], None, op0=ALU.mult)
                                S.ts("pool", dg[:, c, :], C["identfold"].v, gL[:, c:c + 1], None, op0=ALU.mult)
                            for h in range(2):
                                hs = slice(64 * h, 64 * h + 64)
                                S.copy("act", dgh[hs, h], dg[hs])
                                S.tt("pool", dgt[hs], dg[hs], dgh[hs, h], ALU.subtract)
                                S.copy("act", dgl[hs, h], dgt[hs])
                            S.ts("dve", vbf.v, A(2), vflag, None, op0=ALU.mult)
                            if own:
                                S.stt(rkb.v, A(0), C["rk_l"][:, hp:hp + 1], A(7), ALU.mult, ALU.mult)
                            if blk == 0 and hp == 0:
                                chki("b0d")
                            if hp + 1 < 8:
                                pair_weights(hp + 1, blk)
                            elif blk + 1 < NBLK_MIX:
                                pair_weights(0, blk + 1)
                            W = 256 if own else 128
                            h3 = "p (h f) -> p h f"
                            h4 = "p (h f) -> p h f"
                            CR = range(NT)
                            csl = [slice(c * 128, (c + 1) * 128) for c in CR]
                            for c in CR:
                                pb = PB()
                                S.tr(pb[:, 0:128], Bh[:, csl[c]], identb.v)
                                S.tr(pb[:, 128:256], Kh[:, csl[c]], identb.v)
                                S.tr(pb[:, 256:384], AR[:, c, 0:128], identb.v)
                                S.tr(pb[:, 384:512], vbf[:, csl[c]], identb.v)
                                S.copy("act", TOK.s(c), pb[:, 0:512])
                            for c in CR:
                                psg, psk, psn = PF(), PF(), PF()
                                for h in range(2):
                                    S.mm(psg[:, h * 256:h * 256 + W], btm[:, h, csl[c]], AR[:, c, 0:W])
                                    S.mm(psk[:, h * 256:h * 256 + W], ktm[:, h, csl[c]], AR[:, c, 0:W])
                                    S.mm(psn[:, h * 128:(h + 1) * 128], AR[:, c, 0:128], btm[:, h, csl[c]])
                                NBs, KBs, Gs = nbsb[c], kbsb[c], gsb[c]
                                S.tt("dve", NBs.v.re(h3, h=2)[:, :, 0:W], psg[:, 0:512].re(h3, h=2)[:, :, 0:W],
                                     C["gmn"].v.re(h3, h=2)[:, :, 0:W], ALU.mult)
                                S.tt("dve", KBs.v.re(h3, h=2)[:, :, 0:W], psk[:, 0:512].re(h3, h=2)[:, :, 0:W],
                                     C["gm"].v.re(h3, h=2)[:, :, 0:W], ALU.mult)
                                S.tt("dve", Gs.v, psn[:, 0:256], C["msl2d"].v, ALU.mult)
                                S.tt("dve", rxb[c][0].v.re(h4, h=2)[:, :, 128:256], psn[:, 0:256].re(h4, h=2),
                                     C["msl2o"].v.re(h4, h=2), ALU.mult)
                            for c in CR:
                                tok = TOK.s(c)
                                RX = rxb[c][0]
                                psz = PF()
                                for h in range(2):
                                    S.mm(psz[:, h * 64:(h + 1) * 64], kbsb[c][:, h * 256:h * 256 + 128],
                                         tok[:, 384 + 64 * h:384 + 64 * h + 64])
                                for h in range(2):
                                    S.copy("pool", RX[:, h * 256:h * 256 + 64], tok[:, 256 + 64 * h:256 + 64 * h + 64])
                                S.copy("act", RX.v.re(h4, h=2)[:, :, 64:128], psz[:, 0:128].re(h4, h=2))
                            Nv = [[nbsb[c][:, h * 256:h * 256 + 128] for h in range(2)] for c in CR]
                            Gv = [[gsb[c][:, h * 128:(h + 1) * 128] for h in range(2)] for c in CR]
                            rxi = [0 for c in CR]
                            for k in range(5):
                                for c in CR:
                                    RX = rxb[c][rxi[c]]
                                    psr = PF()
                                    for h in range(2):
                                        S.mm(psr[:, h * 256:(h + 1) * 256], Nv[c][h], RX[:, h * 256:(h + 1) * 256])
                                    rxi[c] ^= 1
                                    S.tt("dve", rxb[c][rxi[c]].v, psr[:, 0:512], RX.v, ALU.add)
                                if k < 4:
                                    for c in CR:
                                        pss = PF()
                                        for h in range(2):
                                            S.mm(pss[:, h * 128:(h + 1) * 128], Gv[c][h], Nv[c][h])
                                            S.mm(pss[:, 256 + h * 128:256 + (h + 1) * 128], Nv[c][h], Gv[c][h])
                                        NGn = ngb[c][k % 2]
                                        S.copy("act", NGn.v, pss[:, 0:512])
                                        Nv[c] = [NGn[:, h * 128:(h + 1) * 128] for h in range(2)]
                                        Gv[c] = [NGn[:, 256 + h * 128:256 + (h + 1) * 128] for h in range(2)]
                            ETs = [ngb[c][0][:, 0:256] for c in CR]
                            E2T = [ngb[c][0][:, 256:512] for c in CR]
                            R2s = [ngb[c][1][:, 0:256] for c in CR]
                            Rs = [ngb[c][1][:, 256:512] for c in CR]
                            RXf = [rxb[c][rxi[c]] for c in CR]
                            for c in CR:
                                pbe = PB()
                                for h in range(2):
                                    S.tr(pbe[:, h * 128:(h + 1) * 128], RXf[c][:, h * 256 + 128:h * 256 + 256], identb.v)
                                S.copy("act", ETs[c], pbe[:, 0:256])
                            for c in CR:
                                psr = PF()
                                for h in range(2):
                                    S.mm(psr[:, h * 128:(h + 1) * 128], ETs[c][:, h * 128:(h + 1) * 128], RXf[c][:, h * 256:h * 256 + 128])
                                S.tt("dve", R2s[c].re(h4, h=2), psr[:, 0:256].re(h4, h=2), RXf[c].v.re(h4, h=2)[:, :, 0:128], ALU.add)
                                pse = PF()
                                for h in range(2):
                                    S.mm(pse[:, h * 128:(h + 1) * 128], RXf[c][:, h * 256 + 128:h * 256 + 256], ETs[c][:, h * 128:(h + 1) * 128])
                                S.copy("act", E2T[c], pse[:, 0:256])
                            for c in CR:
                                psr = PF()
                                for h in range(2):
                                    S.mm(psr[:, h * 128:(h + 1) * 128], E2T[c][:, h * 128:(h + 1) * 128], R2s[c][:, h * 128:(h + 1) * 128])
                                S.tt("dve", Rs[c], psr[:, 0:256], R2s[c], ALU.add)
                            for c in CR:
                                tok = TOK.s(c)
                                R = Rs[c]
                                NBs, KBs = nbsb[c], kbsb[c]
                                pspq = PF()
                                for h in range(2):
                                    o = h * 128
                                    S.mm(pspq[0:64, o:o + 64], R[:, o:o + 64], tok[:, 64 * h:64 * h + 64], start=True, stop=False)
                                    S.mm(pspq[0:64, o:o + 64], identfold_b.v, dgh[:, h, c, :], start=False, stop=False)
                                    S.mm(pspq[0:64, o:o + 64], identfold_b.v, dgl[:, h, c, :], start=False, stop=True)
                                    S.mm(pspq[0:64, o + 64:o + 128], tok[:, 64 * h:64 * h + 64], R[:, o + 64:o + 128], start=True, stop=False)
                                    S.mm(pspq[0:64, o + 64:o + 128], tok[:, 128 + 64 * h:128 + 64 * h + 64],
                                         tok[:, 384 + 64 * h:384 + 64 * h + 64], start=False, stop=True)
                                S.copy("act", PQ.s(c), pspq[0:64, 0:256])
                                if own:
                                    psrp = PF()
                                    for h in range(2):
                                        o = h * 128
                                        S.mm(psrp[0:64, o:o + 128], R[:, o:o + 64], NBs[:, h * 256 + 128:h * 256 + 256], start=True, stop=False)
                                        S.mm(psrp[0:64, o:o + 128], identb[:, 64 * h:64 * h + 64], AR[:, c, 128:256], start=False, stop=True)
                                    S.copy("act", RP.s(c), psrp[0:64, 0:256])
                                    psyp = PF()
                                    for h in range(2):
                                        o = h * 128
                                        S.mm(psyp[:, h * 64:(h + 1) * 64], NBs[:, h * 256 + 128:h * 256 + 256], R[:, o + 64:o + 128],
                                             start=True, stop=False)
                                        S.mm(psyp[:, h * 64:(h + 1) * 64], KBs[:, h * 256 + 128:h * 256 + 256],
                                             tok[:, 384 + 64 * h:384 + 64 * h + 64], start=False, stop=True)
                                    S.copy("dve", YP.s(c), psyp[:, 0:128])
                            if blk == 0 and hp == 0:
                                chki("b0e")
                            stv = ST[:, 2 * hp:2 * hp + 2, :]
                            for c in range(NT):
                                if own:
                                    S.copy("pool", STc.s(c).re("p (h i) -> p h i", h=2), stv)
                                pst = PF()
                                for h in range(2):
                                    S.mm(pst[:, h * 64:(h + 1) * 64], PQ.s(c)[:, h * 128:h * 128 + 128],
                                         ST[:, 2 * hp + h, :], signal=(h == 1))
                                S.tt("dve", stv, pst[0:64, 0:128].re("p (h i) -> p h i", h=2),
                                     PQ.s(c).re("p (h f) -> p h f", h=2)[:, :, 64:128], ALU.add)
                            if own:
                                S.dma("sp", "lnw", lnw.ap, d_lnw[:, hc], outs=[lnw.v])
                                S.dma("sp", "lnb", lnb.ap, d_lnb[:, hc], outs=[lnb.v])
                                for c in range(NT):
                                    cs_ = slice(c * 128, (c + 1) * 128)
                                    tok = TOK.s(c)
                                    psy = PF()
                                    for h in range(2):
                                        S.mm(psy[:, h * 64:(h + 1) * 64], RP.s(c)[:, h * 128:(h + 1) * 128],
                                             STc.s(c)[:, h * 64:(h + 1) * 64], signal=(h == 1))
                                    S.tt("dve", yt.v, psy[:, 0:128], YP.s(c), ALU.add)
                                    y3 = yt.v.re("p (h i) -> p h i", h=2)
                                    S.op("dve", "tensor_reduce", [st1.v], [yt.v], (st1.ap, y3.ap), dict(axis=AX.X, op=ALU.add))
                                    S.act(ysq.v, yt.v, AF.Square)
                                    S.op("dve", "tensor_reduce", [st2.v], [ysq.v],
                                         (st2.ap, ysq.v.re("p (h i) -> p h i", h=2).ap), dict(axis=AX.X, op=ALU.add))
                                    S.ts("dve", mean.v, st1.v, 1.0 / 64, None, op0=ALU.mult)
                                    S.tt("dve", msq.v, mean.v, mean.v, ALU.mult)
                                    S.stt(var.v, st2.v, 1.0 / 64, msq.v, ALU.mult, ALU.subtract)
                                    S.act(var.v, var.v, AF.Sqrt, bias=eps_g.v)
                                    S.op("dve", "reciprocal", [var.v], [var.v], (var.ap, var.ap))
                                    for h in range(2):
                                        S.ts("dve", yt[:, h * 64:(h + 1) * 64], yt[:, h * 64:(h + 1) * 64], mean[:, h:h + 1],
                                             var[:, h:h + 1], op0=ALU.subtract, op1=ALU.mult)
                                    S.tt("pool", yt.v, yt.v, lnw.v, ALU.mult)
                                    S.tt("pool", yt.v, yt.v, lnb.v, ALU.add)
                                    psb_ = PF()
                                    S.mm(psb_[:, 0:2], rkb[:, cs_], headsel_b.v)
                                    S.copy("act", bon.v, psb_[:, 0:2])
                                    for h in range(2):
                                        S.stt(yt[:, h * 64:(h + 1) * 64], tok[:, 384 + 64 * h:384 + 64 * h + 64], bon[:, h:h + 1],
                                              yt[:, h * 64:(h + 1) * 64], ALU.mult, ALU.add)
                                    psg_ = PF()
                                    S.mm(psg_[:, 0:128], sgd0[:, cs_], g2a.v, start=True, stop=False, signal=False)
                                    S.mm(psg_[:, 0:128], sgd1[0:32, cs_], g2b.v, start=False, stop=True)
                                    S.tt("dve", ytb.v, psg_[:, 0:128], yt.v, ALU.mult)
                                    pb = PB()
                                    S.tr(pb[:, 0:128], ytb.v, identb.v)
                                    S.copy("act", yT.s(hp)[:, ob * TB + c * 128:ob * TB + (c + 1) * 128], pb[:, 0:128])
                        if blk == 0:
                            chki("b0f")
                        if halo or own:
                            for g in range(2):
                                psv, _ = misc_proj("ka%d" % g)
                                for par in range(2):
                                    pl = slice(64 * par, 64 * par + 64)
                                    S.act(kdup[g][par][pl, 128:128 + TB], psv[pl, :], AF.Identity,
                                          bias=C["b_in_l"][pl, CBI["ka%d" % g]:CBI["ka%d" % g] + 1])
                            psv, _ = misc_proj("va")
                            S.act(vaT.v, psv, AF.Identity, bias=C["b_in_l"][:, CBI["va"]:CBI["va"] + 1])
                            for n in range(NT):
                                pb = PB()
                                S.tr(pb[:, 0:128], vaT[:, n * 128:(n + 1) * 128], identb.v)
                                S.copy("dve", vaug[:, 1 + n, :, 0:64], pb[:, 0:128].re("p (g d) -> p g d", g=2))
                        if own:
                            for i in range(8):
                                psv, _ = misc_proj("q%d" % i)
                                S.act(qT.s(i), psv, AF.Identity, bias=bq8[:, i:i + 1], scale=0.125)
                            for n in range(NT):
                                for g in range(2):
                                    for par in range(2):
                                        pl = slice(64 * par, 64 * par + 64)
                                        ps0, ps1 = PF(), PF()
                                        qv = qT[:, 4 * g:4 * g + 4, n * 128:(n + 1) * 128]
                                        S.mm(ps0[:, 0:512].re("p (j t) -> p j t", j=4), kdup[g][par][:, n * 128:(n + 1) * 128], qv)
                                        S.mm(ps1[:, 0:512].re("p (j t) -> p j t", j=4), kdup[g][par][:, (n + 1) * 128:(n + 2) * 128], qv)
                                        for j in range(4):
                                            h = 8 * g + 2 * j + par
                                            js = slice(j * 128, (j + 1) * 128)
                                            S.stt(et[0][:, js], C["negdp"].v, SLOPES[h], ps0[:, js], ALU.mult, ALU.add)
                                            S.stt(et[1][:, js], C["negdc"].v, SLOPES[h], ps1[:, js], ALU.mult, ALU.add)
                                        if blk == 6 and n == 0:
                                            S.act(PT[0].v, et[0].v, AF.Exp, bias=hb.v)
                                        else:
                                            S.act(PT[0].v, et[0].v, AF.Exp)
                                        S.act(PT[1].v, et[1].v, AF.Exp)
                                        pso = PF()
                                        for j in range(4):
                                            js = slice(j * 128, (j + 1) * 128)
                                            S.mm(pso[:, j * 65:(j + 1) * 65], PT[0][:, js], vaug[:, n, g, :],
                                                 start=True, stop=False, signal=False)
                                            S.mm(pso[:, j * 65:(j + 1) * 65], PT[1][:, js], vaug[:, n + 1, g, :],
                                                 start=False, stop=True, signal=(j == 3))
                                        i0 = g * 8 + par * 4
                                        S.tt("dve", den.v, pso[:, 0:260].re("p (j e) -> p j e", j=4)[:, :, 64],
                                             esink[:, i0:i0 + 4], ALU.add)
                                        S.op("dve", "reciprocal", [den.v], [den.v], (den.ap, den.ap))
                                        for j in range(4):
                                            h = 8 * g + 2 * j + par
                                            S.act(otok[:, h * 64:(h + 1) * 64], pso[:, j * 65:j * 65 + 64], AF.Identity,
                                                  scale=den[:, j:j + 1])
                                for i in range(8):
                                    pb = PB()
                                    S.tr(pb[:, 0:128], otok[:, i * 128:(i + 1) * 128], identb.v)
                                    S.copy("act" if i % 2 else "dve",
                                           yT.s(8 + i)[:, ob * TB + n * 128:ob * TB + (n + 1) * 128], pb[:, 0:128])
                        if halo or own:
                            for g in range(2):
                                for par in range(2):
                                    pl = slice(64 * par, 64 * par + 64)
                                    S.copy("pool", kdup[g][par][pl, 0:128], kdup[g][par][pl, TB:TB + 128])
                            S.copy("pool", vaug[:, 0, :, 0:64], vaug[:, NT, :, 0:64])
                        S.flush()
                        if chk("blk%d" % blk, soft=True):
                            break
                except StopInner:
                    stopflag[0] = True
                S.barrier()
            if stopflag[0]:
                raise StopBuild()
            chk("p1")

            with ExitStack() as p2:
                slabs = [sb(p2, "slab%d" % i, [128, 16, 256], BF16) for i in range(4)]
                mg = sb(p2, "mg", [128, 16, TO], BF16, slots=16)
                scnt = [0]

                def slab_load(src_v, nk):
                    scnt[0] += 1
                    i = scnt[0] % 4
                    S.dma_split("pool", "slab%d" % i, slabs[i].ap[:, 0:nk, :], src_v, [slabs[i].v], nk)
                    return slabs[i]

                def slab_load2(src_a, src_b):
                    scnt[0] += 1
                    i = scnt[0] % 4
                    S.dma_split("pool", "slab%d" % i, slabs[i].ap[:, 0:8, :], src_a, [slabs[i].v], 8)
                    S.dma_split("pool", "slab%d" % i, slabs[i].ap[:, 8:16, :], src_b, [slabs[i].v], 8)
                    return slabs[i]

                wbr_v = w_br.rearrange("(kc p) n -> p kc n", p=128)
                wba_v = w_ba.rearrange("(kc p) n -> p kc n", p=128)
                wout_v = w_out.rearrange("(kc p) n -> p kc n", p=128)
                wup_v = w_up.rearrange("(kc p) n -> p kc n", p=128)
                wdn_v = w_down.rearrange("(fc p) n -> p fc n", p=128)

                with ExitStack() as d1:
                    hT = sb(d1, "hT", [128, 16, TO], BF16, slots=16)
                    xs2 = sb(d1, "xs2", [128, D])
                    xTt = sb(d1, "xTt", [128, 16, 128])
                    sq2 = [sb(d1, "sq2%d" % i, [128, 128]) for i in range(2)]
                    rs2 = sb(d1, "rs2", [128, 128])
                    tg = [sb(d1, "tg%d" % i, [128, GS]) for i in range(4)]
                    for n in range(2 * NT):
                        ts_ = slice(n * 128, (n + 1) * 128)
                        S.dma("sp", "xs2", xs2.ap, xb[6 + n // NT, (n % NT) * 128:(n % NT + 1) * 128, :], outs=[xs2.v])
                        for g4 in range(4):
                            ps = PF()
                            for j in range(4):
                                dc = 4 * g4 + j
                                S.tr(ps[:, j * 128:(j + 1) * 128], xs2[:, dc * 128:(dc + 1) * 128], identf.v, signal=(j == 3))
                            S.copy("act" if g4 % 2 else "dve", xTt[:, 4 * g4:4 * g4 + 4, :],
                                   ps[:, 0:512].re("p (j t) -> p j t", j=4))
                        msps = PF()
                        for dc in range(16):
                            sq = sq2[dc % 2]
                            S.act(sq.v, xTt[:, dc, :], AF.Square)
                            S.mm(msps[:, 0:128], C["onesdiv"].v, sq.v, start=(dc == 0), stop=(dc == 15), signal=(dc == 15))
                        S.act(rs2.v, msps[:, 0:128], AF.Sqrt, bias=eps_n.v)
                        S.op("dve", "reciprocal", [rs2.v], [rs2.v], (rs2.ap, rs2.ap))
                        for dc in range(16):
                            tmp = sq2[dc % 2]
                            S.tt("dve", tmp.v, xTt[:, dc, :], rs2.v, ALU.mult)
                            S.act(hT.s(dc)[:, ts_], tmp.v, AF.Identity, bias=sh1[:, dc:dc + 1], scale=g1[:, dc:dc + 1])
                    gt0 = CBI["gt0"]
                    for d2 in range(8):
                        c0 = d2 * 256
                        s_gr = slab_load(win_v[:, :, 4640 + c0:4640 + c0 + 256], 16)
                        s_ga = slab_load(win_v[:, :, 4640 + 2048 + c0:4640 + 2048 + c0 + 256], 16)
                        s_b = slab_load2(wbr_v[:, :, c0:c0 + 256], wba_v[:, :, c0:c0 + 256])
                        for jj in range(2):
                            dblk = 2 * d2 + jj
                            js = slice(jj * 128, (jj + 1) * 128)
                            for grp in range(NG):
                                gs = slice(grp * GS, (grp + 1) * GS)
                                ps = PF()
                                for kc in range(16):
                                    S.mm(ps[:, 0:GS], s_gr[:, kc, js], hT.s(kc)[:, gs], start=(kc == 0), stop=(kc == 15),
                                         signal=(kc == 15))
                                S.act(tg[0].v, ps[:, 0:GS], AF.Sigmoid, bias=C["b_in_l"][:, gt0 + dblk:gt0 + dblk + 1])
                                ps = PF()
                                for kc in range(16):
                                    S.mm(ps[:, 0:GS], s_ga[:, kc, js], hT.s(kc)[:, gs], start=(kc == 0), stop=(kc == 15),
                                         signal=(kc == 15))
                                S.act(tg[1].v, ps[:, 0:GS], AF.Sigmoid,
                                      bias=C["b_in_l"][:, gt0 + 16 + dblk:gt0 + 16 + dblk + 1])
                                ps = PF()
                                for kc in range(8):
                                    S.mm(ps[:, 0:GS], s_b[:, kc, js], yT.s(kc)[:, gs], start=(kc == 0), stop=(kc == 7),
                                         signal=(kc == 7))
                                S.tt("dve", tg[2].v, ps[:, 0:GS], tg[0].v, ALU.mult)
                                ps = PF()
                                for kc in range(8):
                                    S.mm(ps[:, 0:GS], s_b[:, 8 + kc, js], yT.s(8 + kc)[:, gs], start=(kc == 0), stop=(kc == 7),
                                         signal=(kc == 7))
                                S.tt("dve", tg[3].v, ps[:, 0:GS], tg[1].v, ALU.mult)
                                S.tt("pool", mg.s(dblk)[:, gs], tg[2].v, tg[3].v, ALU.add)
                    S.barrier()

                with ExitStack() as d2s:
                    x1T = sb(d2s, "x1T", [128, 16, TO], F32, slots=16)
                    xs3 = sb(d2s, "xs3", [128, D])
                    sq3 = [sb(d2s, "sq3%d" % i, [128, TO]) for i in range(2)]
                    rs3 = sb(d2s, "rs3", [128, TO])
                    fin_t = sb(d2s, "fin_t", [128, 16, 128])
                    ostage = xs3
                    for n in range(2 * NT):
                        S.dma("sp", "xs3", xs3.ap, xb[6 + n // NT, (n % NT) * 128:(n % NT + 1) * 128, :], outs=[xs3.v])
                        for g4 in range(4):
                            ps = PF()
                            for j in range(4):
                                dc = 4 * g4 + j
                                S.tr(ps[:, j * 128:(j + 1) * 128], xs3[:, dc * 128:(dc + 1) * 128], identf.v, signal=(j == 3))
                            S.copy("act" if g4 % 2 else "dve", x1T.s(4 * g4, 4)[:, :, n * 128:(n + 1) * 128],
                                   ps[:, 0:512].re("p (j t) -> p j t", j=4))
                    for d2 in range(8):
                        sl = slab_load(wout_v[:, :, d2 * 256:(d2 + 1) * 256], 16)
                        for jj in range(2):
                            dblk = 2 * d2 + jj
                            js = slice(jj * 128, (jj + 1) * 128)
                            for grp in range(NG):
                                gs = slice(grp * GS, (grp + 1) * GS)
                                ps = PF()
                                for kc in range(16):
                                    S.mm(ps[:, 0:GS], sl[:, kc, js], mg.s(kc)[:, gs], start=(kc == 0), stop=(kc == 15),
                                         signal=(kc == 15))
                                S.stt(x1T.s(dblk)[:, gs], ps[:, 0:GS], gt1[:, dblk:dblk + 1], x1T.s(dblk)[:, gs],
                                      ALU.mult, ALU.add)

                    def rms_stats():
                        for grp in range(NG):
                            gs = slice(grp * GS, (grp + 1) * GS)
                            msps = PF()
                            for dc in range(16):
                                sq = sq3[dc % 2]
                                S.act(sq[:, gs], x1T.s(dc)[:, gs], AF.Square)
                                S.mm(msps[:, 0:GS], C["onesdiv"].v, sq[:, gs], start=(dc == 0), stop=(dc == 15),
                                     signal=(dc == 15))
                            S.act(rs3[:, gs], msps[:, 0:GS], AF.Sqrt, bias=eps_n.v)
                        S.op("dve", "reciprocal", [rs3.v], [rs3.v], (rs3.ap, rs3.ap))

                    rms_stats()
                    h2T = yT
                    for dc in range(16):
                        tmp = sq3[dc % 2]
                        S.tt("dve", tmp.v, x1T.s(dc), rs3.v, ALU.mult)
                        S.act(h2T.s(dc), tmp.v, AF.Identity, bias=sh2[:, dc:dc + 1], scale=g2[:, dc:dc + 1])
                    for fg in range(4):
                        for f2 in range(8):
                            c0 = fg * 2048 + f2 * 256
                            sl = slab_load(wup_v[:, :, c0:c0 + 256], 16)
                            for jj in range(2):
                                fb = 2 * f2 + jj
                                js = slice(jj * 128, (jj + 1) * 128)
                                for grp in range(NG):
                                    gs = slice(grp * GS, (grp + 1) * GS)
                                    ps = PF()
                                    for kc in range(16):
                                        S.mm(ps[:, 0:GS], sl[:, kc, js], h2T.s(kc)[:, gs], start=(kc == 0), stop=(kc == 15),
                                             signal=(kc == 15))
                                    rl = sq3[(fb + grp) % 2]
                                    S.act(rl[:, 0:GS], ps[:, 0:GS], AF.Relu)
                                    S.tt("pool", mg.s(fb)[:, gs], rl[:, 0:GS], rl[:, 0:GS], ALU.mult)
                        for d2 in range(8):
                            sl = slab_load(wdn_v[:, fg * 16:(fg + 1) * 16, d2 * 256:(d2 + 1) * 256], 16)
                            for jj in range(2):
                                dblk = 2 * d2 + jj
                                js = slice(jj * 128, (jj + 1) * 128)
                                for grp in range(NG):
                                    gs = slice(grp * GS, (grp + 1) * GS)
                                    ps = PF()
                                    for fb in range(16):
                                        S.mm(ps[:, 0:GS], sl[:, fb, js], mg.s(fb)[:, gs], start=(fb == 0), stop=(fb == 15),
                                             signal=(fb == 15))
                                    S.stt(x1T.s(dblk)[:, gs], ps[:, 0:GS], gt2[:, dblk:dblk + 1], x1T.s(dblk)[:, gs],
                                          ALU.mult, ALU.add)
                    rms_stats()
                    outb = Buf()
                    for n in range(2 * NT):
                        ts_ = slice(n * 128, (n + 1) * 128)
                        for dc in range(16):
                            S.stt(fin_t[:, dc, :], x1T.s(dc)[:, ts_], C["gainf_l"][:, dc:dc + 1], rs3[:, ts_],
                                  ALU.mult, ALU.mult)
                        for g4 in range(4):
                            ps = PF()
                            for j in range(4):
                                dc = 4 * g4 + j
                                S.tr(ps[:, j * 128:(j + 1) * 128], fin_t[:, dc, :], identf.v, signal=(j == 3))
                            S.copy("act" if g4 % 2 else "dve", ostage[:, g4 * 512:(g4 + 1) * 512], ps[:, 0:512])
                        S.dma("sp", "ostore", out[n * 128:(n + 1) * 128, :], ostage.ap, outs=[V(None, [outb])], ins=[ostage.v])
                    S._wait(S.E["sp"], outb.w)
                    S.barrier()
        except StopBuild:
            S.barrier()
    return nc


_NC_CACHE = {}
_DBG = {"stop": None, "cores": None}


def kernel(**inputs):
    NT = inputs["x"].shape[1] // (8 * 128)
    if NT not in _NC_CACHE:
        _NC_CACHE[NT] = build(NT, _DBG["stop"])
    nc = _NC_CACHE[NT]
    maps = host_inputs(inputs, NT)
    if _DBG["cores"] is not None:
        sel = _DBG["cores"]
        res = run_bass_kernel_spmd(nc, [maps[c] for c in sel], core_ids=list(range(len(sel))))
        return {c: np.asarray(res.results[i]["out"], np.float32) for i, c in enumerate(sel)}
    res = run_bass_kernel_spmd(nc, maps, core_ids=list(range(8)))
    TB = 128 * NT
    B = inputs["x"].shape[0]
    outp = np.zeros((B, 8 * TB, D), np.float32)
    for cid in range(8):
        b, q = cid // 4, cid % 4
        outp[b, q * 2 * TB:(q + 1) * 2 * TB] = np.asarray(res.results[cid]["out"], np.float32)
    return outp
```

```python
import math
from contextlib import ExitStack

import numpy as np
import concourse.bass as bass
import concourse.mybir as mybir
from concourse.bass_utils import run_bass_kernel_spmd

F32 = mybir.dt.float32
BF16 = mybir.dt.bfloat16
AF = mybir.ActivationFunctionType
ALU = mybir.AluOpType
AX = mybir.AxisListType

D = 2048
HD = 64
NBLK_MIX = 8
CDEC = math.exp(-0.5)
NEG = -1.0e6


class Buf:
    __slots__ = ("w", "r")

    def __init__(self):
        self.w = None
        self.r = []


class V:
    __slots__ = ("ap", "bufs")

    def __init__(self, ap, bufs):
        self.ap, self.bufs = ap, bufs

    def __getitem__(self, idx):
        return V(self.ap[idx], self.bufs)

    def re(self, pat, **kw):
        return V(self.ap.rearrange(pat, **kw), self.bufs)


class TT:
    def __init__(self, t, slots=0):
        self.t = t
        self.ap = t[:] if not isinstance(t, bass.AP) else t
        self.slots = slots
        self.bufs = [Buf() for _ in range(max(1, slots))]

    @property
    def v(self):
        return V(self.ap, self.bufs)

    def __getitem__(self, idx):
        return V(self.ap[idx], self.bufs)

    def s(self, i, n=1):
        if n == 1:
            return V(self.ap[:, i], self.bufs[i:i + 1])
        return V(self.ap[:, i:i + n], self.bufs[i:i + n])


class Eng:
    def __init__(self, name, sem):
        self.name, self.sem = name, sem
        self.n = 0
        self.seen = {}
        self.prog = []


class Sync:
    ENG = ("pe", "act", "dve", "pool", "sp")

    def __init__(self, nc, sems):
        self.nc = nc
        self.E = {n: Eng(n, s) for n, s in zip(self.ENG, sems[:5])}
        self.free_sems = list(sems[5:])
        self.dmasem = {}
        self.block = None
        self.ninst = 0
        self.limit = None

    def _wait(self, e, ev):
        if ev is None:
            return
        key, cnt = ev
        if key == e.name and (key == "pe" or cnt > e.n):
            return
        if e.seen.get(key, 0) >= cnt:
            return
        e.seen[key] = cnt
        sem = self.E[key].sem if key in self.E else self.dmasem[key][0]
        e.prog.append(("wait_ge", (sem, cnt), {}, None))

    def _deps(self, e, reads, writes):
        for b in reads:
            self._wait(e, b.w)
        for b in writes:
            self._wait(e, b.w)
            for ev in b.r:
                self._wait(e, ev)

    def _record(self, ev, reads, writes):
        for b in reads:
            b.r.append(ev)
            if len(b.r) > 16:
                m = {}
                for k, c in b.r:
                    m[k] = max(m.get(k, 0), c)
                b.r = list(m.items())
        for b in writes:
            b.w = ev
            b.r = []

    def op(self, ename, method, outs, ins, args=(), kwargs=None, signal=True):
        if self.limit is not None and self.ninst >= self.limit:
            return
        e = self.E[ename]
        reads = [b for v in ins if isinstance(v, V) for b in v.bufs]
        writes = [b for v in outs for b in v.bufs]
        self._deps(e, reads, writes)
        if signal or True:
            e.n += 1
            ev = (ename, e.n)
            e.prog.append((method, args, kwargs or {}, (e.sem, 1)))
        else:
            ev = (ename, e.n + 1)
            e.prog.append((method, args, kwargs or {}, None))
        self._record(ev, reads, writes)
        self.ninst += 1

    def dma(self, qname, key, out, in_, outs=(), ins=()):
        e = self.E[qname]
        reads = [b for v in ins for b in v.bufs]
        writes = [b for v in outs for b in v.bufs]
        self._deps(e, reads, writes)
        if key not in self.dmasem:
            self.dmasem[key] = [self.free_sems.pop(), 0]
        ds = self.dmasem[key]
        ds[1] += 16
        e.prog.append(("dma_start", (), dict(out=out, in_=in_), (ds[0], 16)))
        self._record((key, ds[1]), reads, writes)

    def dma_split(self, qname, key, out, in_, outs, nk, step=4):
        for k0 in range(0, nk, step):
            k1 = min(nk, k0 + step)
            self.dma(qname, key, out[:, k0:k1, :], in_[:, k0:k1, :], outs=outs)

    def flush(self):
        decs = dict(pe=self.block.tensor, act=self.block.scalar, dve=self.block.vector,
                    pool=self.block.gpsimd, sp=self.block.sync)
        for name in self.ENG:
            e = self.E[name]
            if not e.prog:
                continue
            prog = e.prog
            e.prog = []

            def body(eng, prog=prog):
                for method, args, kwargs, inc in prog:
                    ins = getattr(eng, method)(*args, **kwargs)
                    if inc is not None:
                        ins.then_inc(inc[0], inc[1])
            decs[name](body)

    def barrier(self):
        evs = [(n, self.E[n].n) for n in self.ENG if self.E[n].n > 0]
        evs += [(k, v[1]) for k, v in self.dmasem.items()]
        for n in self.ENG:
            for ev in evs:
                if ev[0] != n:
                    self._wait(self.E[n], ev)
        self.flush()

    def mm(self, out, lhsT, rhs, start=True, stop=True, signal=True):
        self.op("pe", "matmul", [out], [lhsT, rhs], (out.ap, lhsT.ap, rhs.ap),
                dict(start=start, stop=stop), signal=signal)

    def tr(self, out, in_, ident, signal=True):
        self.op("pe", "transpose", [out], [in_, ident], (out.ap, in_.ap, ident.ap), signal=signal)

    def act(self, out, in_, func, bias=None, scale=None):
        kw = {}
        ins = [in_]
        if bias is not None:
            kw["bias"] = bias.ap if isinstance(bias, V) else bias
            ins.append(bias)
        if scale is not None:
            kw["scale"] = scale.ap if isinstance(scale, V) else scale
            ins.append(scale)
        self.op("act", "activation", [out], ins, (out.ap, in_.ap, func), kw)

    def tt(self, eng, out, in0, in1, op):
        self.op(eng, "tensor_tensor", [out], [in0, in1], (out.ap, in0.ap, in1.ap, op))

    def ts(self, eng, out, in0, s1, s2=None, op0=ALU.mult, op1=None):
        a1 = s1.ap if isinstance(s1, V) else s1
        a2 = s2.ap if isinstance(s2, V) else s2
        kw = {}
        if op1 is not None:
            kw["op1"] = op1
        self.op(eng, "tensor_scalar", [out], [in0, s1, s2], (out.ap, in0.ap, a1, a2, op0), kw)

    def stt(self, out, in0, scalar, in1, op0, op1):
        sc = scalar.ap if isinstance(scalar, V) else scalar
        self.op("dve", "scalar_tensor_tensor", [out], [in0, scalar, in1],
                (out.ap, in0.ap, sc, in1.ap, op0, op1))

    def copy(self, eng, out, in_):
        if eng == "act":
            self.op("act", "copy", [out], [in_], (out.ap, in_.ap))
        else:
            self.op(eng, "tensor_copy", [out], [in_], (out.ap, in_.ap))

    def memset(self, eng, out, val):
        self.op(eng, "memset", [out], [], (out.ap, val))


def col_blocks():
    blocks = []
    for i in range(8):
        blocks.append(("r%d" % i, [(128 * i, 128)]))
    for i in range(8):
        blocks.append(("k%d" % i, [(1024 + 128 * i, 128)]))
    for i in range(8):
        blocks.append(("v%d" % i, [(2048 + 128 * i, 128)]))
    blocks.append(("wa", [(3072, 128)]))
    blocks.append(("g0", [(3200, 128)]))
    blocks.append(("g1", [(3328, 32)]))
    for i in range(8):
        blocks.append(("q%d" % i, [(3360 + 128 * i, 128)]))
    blocks.append(("ka0", [(4384, 64), (4384, 64)]))
    blocks.append(("ka1", [(4448, 64), (4448, 64)]))
    blocks.append(("va", [(4512, 128)]))
    for i in range(32):
        blocks.append(("gt%d" % i, [(4640 + 128 * i, 128)]))
    return blocks


CB = col_blocks()
CBI = {name: i for i, (name, _) in enumerate(CB)}
NCB = len(CB)

HEAD_ORDER = [8 * g + 2 * j + par for g in range(2) for par in range(2) for j in range(4)]
SLOPES = [2.0 ** (-8.0 * (h + 1) / 16.0) for h in range(16)]


def pmaj(vec, n):
    return np.ascontiguousarray(np.asarray(vec, np.float32).reshape(n, 128).T)


def make_consts(TB):
    p = np.arange(128)[:, None]
    f = np.arange(128)[None, :]
    c = {}
    c["ident"] = (p == f).astype(np.float32)
    msu = (f > p).astype(np.float32)
    mu = (f >= p).astype(np.float32)
    msl = (p > f).astype(np.float32)
    bd = ((p // 32) == (f // 32)).astype(np.float32)
    c["gm"] = np.ascontiguousarray(np.concatenate([msu, mu, msu, mu], 1))
    c["gmn"] = np.ascontiguousarray(np.concatenate([msu * bd, mu, msu * bd, mu], 1))
    c["msl2d"] = np.ascontiguousarray(np.concatenate([msl * bd, msl * bd], 1))
    c["msl2o"] = np.ascontiguousarray(np.concatenate([msl * (1 - bd), msl * (1 - bd)], 1))
    cm = np.ones((128, TB), np.float32)
    cm[:, ::128] = 0.0
    c["chunkmask"] = cm
    c["identfold"] = ((p % 64) == np.arange(64)[None, :]).astype(np.float32)
    c["headsel"] = ((p // 64) == np.arange(2)[None, :]).astype(np.float32)
    c["blkones"] = ((p // 64) == (f // 64)).astype(np.float32)
    dist_prev = f + 128 - p
    c["negdp"] = np.where(p > f, -dist_prev, NEG).astype(np.float32)
    dist_cur = f - p
    c["negdc"] = np.where(p <= f, -dist_cur, NEG).astype(np.float32)
    c["onesdiv"] = np.full((128, 128), 1.0 / D, np.float32)
    return c


def host_inputs(inp, NT):
    TB = 128 * NT
    x = np.asarray(inp["x"], np.float32)
    B, T, _ = x.shape
    assert T == 8 * TB and B == 2
    shared = {}
    shared.update(make_consts(TB))
    shared["w_ada"] = np.ascontiguousarray(inp["w_ada"][0])
    shared["w_in"] = np.ascontiguousarray(inp["w_in"][0])
    shared["w_br"] = np.ascontiguousarray(inp["w_branch_rwkv"][0])
    shared["w_ba"] = np.ascontiguousarray(inp["w_branch_attn"][0])
    shared["w_out"] = np.ascontiguousarray(inp["w_out"][0])
    shared["w_up"] = np.ascontiguousarray(inp["w_up"][0])
    shared["w_down"] = np.ascontiguousarray(inp["w_down"][0])
    shared["b_ada_l"] = pmaj(inp["b_ada"][0], 96)
    shared["gain1_l"] = pmaj(inp["norm1_gain"][0], 16)
    shared["gain2_l"] = pmaj(inp["norm2_gain"][0], 16)
    shared["gainf_l"] = pmaj(inp["final_gain"], 16)
    b_in = np.asarray(inp["b_in"][0], np.float32)
    bl = np.zeros((128, NCB), np.float32)
    for i, (_, pieces) in enumerate(CB):
        off = 0
        for st, wd in pieces:
            bl[off:off + wd, i] = b_in[st:st + wd]
            off += wd
    shared["b_in_l"] = bl
    mix = np.asarray(inp["rwkv_mix"][0], np.float32)
    ml = np.zeros((128, 27), np.float32)
    for i in range(27):
        st, wd = CB[i][1][0]
        ml[:wd, i] = mix[st:st + wd]
    shared["mix_l"] = ml
    shared["w0_l"] = pmaj(inp["rwkv_w0"][0], 8)
    shared["a0_l"] = pmaj(inp["rwkv_a0"][0], 8)
    shared["kk_l"] = pmaj(inp["rwkv_k_k"][0], 8)
    shared["ka_l"] = pmaj(inp["rwkv_k_a"][0], 8)
    shared["rk_l"] = pmaj(np.asarray(inp["rwkv_r_k"][0]).reshape(-1), 8)
    z64 = np.zeros((64, 1024), np.float32)
    shared["w2z"] = np.ascontiguousarray(np.concatenate([np.asarray(inp["rwkv_w2"][0], np.float32), z64], 0))
    shared["a2z"] = np.ascontiguousarray(np.concatenate([z64, np.asarray(inp["rwkv_a2"][0], np.float32)], 0))
    g2 = np.asarray(inp["rwkv_g2"][0], np.float32)
    shared["g2a"] = np.ascontiguousarray(g2[:128])
    shared["g2b"] = np.ascontiguousarray(g2[128:160])
    shared["lnw_b"] = np.ascontiguousarray(np.broadcast_to(np.asarray(inp["rwkv_ln_w"][0], np.float32)[None], (128, 1024)))
    shared["lnb_b"] = np.ascontiguousarray(np.broadcast_to(np.asarray(inp["rwkv_ln_b"][0], np.float32)[None], (128, 1024)))
    sinks = np.asarray(inp["attn_sinks"][0], np.float32)[HEAD_ORDER]
    shared["sinks_b"] = np.ascontiguousarray(np.broadcast_to(sinks[None], (128, 16)))
    maps = []
    for cid in range(8):
        b, q = cid // 4, cid % 4
        m = dict(shared)
        xb = np.zeros((NBLK_MIX, TB, D), np.float32)
        fl = np.zeros((128, 8), np.float32)
        for j in range(NBLK_MIX):
            st = (2 * q - 6 + j) * TB
            if st >= 0:
                xb[j] = x[b, st:st + TB]
                fl[:, j] = 1.0
        m["xb"] = xb
        m["flags"] = fl
        m["cvec"] = pmaj(inp["c"][b], 16)
        maps.append(m)
    return maps


class StopBuild(Exception):
    pass


class StopInner(Exception):
    pass


def build(NT, stop=None):
    TB = 128 * NT

    stopflag = [False]

    def chki(tag):
        print("chk", tag, S.ninst)
        if stop is not None and stop == tag:
            raise StopInner()

    def chk(tag, soft=False):
        if stop is not None and stop == tag:
            if soft:
                stopflag[0] = True
                return True
            raise StopBuild()
        return False

    TO = 2 * TB
    GS = min(512, TO)
    NG = TO // GS
    nc = bass.Bass("TRN2", target_bir_lowering=False)

    def din(name, shape):
        return nc.dram_tensor(name, list(shape), F32, kind="ExternalInput").ap()

    xb = din("xb", [NBLK_MIX, TB, D])
    d_flags = din("flags", [128, 8])
    d_cvec = din("cvec", [128, 16])
    w_ada = din("w_ada", [D, 6 * D])
    w_in = din("w_in", [D, 8736])
    w_br = din("w_br", [1024, D])
    w_ba = din("w_ba", [1024, D])
    w_out = din("w_out", [D, D])
    w_up = din("w_up", [D, 4 * D])
    w_down = din("w_down", [4 * D, D])
    small = {}
    for name, shape in [("b_ada_l", [128, 96]), ("gain1_l", [128, 16]), ("gain2_l", [128, 16]),
                        ("gainf_l", [128, 16]), ("b_in_l", [128, NCB]), ("mix_l", [128, 27]),
                        ("w0_l", [128, 8]), ("a0_l", [128, 8]), ("kk_l", [128, 8]), ("ka_l", [128, 8]),
                        ("rk_l", [128, 8]), ("sinks_b", [128, 16]),
                        ("ident", [128, 128]), ("gm", [128, 512]), ("gmn", [128, 512]), ("msl2d", [128, 256]),
                        ("msl2o", [128, 256]),
                        ("chunkmask", [128, TB]), ("identfold", [128, 64]), ("headsel", [128, 2]),
                        ("blkones", [128, 128]), ("negdp", [128, 128]), ("negdc", [128, 128]),
                        ("onesdiv", [128, 128])]:
        small[name] = (din(name, shape), shape)
    d_w2z = din("w2z", [128, 1024])
    d_a2z = din("a2z", [128, 1024])
    d_g2a = din("g2a", [128, 1024])
    d_g2b = din("g2b", [32, 1024])
    d_lnw = din("lnw_b", [128, 1024])
    d_lnb = din("lnb_b", [128, 1024])
    out = nc.dram_tensor("out", [TO, D], F32, kind="ExternalOutput").ap()

    win_v = w_in.rearrange("(kc p) n -> p kc n", p=128)

    with ExitStack() as es:
        sems = [es.enter_context(nc.semaphore("s%d" % i)) for i in range(40)]
        S = Sync(nc, sems)
        if isinstance(stop, str) and stop.startswith("n"):
            S.limit = int(stop[1:])
        S.block = es.enter_context(nc.Block())

        def sb(st, name, shape, dt=F32, slots=0):
            return TT(st.enter_context(nc.sbuf_tensor("sb_" + name, list(shape), dt)), slots)

        psf = [TT(es.enter_context(nc.psum_tensor("psf%d" % i, [128, 512], F32))) for i in range(6)]
        psb = [TT(es.enter_context(nc.psum_tensor("psb%d" % i, [128, 1024], BF16))) for i in range(2)]
        pcnt = [0, 0]

        def PF():
            pcnt[0] += 1
            return psf[pcnt[0] % 6]

        def PB():
            pcnt[1] += 1
            return psb[pcnt[1] % 2]

        C = {}
        for name, (ap, shape) in small.items():
            C[name] = sb(es, "c_" + name, shape)
        for name, (ap, shape) in small.items():
            S.dma("sp", "const", C[name].ap, ap, outs=[C[name].v])
        identb = sb(es, "identb", [128, 128], BF16)
        blkones_b = sb(es, "blkonesb", [128, 128], BF16)
        headsel_b = sb(es, "headselb", [128, 2], BF16)
        flags = sb(es, "flags", [128, 8])
        cvec = sb(es, "cvec", [128, 16])
        S.dma("sp", "const", flags.ap, d_flags, outs=[flags.v])
        S.dma("sp", "const", cvec.ap, d_cvec, outs=[cvec.v])
        fin = ("const", S.dmasem["const"][1])
        for t in list(C.values()) + [flags, cvec]:
            t.bufs[0].w = fin
        S.copy("dve", identb.v, C["ident"].v)
        S.copy("dve", blkones_b.v, C["blkones"].v)
        S.copy("dve", headsel_b.v, C["headsel"].v)
        identf = C["ident"]
        yT = sb(es, "yT", [128, 16, TO], BF16, slots=16)
        ST = sb(es, "ST", [64, 16, 64])
        S.memset("pool", ST.v, 0.0)
        mod = sb(es, "mod", [128, 96])
        g1 = sb(es, "g1", [128, 16])
        g2 = sb(es, "g2", [128, 16])
        eps_n = sb(es, "eps_n", [128, 1])
        eps_g = sb(es, "eps_g", [128, 1])
        S.memset("pool", eps_n.v, 1e-6)
        S.memset("pool", eps_g.v, 64e-5)
        omka = sb(es, "omka", [128, 8])
        S.ts("dve", omka.v, C["ka_l"].v, -1.0, 1.0, op0=ALU.mult, op1=ALU.add)
        bq8 = sb(es, "bq8", [128, 8])
        S.ts("dve", bq8.v, C["b_in_l"][:, CBI["q0"]:CBI["q0"] + 8], 0.125, None, op0=ALU.mult)
        esink = sb(es, "esink", [128, 16])
        S.act(esink.v, C["sinks_b"].v, AF.Exp)
        hb = sb(es, "hb", [128, 1])
        S.ts("dve", hb.v, flags[:, 5:6], -1.0, 30000.0, op0=ALU.add, op1=ALU.mult)

        try:
            chk("c0")
            with ExitStack() as p0:
                cact = sb(p0, "cact", [128, 16], BF16)
                S.act(cact.v, cvec.v, AF.Silu)
                wsl = [sb(p0, "wada%d" % i, [128, 16, 512], BF16) for i in range(3)]
                wada_v = w_ada.rearrange("(kc p) n -> p kc n", p=128)
                modps = PF()
                for s in range(24):
                    sl = wsl[s % 3]
                    for k4 in range(4):
                        S.dma("pool", "wada%d" % (s % 3), sl.ap[:, 4 * k4:4 * k4 + 4, :],
                              wada_v[:, 4 * k4:4 * k4 + 4, s * 512:(s + 1) * 512], outs=[sl.v])
                    for j in range(4):
                        col = 4 * s + j
                        for kc in range(16):
                            S.mm(modps[:, col:col + 1], sl[:, kc, j * 128:(j + 1) * 128], cact[:, kc:kc + 1],
                                 start=(kc == 0), stop=(kc == 15), signal=(kc == 15 and j == 3))
                S.tt("dve", mod.v, modps[:, 0:96], C["b_ada_l"].v, ALU.add)
                S.stt(g1.v, mod[:, 16:32], 1.0, C["gain1_l"].v, ALU.add, ALU.mult)
                S.stt(g2.v, mod[:, 64:80], 1.0, C["gain2_l"].v, ALU.add, ALU.mult)
                S.barrier()
            sh1, gt1, sh2, gt2 = mod[:, 0:16], mod[:, 32:48], mod[:, 48:64], mod[:, 80:96]
            chk("p0")

            with ExitStack() as p1:
                arena = sb(p1, "arena", [128, 16, TB], F32, slots=16)
                hTb = sb(p1, "hTb", [128, 16, TB], BF16, slots=16)
                pslab = [sb(p1, "pslab%d" % i, [128, 16, 384], BF16) for i in range(1)]
                mslab = [sb(p1, "mslab%d" % i, [128, 16, 128], BF16) for i in range(2)]
                xs = sb(p1, "xs", [128, D // 2])
                sqt = [sb(p1, "sqt%d" % i, [128, TB]) for i in range(2)]
                rstd = sb(p1, "rstd", [128, TB])
                ptmp = sb(p1, "ptmp", [128, TB + 1])
                carry = sb(p1, "carry", [128, 27])
                S.memset("pool", carry.v, 0.0)
                w2z = sb(p1, "w2z", [128, 128], BF16)
                a2z = sb(p1, "a2z", [128, 128], BF16)
                g2a = sb(p1, "g2a", [128, 128], BF16)
                g2b = sb(p1, "g2b", [32, 128], BF16)
                lnw = sb(p1, "lnw", [128, 128])
                lnb = sb(p1, "lnb", [128, 128])


                pass
                wa_bf = sb(p1, "wa_bf", [128, TB], BF16)
                sgd0 = sb(p1, "sgd0", [128, TB], BF16)
                sgd1 = sb(p1, "sgd1", [32, TB], BF16)
                kk2 = sb(p1, "kk2", [128, TB], BF16)
                btm = sb(p1, "btm", [128, 2, TB], BF16)
                ktm = sb(p1, "ktm", [128, 2, TB], BF16)
                S.memset("pool", btm.v, 0.0)
                S.memset("pool", ktm.v, 0.0)
                dgh = sb(p1, "dgh", [128, 2, NT, 64], BF16)
                dgl = sb(p1, "dgl", [128, 2, NT, 64], BF16)
                dgt = sb(p1, "dgt", [128, NT, 64])
                S.memset("pool", dgh.v, 0.0)
                S.memset("pool", dgl.v, 0.0)
                identfold_b = sb(p1, "identfoldb", [128, 64], BF16)
                S.copy("dve", identfold_b.v, C["identfold"].v)
                Bh = sb(p1, "Bh", [128, TB], BF16)
                Kh = sb(p1, "Kh", [128, TB], BF16)
                vbf = sb(p1, "vbf", [128, TB], BF16)
                rkb = sb(p1, "rkb", [128, TB], BF16)
                AR = sb(p1, "AR", [128, NT, 256], BF16)
                gL = sb(p1, "gL", [128, NT])
                dg = sb(p1, "dg", [128, NT, 64])
                TOK = sb(p1, "TOK", [128, NT, 512], BF16, slots=NT)
                nbsb = [sb(p1, "nbsb%d" % i, [128, 512], BF16) for i in range(NT)]
                kbsb = [sb(p1, "kbsb%d" % i, [128, 512], BF16) for i in range(NT)]
                gsb = [sb(p1, "gsb%d" % i, [128, 256], BF16) for i in range(NT)]
                ngb = [[sb(p1, "ngb%d_%d" % (i, j), [128, 512], BF16) for j in range(2)] for i in range(NT)]
                rxb = [[sb(p1, "rxb%d_%d" % (i, j), [128, 512], BF16) for j in range(2)] for i in range(NT)]
                PQ = sb(p1, "PQ", [64, NT, 256], F32, slots=NT)
                RP = sb(p1, "RP", [64, NT, 256], BF16, slots=NT)
                YP = sb(p1, "YP", [128, NT, 128], F32, slots=NT)
                STc = sb(p1, "STc", [64, NT, 128], BF16, slots=NT)
                yt = sb(p1, "yt", [128, 128])
                ysq = sb(p1, "ysq", [128, 128])
                ytb = sb(p1, "ytb", [128, 128], BF16)
                st1 = sb(p1, "st1", [128, 2])
                st2 = sb(p1, "st2", [128, 2])
                mean = sb(p1, "mean", [128, 2])
                msq = sb(p1, "msq", [128, 2])
                var = sb(p1, "var", [128, 2])
                bon = sb(p1, "bon", [128, 2])
                qT = sb(p1, "qT", [128, 8, TB], BF16, slots=8)
                kdup = [[sb(p1, "kdup%d_%d" % (g, par), [128, 128 + TB], BF16) for par in range(2)] for g in range(2)]
                vaT = kk2
                vaug = sb(p1, "vaug", [128, NT + 1, 2, 65], BF16)
                S.memset("pool", vaug.v, 1.0)
                for g in range(2):
                    for par in range(2):
                        S.memset("pool", kdup[g][par].v, 0.0)
                et = sqt if TB == 512 else [sb(p1, "et%d" % i, [128, 512]) for i in range(2)]
                PT = [Bh, Kh] if TB == 512 else [sb(p1, "PT%d" % i, [128, 512], BF16) for i in range(2)]
                den = sb(p1, "den", [128, 4])
                otok = sb(p1, "otok", [128, 1024], BF16)

                def A(i):
                    return arena.s(i)

                def load_cb(slab_v, key, name):
                    off = 0
                    for (st_, wd) in CB[CBI[name]][1]:
                        S.dma_split("pool", key, slab_v.ap[:, :, off:off + wd], win_v[:, :, st_:st_ + wd], [slab_v], 16)
                        off += wd
                    return off

                def proj_mix(slab_v, width):
                    ps = PF()
                    for kc in range(16):
                        S.mm(ps[0:width, 0:TB], slab_v[:, kc, 0:width], hTb.s(kc), start=(kc == 0), stop=(kc == 15),
                             signal=(kc == 15))
                    return ps[0:width, 0:TB]

                mcnt = [0]

                def misc_proj(name):
                    mcnt[0] += 1
                    sl = mslab[mcnt[0] % 2]
                    wdt = load_cb(sl.v, "mslab%d" % (mcnt[0] % 2), name)
                    return proj_mix(sl.v, wdt), wdt

                tmp1 = sb(p1, "tmp1", [128, 1])

                def carry_last(slab_v, width, cbi, blk_):
                    ps = PF()
                    for kc in range(16):
                        S.mm(ps[0:width, 0:1], slab_v[:, kc, 0:width], hTb.s(kc)[:, TB - 1:TB], start=(kc == 0),
                             stop=(kc == 15))
                    S.act(tmp1[0:width, :], ps[0:width, 0:1], AF.Identity, bias=C["b_in_l"][0:width, cbi:cbi + 1])
                    S.ts("dve", carry[0:width, cbi:cbi + 1], tmp1[0:width, :], flags[0:width, blk_:blk_ + 1], None,
                         op0=ALU.mult)

                try:
                    for blk in range(NBLK_MIX):
                        own = blk >= 6
                        halo = blk == 5
                        ob = blk - 6
                        vflag = flags[:, blk:blk + 1]

                        def lerp(psv, cbi, width, out_v):
                            S.act(ptmp[0:width, 1:TB + 1], psv, AF.Identity, bias=C["b_in_l"][0:width, cbi:cbi + 1])
                            S.copy("act", ptmp[0:width, 0:1], carry[0:width, cbi:cbi + 1])
                            dt_ = A(13)[0:width, :]
                            S.tt("dve", dt_, ptmp[0:width, 0:TB], ptmp[0:width, 1:TB + 1], ALU.subtract)
                            S.stt(out_v, dt_, C["mix_l"][0:width, cbi:cbi + 1], ptmp[0:width, 1:TB + 1], ALU.mult, ALU.add)
                            S.ts("dve", carry[0:width, cbi:cbi + 1], ptmp[0:width, TB:TB + 1], flags[0:width, blk:blk + 1],
                                 None, op0=ALU.mult)

                        for n in range(NT):
                            for g4 in range(4):
                                if g4 % 2 == 0:
                                    S.dma("sp", "xs", xs.ap, xb[blk, n * 128:(n + 1) * 128, g4 * 512:g4 * 512 + 1024], outs=[xs.v])
                                ps = PF()
                                for j in range(4):
                                    dc = 4 * g4 + j
                                    dl = dc % 8
                                    S.tr(ps[:, j * 128:(j + 1) * 128], xs[:, dl * 128:(dl + 1) * 128], identf.v,
                                         signal=(j == 3))
                                dst = arena.s(4 * g4, 4)[:, :, n * 128:(n + 1) * 128]
                                S.copy("act" if g4 % 2 else "dve", dst, ps[:, 0:512].re("p (j t) -> p j t", j=4))
                        if blk == 0:
                            chki("b0a")
                        msps = PF()
                        for dc in range(16):
                            sq = sqt[dc % 2]
                            S.act(sq.v, A(dc), AF.Square)
                            S.mm(msps[:, 0:TB], C["onesdiv"].v, sq.v, start=(dc == 0), stop=(dc == 15), signal=(dc == 15))
                        S.act(rstd.v, msps[:, 0:TB], AF.Sqrt, bias=eps_n.v)
                        S.op("dve", "reciprocal", [rstd.v], [rstd.v], (rstd.ap, rstd.ap))
                        for dc in range(16):
                            tmp = sqt[dc % 2]
                            S.tt("dve", tmp.v, A(dc), rstd.v, ALU.mult)
                            S.act(hTb.s(dc), tmp.v, AF.Identity, bias=sh1[:, dc:dc + 1], scale=g1[:, dc:dc + 1])
                        if blk == 0:
                            chki("b0b")
                        psv, _ = misc_proj("wa")
                        lerp(psv, CBI["wa"], 128, A(15))
                        S.act(wa_bf[0:64, :], A(15)[0:64, :], AF.Tanh)
                        S.copy("act", wa_bf[64:128, :], A(15)[64:128, :])
                        if halo:
                            for nm_ in ("g0", "g1"):
                                mcnt[0] += 1
                                sl_ = mslab[mcnt[0] % 2]
                                wdt_ = load_cb(sl_.v, "mslab%d" % (mcnt[0] % 2), nm_)
                                carry_last(sl_.v, wdt_, CBI[nm_], blk)
                        if own:
                            psv, _ = misc_proj("g0")
                            lerp(psv, CBI["g0"], 128, A(15))
                            S.act(sgd0.v, A(15), AF.Sigmoid)
                            psv, _ = misc_proj("g1")
                            lerp(psv, CBI["g1"], 32, A(15)[0:32, :])
                            S.act(sgd1.v, A(15)[0:32, :], AF.Sigmoid)
                        if blk == 0:
                            chki("b0c")
                        def pair_weights(hp_, blk_):
                            sl_ = pslab[0]
                            if blk_ >= 5:
                                S.dma_split("pool", "pslab0", sl_.ap[:, :, 0:128], win_v[:, :, hp_ * 128:(hp_ + 1) * 128], [sl_.v], 16)
                            S.dma_split("pool", "pslab0", sl_.ap[:, :, 128:256], win_v[:, :, 1024 + hp_ * 128:1024 + (hp_ + 1) * 128], [sl_.v], 16)
                            S.dma_split("pool", "pslab0", sl_.ap[:, :, 256:384], win_v[:, :, 2048 + hp_ * 128:2048 + (hp_ + 1) * 128], [sl_.v], 16)

                        for hp in range(8):
                            sl = pslab[0]
                            key = "pslab0"
                            hc = slice(hp * 128, (hp + 1) * 128)
                            if hp == 0 and blk == 0:
                                pair_weights(0, blk)
                            if halo:
                                carry_last(sl[:, :, 0:128], 128, hp, blk)
                            if own:
                                lerp(proj_mix(sl[:, :, 0:128], 128), hp, 128, A(0))
                            lerp(proj_mix(sl[:, :, 128:256], 128), 8 + hp, 128, A(1))
                            lerp(proj_mix(sl[:, :, 256:384], 128), 16 + hp, 128, A(2))
                            S.dma("pool", "w2z", w2z.ap, d_w2z[:, hc], outs=[w2z.v])
                            S.dma("pool", "a2z", a2z.ap, d_a2z[:, hc], outs=[a2z.v])
                            if own:
                                S.dma("pool", "g2a", g2a.ap, d_g2a[:, hc], outs=[g2a.v])
                                S.dma("pool", "g2b", g2b.ap, d_g2b[:, hc], outs=[g2b.v])
                            ps = PF()
                            S.mm(ps[:, 0:TB], w2z.v, wa_bf.v)
                            S.act(A(3), ps[:, 0:TB], AF.Sigmoid, bias=C["w0_l"][:, hp:hp + 1])
                            ps = PF()
                            S.mm(ps[:, 0:TB], a2z.v, wa_bf.v)
                            S.act(A(5), ps[:, 0:TB], AF.Sigmoid, bias=C["a0_l"][:, hp:hp + 1])
                            S.op("dve", "tensor_tensor_scan", [A(4)], [C["chunkmask"].v, A(3)],
                                 (A(4).ap, C["chunkmask"].ap, A(3).ap, 0.0, ALU.mult, ALU.add))
                            c3 = "p (c t) -> p c t"
                            S.act(A(9), A(4), AF.Exp, scale=-CDEC)
                            if own:
                                S.tt("dve", AR[:, :, 128:256], A(0).re(c3, c=NT), A(9).re(c3, c=NT), ALU.mult)
                            S.tt("dve", A(14), A(4), A(3), ALU.subtract)
                            S.act(A(10), A(14), AF.Exp, scale=-CDEC)
                            S.act(gL.v, A(4).re(c3, c=NT)[:, :, 127], AF.Exp, scale=-CDEC)
                            S.act(A(9), A(4), AF.Exp, scale=CDEC)
                            S.act(A(6), A(1), AF.Identity, scale=C["kk_l"][:, hp:hp + 1])
                            S.act(kk2.v, A(6), AF.Square)
                            ps = PF()
                            S.mm(ps[:, 0:TB], blkones_b.v, kk2.v)
                            S.act(A(14), ps[:, 0:TB], AF.Sqrt)
                            S.ts("dve", A(14), A(14), 1e-12, None, op0=ALU.max)
                            S.op("dve", "reciprocal", [A(14)], [A(14)], (A(14).ap, A(14).ap))
                            S.tt("dve", A(6), A(6), A(14), ALU.mult)
                            S.ts("dve", A(7), A(5), C["ka_l"][:, hp:hp + 1], omka[:, hp:hp + 1], op0=ALU.mult, op1=ALU.add)
                            S.tt("dve", A(7), A(1), A(7), ALU.mult)
                            S.tt("dve", A(8), A(6), A(5), ALU.mult)
                            S.stt(AR[:, :, 0:128], A(6).re(c3, c=NT), -1.0, A(10).re(c3, c=NT), ALU.mult, ALU.mult)
                            S.tt("dve", A(11), A(8), A(9), ALU.mult)
                            S.tt("dve", A(12), A(7), A(9), ALU.mult)
                            for h in range(2):
                                hs = slice(64 * h, 64 * h + 64)
                                S.copy("act", btm[hs, h, :], A(11)[hs, :])
                                S.copy("act", ktm[hs, h, :], A(12)[hs, :])
                            for c in range(NT):
                                cs_ = slice(c * 128, (c + 1) * 128)
                                S.act(Bh[:, cs_], A(11)[:, cs_], AF.Identity, scale=gL[:, c:c + 1])
                                S.act(Kh[:, cs_], A(12)[:, cs_], AF.Identity, scale=gL[:, c:c + 1])
                                S.act(dg[:, c, :], C["identfold"].v, AF.Identity, scale=gL[:, c:c + 1])
                            for h in range(2):
                                hs = slice(64 * h, 64 * h + 64)
                                S.copy("act", dgh[hs, h], dg[hs])
                                S.tt("dve", dgt[hs], dg[hs], dgh[hs, h], ALU.subtract)
                                S.copy("act", dgl[hs, h], dgt[hs])
                            S.ts("dve", vbf.v, A(2), vflag, None, op0=ALU.mult)
                            if own:
                                S.stt(rkb.v, A(0), C["rk_l"][:, hp:hp + 1], A(7), ALU.mult, ALU.mult)
                            if blk == 0 and hp == 0:
                                chki("b0d")
                            if hp + 1 < 8:
                                pair_weights(hp + 1, blk)
                            elif blk + 1 < NBLK_MIX:
                                pair_weights(0, blk + 1)
                            W = 256 if own else 128
                            h3 = "p (h f) -> p h f"
                            h4 = "p (h f) -> p h f"
                            CR = range(NT)
                            csl = [slice(c * 128, (c + 1) * 128) for c in CR]
                            for c in CR:
                                pb = PB()
                                S.tr(pb[:, 0:128], Bh[:, csl[c]], identb.v)
                                S.tr(pb[:, 128:256], Kh[:, csl[c]], identb.v)
                                S.tr(pb[:, 256:384], AR[:, c, 0:128], identb.v)
                                S.tr(pb[:, 384:512], vbf[:, csl[c]], identb.v)
                                S.copy("act", TOK.s(c), pb[:, 0:512])
                            for c in CR:
                                psg, psk, psn = PF(), PF(), PF()
                                for h in range(2):
                                    S.mm(psg[:, h * 256:h * 256 + W], btm[:, h, csl[c]], AR[:, c, 0:W])
                                    S.mm(psk[:, h * 256:h * 256 + W], ktm[:, h, csl[c]], AR[:, c, 0:W])
                                    S.mm(psn[:, h * 128:(h + 1) * 128], AR[:, c, 0:128], btm[:, h, csl[c]])
                                NBs, KBs, Gs = nbsb[c], kbsb[c], gsb[c]
                                S.tt("dve", NBs.v.re(h3, h=2)[:, :, 0:W], psg[:, 0:512].re(h3, h=2)[:, :, 0:W],
                                     C["gmn"].v.re(h3, h=2)[:, :, 0:W], ALU.mult)
                                S.tt("dve", KBs.v.re(h3, h=2)[:, :, 0:W], psk[:, 0:512].re(h3, h=2)[:, :, 0:W],
                                     C["gm"].v.re(h3, h=2)[:, :, 0:W], ALU.mult)
                                S.tt("dve", Gs.v, psn[:, 0:256], C["msl2d"].v, ALU.mult)
                                S.tt("dve", rxb[c][0].v.re(h4, h=2)[:, :, 128:256], psn[:, 0:256].re(h4, h=2),
                                     C["msl2o"].v.re(h4, h=2), ALU.mult)
                            for c in CR:
                                tok = TOK.s(c)
                                RX = rxb[c][0]
                                psz = PF()
                                for h in range(2):
                                    S.mm(psz[:, h * 64:(h + 1) * 64], kbsb[c][:, h * 256:h * 256 + 128],
                                         tok[:, 384 + 64 * h:384 + 64 * h + 64])
                                for h in range(2):
                                    S.copy("dve", RX[:, h * 256:h * 256 + 64], tok[:, 256 + 64 * h:256 + 64 * h + 64])
                                S.copy("act", RX.v.re(h4, h=2)[:, :, 64:128], psz[:, 0:128].re(h4, h=2))
                            Nv = [[nbsb[c][:, h * 256:h * 256 + 128] for h in range(2)] for c in CR]
                            Gv = [[gsb[c][:, h * 128:(h + 1) * 128] for h in range(2)] for c in CR]
                            rxi = [0 for c in CR]
                            for k in range(5):
                                for c in CR:
                                    RX = rxb[c][rxi[c]]
                                    psr = PF()
                                    for h in range(2):
                                        S.mm(psr[:, h * 256:(h + 1) * 256], Nv[c][h], RX[:, h * 256:(h + 1) * 256])
                                    rxi[c] ^= 1
                                    S.tt("dve", rxb[c][rxi[c]].v, psr[:, 0:512], RX.v, ALU.add)
                                if k < 4:
                                    for c in CR:
                                        pss = PF()
                                        for h in range(2):
                                            S.mm(pss[:, h * 128:(h + 1) * 128], Gv[c][h], Nv[c][h])
                                            S.mm(pss[:, 256 + h * 128:256 + (h + 1) * 128], Nv[c][h], Gv[c][h])
                                        NGn = ngb[c][k % 2]
                                        S.copy("act", NGn.v, pss[:, 0:512])
                                        Nv[c] = [NGn[:, h * 128:(h + 1) * 128] for h in range(2)]
                                        Gv[c] = [NGn[:, 256 + h * 128:256 + (h + 1) * 128] for h in range(2)]
                            ETs = [ngb[c][0][:, 0:256] for c in CR]
                            E2T = [ngb[c][0][:, 256:512] for c in CR]
                            R2s = [ngb[c][1][:, 0:256] for c in CR]
                            Rs = [ngb[c][1][:, 256:512] for c in CR]
                            RXf = [rxb[c][rxi[c]] for c in CR]
                            for c in CR:
                                pbe = PB()
                                for h in range(2):
                                    S.tr(pbe[:, h * 128:(h + 1) * 128], RXf[c][:, h * 256 + 128:h * 256 + 256], identb.v)
                                S.copy("act", ETs[c], pbe[:, 0:256])
                            for c in CR:
                                psr = PF()
                                for h in range(2):
                                    S.mm(psr[:, h * 128:(h + 1) * 128], ETs[c][:, h * 128:(h + 1) * 128], RXf[c][:, h * 256:h * 256 + 128])
                                S.tt("dve", R2s[c].re(h4, h=2), psr[:, 0:256].re(h4, h=2), RXf[c].v.re(h4, h=2)[:, :, 0:128], ALU.add)
                                pse = PF()
                                for h in range(2):
                                    S.mm(pse[:, h * 128:(h + 1) * 128], RXf[c][:, h * 256 + 128:h * 256 + 256], ETs[c][:, h * 128:(h + 1) * 128])
                                S.copy("act", E2T[c], pse[:, 0:256])
                            for c in CR:
                                psr = PF()
                                for h in range(2):
                                    S.mm(psr[:, h * 128:(h + 1) * 128], E2T[c][:, h * 128:(h + 1) * 128], R2s[c][:, h * 128:(h + 1) * 128])
                                S.tt("dve", Rs[c], psr[:, 0:256], R2s[c], ALU.add)
                            for c in CR:
                                tok = TOK.s(c)
                                R = Rs[c]
                                NBs, KBs = nbsb[c], kbsb[c]
                                pspq = PF()
                                for h in range(2):
                                    o = h * 128
                                    S.mm(pspq[0:64, o:o + 64], R[:, o:o + 64], tok[:, 64 * h:64 * h + 64], start=True, stop=False)
                                    S.mm(pspq[0:64, o:o + 64], identfold_b.v, dgh[:, h, c, :], start=False, stop=False)
                                    S.mm(pspq[0:64, o:o + 64], identfold_b.v, dgl[:, h, c, :], start=False, stop=True)
                                    S.mm(pspq[0:64, o + 64:o + 128], tok[:, 64 * h:64 * h + 64], R[:, o + 64:o + 128], start=True, stop=False)
                                    S.mm(pspq[0:64, o + 64:o + 128], tok[:, 128 + 64 * h:128 + 64 * h + 64],
                                         tok[:, 384 + 64 * h:384 + 64 * h + 64], start=False, stop=True)
                                S.copy("act", PQ.s(c), pspq[0:64, 0:256])
                                if own:
                                    psrp = PF()
                                    for h in range(2):
                                        o = h * 128
                                        S.mm(psrp[0:64, o:o + 128], R[:, o:o + 64], NBs[:, h * 256 + 128:h * 256 + 256], start=True, stop=False)
                                        S.mm(psrp[0:64, o:o + 128], identb[:, 64 * h:64 * h + 64], AR[:, c, 128:256], start=False, stop=True)
                                    S.copy("act", RP.s(c), psrp[0:64, 0:256])
                                    psyp = PF()
                                    for h in range(2):
                                        o = h * 128
                                        S.mm(psyp[:, h * 64:(h + 1) * 64], NBs[:, h * 256 + 128:h * 256 + 256], R[:, o + 64:o + 128],
                                             start=True, stop=False)
                                        S.mm(psyp[:, h * 64:(h + 1) * 64], KBs[:, h * 256 + 128:h * 256 + 256],
                                             tok[:, 384 + 64 * h:384 + 64 * h + 64], start=False, stop=True)
                                    S.copy("dve", YP.s(c), psyp[:, 0:128])
                            if blk == 0 and hp == 0:
                                chki("b0e")
                            stv = ST[:, 2 * hp:2 * hp + 2, :]
                            for c in range(NT):
                                if own:
                                    S.copy("act", STc.s(c).re("p (h i) -> p h i", h=2), stv)
                                pst = PF()
                                for h in range(2):
                                    S.mm(pst[:, h * 64:(h + 1) * 64], PQ.s(c)[:, h * 128:h * 128 + 128],
                                         ST[:, 2 * hp + h, :], signal=(h == 1))
                                S.tt("dve", stv, pst[0:64, 0:128].re("p (h i) -> p h i", h=2),
                                     PQ.s(c).re("p (h f) -> p h f", h=2)[:, :, 64:128], ALU.add)
                            if own:
                                S.dma("sp", "lnw", lnw.ap, d_lnw[:, hc], outs=[lnw.v])
                                S.dma("sp", "lnb", lnb.ap, d_lnb[:, hc], outs=[lnb.v])
                                for c in range(NT):
                                    cs_ = slice(c * 128, (c + 1) * 128)
                                    tok = TOK.s(c)
                                    psy = PF()
                                    for h in range(2):
                                        S.mm(psy[:, h * 64:(h + 1) * 64], RP.s(c)[:, h * 128:(h + 1) * 128],
                                             STc.s(c)[:, h * 64:(h + 1) * 64], signal=(h == 1))
                                    S.tt("dve", yt.v, psy[:, 0:128], YP.s(c), ALU.add)
                                    y3 = yt.v.re("p (h i) -> p h i", h=2)
                                    S.op("dve", "tensor_reduce", [st1.v], [yt.v], (st1.ap, y3.ap), dict(axis=AX.X, op=ALU.add))
                                    S.act(ysq.v, yt.v, AF.Square)
                                    S.op("dve", "tensor_reduce", [st2.v], [ysq.v],
                                         (st2.ap, ysq.v.re("p (h i) -> p h i", h=2).ap), dict(axis=AX.X, op=ALU.add))
                                    S.ts("dve", mean.v, st1.v, 1.0 / 64, None, op0=ALU.mult)
                                    S.tt("dve", msq.v, mean.v, mean.v, ALU.mult)
                                    S.stt(var.v, st2.v, 1.0 / 64, msq.v, ALU.mult, ALU.subtract)
                                    S.act(var.v, var.v, AF.Sqrt, bias=eps_g.v)
                                    S.op("dve", "reciprocal", [var.v], [var.v], (var.ap, var.ap))
                                    for h in range(2):
                                        S.ts("dve", yt[:, h * 64:(h + 1) * 64], yt[:, h * 64:(h + 1) * 64], mean[:, h:h + 1],
                                             var[:, h:h + 1], op0=ALU.subtract, op1=ALU.mult)
                                    S.tt("dve", yt.v, yt.v, lnw.v, ALU.mult)
                                    S.tt("dve", yt.v, yt.v, lnb.v, ALU.add)
                                    psb_ = PF()
                                    S.mm(psb_[:, 0:2], rkb[:, cs_], headsel_b.v)
                                    S.copy("act", bon.v, psb_[:, 0:2])
                                    for h in range(2):
                                        S.stt(yt[:, h * 64:(h + 1) * 64], tok[:, 384 + 64 * h:384 + 64 * h + 64], bon[:, h:h + 1],
                                              yt[:, h * 64:(h + 1) * 64], ALU.mult, ALU.add)
                                    psg_ = PF()
                                    S.mm(psg_[:, 0:128], sgd0[:, cs_], g2a.v, start=True, stop=False, signal=False)
                                    S.mm(psg_[:, 0:128], sgd1[0:32, cs_], g2b.v, start=False, stop=True)
                                    S.tt("dve", ytb.v, psg_[:, 0:128], yt.v, ALU.mult)
                                    pb = PB()
                                    S.tr(pb[:, 0:128], ytb.v, identb.v)
                                    S.copy("act", yT.s(hp)[:, ob * TB + c * 128:ob * TB + (c + 1) * 128], pb[:, 0:128])
                        if blk == 0:
                            chki("b0f")
                        if halo or own:
                            for g in range(2):
                                psv, _ = misc_proj("ka%d" % g)
                                for par in range(2):
                                    pl = slice(64 * par, 64 * par + 64)
                                    S.act(kdup[g][par][pl, 128:128 + TB], psv[pl, :], AF.Identity,
                                          bias=C["b_in_l"][pl, CBI["ka%d" % g]:CBI["ka%d" % g] + 1])
                            psv, _ = misc_proj("va")
                            S.act(vaT.v, psv, AF.Identity, bias=C["b_in_l"][:, CBI["va"]:CBI["va"] + 1])
                            for n in range(NT):
                                pb = PB()
                                S.tr(pb[:, 0:128], vaT[:, n * 128:(n + 1) * 128], identb.v)
                                S.copy("dve", vaug[:, 1 + n, :, 0:64], pb[:, 0:128].re("p (g d) -> p g d", g=2))
                        if own:
                            for i in range(8):
                                psv, _ = misc_proj("q%d" % i)
                                S.act(qT.s(i), psv, AF.Identity, bias=bq8[:, i:i + 1], scale=0.125)
                            for n in range(NT):
                                for g in range(2):
                                    for par in range(2):
                                        pl = slice(64 * par, 64 * par + 64)
                                        ps0, ps1 = PF(), PF()
                                        qv = qT[:, 4 * g:4 * g + 4, n * 128:(n + 1) * 128]
                                        S.mm(ps0[:, 0:512].re("p (j t) -> p j t", j=4), kdup[g][par][:, n * 128:(n + 1) * 128], qv)
                                        S.mm(ps1[:, 0:512].re("p (j t) -> p j t", j=4), kdup[g][par][:, (n + 1) * 128:(n + 2) * 128], qv)
                                        for j in range(4):
                                            h = 8 * g + 2 * j + par
                                            js = slice(j * 128, (j + 1) * 128)
                                            S.stt(et[0][:, js], C["negdp"].v, SLOPES[h], ps0[:, js], ALU.mult, ALU.add)
                                            S.stt(et[1][:, js], C["negdc"].v, SLOPES[h], ps1[:, js], ALU.mult, ALU.add)
                                        if blk == 6 and n == 0:
                                            S.act(PT[0].v, et[0].v, AF.Exp, bias=hb.v)
                                        else:
                                            S.act(PT[0].v, et[0].v, AF.Exp)
                                        S.act(PT[1].v, et[1].v, AF.Exp)
                                        pso = PF()
                                        for j in range(4):
                                            js = slice(j * 128, (j + 1) * 128)
                                            S.mm(pso[:, j * 65:(j + 1) * 65], PT[0][:, js], vaug[:, n, g, :],
                                                 start=True, stop=False, signal=False)
                                            S.mm(pso[:, j * 65:(j + 1) * 65], PT[1][:, js], vaug[:, n + 1, g, :],
                                                 start=False, stop=True, signal=(j == 3))
                                        i0 = g * 8 + par * 4
                                        S.tt("dve", den.v, pso[:, 0:260].re("p (j e) -> p j e", j=4)[:, :, 64],
                                             esink[:, i0:i0 + 4], ALU.add)
                                        S.op("dve", "reciprocal", [den.v], [den.v], (den.ap, den.ap))
                                        for j in range(4):
                                            h = 8 * g + 2 * j + par
                                            S.act(otok[:, h * 64:(h + 1) * 64], pso[:, j * 65:j * 65 + 64], AF.Identity,
                                                  scale=den[:, j:j + 1])
                                for i in range(8):
                                    pb = PB()
                                    S.tr(pb[:, 0:128], otok[:, i * 128:(i + 1) * 128], identb.v)
                                    S.copy("act" if i % 2 else "dve",
                                           yT.s(8 + i)[:, ob * TB + n * 128:ob * TB + (n + 1) * 128], pb[:, 0:128])
                        if halo or own:
                            for g in range(2):
                                for par in range(2):
                                    pl = slice(64 * par, 64 * par + 64)
                                    S.copy("act", kdup[g][par][pl, 0:128], kdup[g][par][pl, TB:TB + 128])
                            S.copy("act", vaug[:, 0, :, 0:64], vaug[:, NT, :, 0:64])
                        S.flush()
                        if chk("blk%d" % blk, soft=True):
                            break
                except StopInner:
                    stopflag[0] = True
                S.barrier()
            if stopflag[0]:
                raise StopBuild()
            chk("p1")

            with ExitStack() as p2:
                slabs = [sb(p2, "slab%d" % i, [128, 16, 256], BF16) for i in range(4)]
                mg = sb(p2, "mg", [128, 16, TO], BF16, slots=16)
                scnt = [0]

                def slab_load(src_v, nk):
                    scnt[0] += 1
                    i = scnt[0] % 4
                    S.dma_split("pool", "slab%d" % i, slabs[i].ap[:, 0:nk, :], src_v, [slabs[i].v], nk)
                    return slabs[i]

                def slab_load2(src_a, src_b):
                    scnt[0] += 1
                    i = scnt[0] % 4
                    S.dma_split("pool", "slab%d" % i, slabs[i].ap[:, 0:8, :], src_a, [slabs[i].v], 8)
                    S.dma_split("pool", "slab%d" % i, slabs[i].ap[:, 8:16, :], src_b, [slabs[i].v], 8)
                    return slabs[i]

                wbr_v = w_br.rearrange("(kc p) n -> p kc n", p=128)
                wba_v = w_ba.rearrange("(kc p) n -> p kc n", p=128)
                wout_v = w_out.rearrange("(kc p) n -> p kc n", p=128)
                wup_v = w_up.rearrange("(kc p) n -> p kc n", p=128)
                wdn_v = w_down.rearrange("(fc p) n -> p fc n", p=128)

                with ExitStack() as d1:
                    hT = sb(d1, "hT", [128, 16, TO], BF16, slots=16)
                    xs2 = sb(d1, "xs2", [128, D])
                    xTt = sb(d1, "xTt", [128, 16, 128])
                    sq2 = [sb(d1, "sq2%d" % i, [128, 128]) for i in range(2)]
                    rs2 = sb(d1, "rs2", [128, 128])
                    tg = [sb(d1, "tg%d" % i, [128, GS]) for i in range(4)]
                    for n in range(2 * NT):
                        ts_ = slice(n * 128, (n + 1) * 128)
                        S.dma("sp", "xs2", xs2.ap, xb[6 + n // NT, (n % NT) * 128:(n % NT + 1) * 128, :], outs=[xs2.v])
                        for g4 in range(4):
                            ps = PF()
                            for j in range(4):
                                dc = 4 * g4 + j
                                S.tr(ps[:, j * 128:(j + 1) * 128], xs2[:, dc * 128:(dc + 1) * 128], identf.v, signal=(j == 3))
                            S.copy("act" if g4 % 2 else "dve", xTt[:, 4 * g4:4 * g4 + 4, :],
                                   ps[:, 0:512].re("p (j t) -> p j t", j=4))
                        msps = PF()
                        for dc in range(16):
                            sq = sq2[dc % 2]
                            S.act(sq.v, xTt[:, dc, :], AF.Square)
                            S.mm(msps[:, 0:128], C["onesdiv"].v, sq.v, start=(dc == 0), stop=(dc == 15), signal=(dc == 15))
                        S.act(rs2.v, msps[:, 0:128], AF.Sqrt, bias=eps_n.v)
                        S.op("dve", "reciprocal", [rs2.v], [rs2.v], (rs2.ap, rs2.ap))
                        for dc in range(16):
                            tmp = sq2[dc % 2]
                            S.tt("dve", tmp.v, xTt[:, dc, :], rs2.v, ALU.mult)
                            S.act(hT.s(dc)[:, ts_], tmp.v, AF.Identity, bias=sh1[:, dc:dc + 1], scale=g1[:, dc:dc + 1])
                    gt0 = CBI["gt0"]
                    for d2 in range(8):
                        c0 = d2 * 256
                        s_gr = slab_load(win_v[:, :, 4640 + c0:4640 + c0 + 256], 16)
                        s_ga = slab_load(win_v[:, :, 4640 + 2048 + c0:4640 + 2048 + c0 + 256], 16)
                        s_b = slab_load2(wbr_v[:, :, c0:c0 + 256], wba_v[:, :, c0:c0 + 256])
                        for jj in range(2):
                            dblk = 2 * d2 + jj
                            js = slice(jj * 128, (jj + 1) * 128)
                            for grp in range(NG):
                                gs = slice(grp * GS, (grp + 1) * GS)
                                ps = PF()
                                for kc in range(16):
                                    S.mm(ps[:, 0:GS], s_gr[:, kc, js], hT.s(kc)[:, gs], start=(kc == 0), stop=(kc == 15),
                                         signal=(kc == 15))
                                S.act(tg[0].v, ps[:, 0:GS], AF.Sigmoid, bias=C["b_in_l"][:, gt0 + dblk:gt0 + dblk + 1])
                                ps = PF()
                                for kc in range(16):
                                    S.mm(ps[:, 0:GS], s_ga[:, kc, js], hT.s(kc)[:, gs], start=(kc == 0), stop=(kc == 15),
                                         signal=(kc == 15))
                                S.act(tg[1].v, ps[:, 0:GS], AF.Sigmoid,
                                      bias=C["b_in_l"][:, gt0 + 16 + dblk:gt0 + 16 + dblk + 1])
                                ps = PF()
                                for kc in range(8):
                                    S.mm(ps[:, 0:GS], s_b[:, kc, js], yT.s(kc)[:, gs], start=(kc == 0), stop=(kc == 7),
                                         signal=(kc == 7))
                                S.tt("dve", tg[2].v, ps[:, 0:GS], tg[0].v, ALU.mult)
                                ps = PF()
                                for kc in range(8):
                                    S.mm(ps[:, 0:GS], s_b[:, 8 + kc, js], yT.s(8 + kc)[:, gs], start=(kc == 0), stop=(kc == 7),
                                         signal=(kc == 7))
                                S.tt("dve", tg[3].v, ps[:, 0:GS], tg[1].v, ALU.mult)
                                S.tt("dve", mg.s(dblk)[:, gs], tg[2].v, tg[3].v, ALU.add)
                    S.barrier()

                with ExitStack() as d2s:
                    x1T = sb(d2s, "x1T", [128, 16, TO], F32, slots=16)
                    xs3 = sb(d2s, "xs3", [128, D])
                    sq3 = [sb(d2s, "sq3%d" % i, [128, TO]) for i in range(2)]
                    rs3 = sb(d2s, "rs3", [128, TO])
                    fin_t = sb(d2s, "fin_t", [128, 16, 128])
                    ostage = xs3
                    for n in range(2 * NT):
                        S.dma("sp", "xs3", xs3.ap, xb[6 + n // NT, (n % NT) * 128:(n % NT + 1) * 128, :], outs=[xs3.v])
                        for g4 in range(4):
                            ps = PF()
                            for j in range(4):
                                dc = 4 * g4 + j
                                S.tr(ps[:, j * 128:(j + 1) * 128], xs3[:, dc * 128:(dc + 1) * 128], identf.v, signal=(j == 3))
                            S.copy("act" if g4 % 2 else "dve", x1T.s(4 * g4, 4)[:, :, n * 128:(n + 1) * 128],
                                   ps[:, 0:512].re("p (j t) -> p j t", j=4))
                    for d2 in range(8):
                        sl = slab_load(wout_v[:, :, d2 * 256:(d2 + 1) * 256], 16)
                        for jj in range(2):
                            dblk = 2 * d2 + jj
                            js = slice(jj * 128, (jj + 1) * 128)
                            for grp in range(NG):
                                gs = slice(grp * GS, (grp + 1) * GS)
                                ps = PF()
                                for kc in range(16):
                                    S.mm(ps[:, 0:GS], sl[:, kc, js], mg.s(kc)[:, gs], start=(kc == 0), stop=(kc == 15),
                                         signal=(kc == 15))
                                S.stt(x1T.s(dblk)[:, gs], ps[:, 0:GS], gt1[:, dblk:dblk + 1], x1T.s(dblk)[:, gs],
                                      ALU.mult, ALU.add)

                    def rms_stats():
                        for grp in range(NG):
                            gs = slice(grp * GS, (grp + 1) * GS)
                            msps = PF()
                            for dc in range(16):
                                sq = sq3[dc % 2]
                                S.act(sq[:, gs], x1T.s(dc)[:, gs], AF.Square)
                                S.mm(msps[:, 0:GS], C["onesdiv"].v, sq[:, gs], start=(dc == 0), stop=(dc == 15),
                                     signal=(dc == 15))
                            S.act(rs3[:, gs], msps[:, 0:GS], AF.Sqrt, bias=eps_n.v)
                        S.op("dve", "reciprocal", [rs3.v], [rs3.v], (rs3.ap, rs3.ap))

                    rms_stats()
                    h2T = yT
                    for dc in range(16):
                        tmp = sq3[dc % 2]
                        S.tt("dve", tmp.v, x1T.s(dc), rs3.v, ALU.mult)
                        S.act(h2T.s(dc), tmp.v, AF.Identity, bias=sh2[:, dc:dc + 1], scale=g2[:, dc:dc + 1])
                    for fg in range(4):
                        for f2 in range(8):
                            c0 = fg * 2048 + f2 * 256
                            sl = slab_load(wup_v[:, :, c0:c0 + 256], 16)
                            for jj in range(2):
                                fb = 2 * f2 + jj
                                js = slice(jj * 128, (jj + 1) * 128)
                                for grp in range(NG):
                                    gs = slice(grp * GS, (grp + 1) * GS)
                                    ps = PF()
                                    for kc in range(16):
                                        S.mm(ps[:, 0:GS], sl[:, kc, js], h2T.s(kc)[:, gs], start=(kc == 0), stop=(kc == 15),
                                             signal=(kc == 15))
                                    rl = sq3[(fb + grp) % 2]
                                    S.act(rl[:, 0:GS], ps[:, 0:GS], AF.Relu)
                                    S.tt("dve", mg.s(fb)[:, gs], rl[:, 0:GS], rl[:, 0:GS], ALU.mult)
                        for d2 in range(8):
                            sl = slab_load(wdn_v[:, fg * 16:(fg + 1) * 16, d2 * 256:(d2 + 1) * 256], 16)
                            for jj in range(2):
                                dblk = 2 * d2 + jj
                                js = slice(jj * 128, (jj + 1) * 128)
                                for grp in range(NG):
                                    gs = slice(grp * GS, (grp + 1) * GS)
                                    ps = PF()
                                    for fb in range(16):
                                        S.mm(ps[:, 0:GS], sl[:, fb, js], mg.s(fb)[:, gs], start=(fb == 0), stop=(fb == 15),
                                             signal=(fb == 15))
                                    S.stt(x1T.s(dblk)[:, gs], ps[:, 0:GS], gt2[:, dblk:dblk + 1], x1T.s(dblk)[:, gs],
                                          ALU.mult, ALU.add)
                    rms_stats()
                    outb = Buf()
                    for n in range(2 * NT):
                        ts_ = slice(n * 128, (n + 1) * 128)
                        for dc in range(16):
                            S.stt(fin_t[:, dc, :], x1T.s(dc)[:, ts_], C["gainf_l"][:, dc:dc + 1], rs3[:, ts_],
                                  ALU.mult, ALU.mult)
                        for g4 in range(4):
                            ps = PF()
                            for j in range(4):
                                dc = 4 * g4 + j
                                S.tr(ps[:, j * 128:(j + 1) * 128], fin_t[:, dc, :], identf.v, signal=(j == 3))
                            S.copy("act" if g4 % 2 else "dve", ostage[:, g4 * 512:(g4 + 1) * 512], ps[:, 0:512])
                        S.dma("sp", "ostore", out[n * 128:(n + 1) * 128, :], ostage.ap, outs=[V(None, [outb])], ins=[ostage.v])
                    S._wait(S.E["sp"], outb.w)
                    S.barrier()
        except StopBuild:
            S.barrier()
    return nc


_NC_CACHE = {}
_DBG = {"stop": None, "cores": None}


def kernel(**inputs):
    NT = inputs["x"].shape[1] // (8 * 128)
    if NT not in _NC_CACHE:
        _NC_CACHE[NT] = build(NT, _DBG["stop"])
    nc = _NC_CACHE[NT]
    maps = host_inputs(inputs, NT)
    if _DBG["cores"] is not None:
        sel = _DBG["cores"]
        res = run_bass_kernel_spmd(nc, [maps[c] for c in sel], core_ids=list(range(len(sel))))
        return {c: np.asarray(res.results[i]["out"], np.float32) for i, c in enumerate(sel)}
    res = run_bass_kernel_spmd(nc, maps, core_ids=list(range(8)))
    TB = 128 * NT
    B = inputs["x"].shape[0]
    outp = np.zeros((B, 8 * TB, D), np.float32)
    for cid in range(8):
        b, q = cid // 4, cid % 4
        outp[b, q * 2 * TB:(q + 1) * 2 * TB] = np.asarray(res.results[cid]["out"], np.float32)
    return outp
```
